# Optimizing a Trainium2 kernel written in Bass

```python
import math
import jax, jax.numpy as jnp
from jax import lax
import numpy as np

D_MODEL = 2048
BATCH = 16
SEQ = 2048
DEPTH = 4
DEC_BATCH = 8
DEC_SEQ = 2048
PAST_LEN = 128

CONV_CH = 1024
N_HEADS = 8
HEAD_DIM = 128
ATTN_W = N_HEADS * HEAD_DIM
DIL_PAIRS = ((128, 1), (512, 4), (2048, 16))
ROPE_THETA = 10000.0
N_FOURIER_GROUPS = 4
D_FF = 5632
N_EVEN = (DEPTH + 1) // 2
N_ODD = DEPTH // 2
D_IN_MIX = 3 * CONV_CH + 3 * ATTN_W
ALPHA = (2 * DEPTH) ** 0.25
BETA = (8 * DEPTH) ** -0.25
LN_EPS = 1e-5

kernel_name = "hybrid_conv_dilattn_fnet_encoder"


def layer_norm(x, g, b):
    xf = x.astype(jnp.float32)
    mu = jnp.mean(xf, axis=-1, keepdims=True)
    var = jnp.mean(jnp.square(xf - mu), axis=-1, keepdims=True)
    y = (xf - mu) * lax.rsqrt(var + LN_EPS) * g.astype(jnp.float32) + b.astype(jnp.float32)
    return y.astype(x.dtype)


def dwconv3(x, w):
    xp = jnp.pad(x, ((0, 0), (1, 1), (0, 0)))
    return xp[:, :-2] * w[0] + xp[:, 1:-1] * w[1] + xp[:, 2:] * w[2]


def rope(x):
    S = x.shape[1]
    half = HEAD_DIM // 2
    inv = 1.0 / (ROPE_THETA ** (jnp.arange(half, dtype=jnp.float32) / half))
    ang = jnp.arange(S, dtype=jnp.float32)[:, None] * inv[None, :]
    cos = jnp.cos(ang)[None, :, None, :]
    sin = jnp.sin(ang)[None, :, None, :]
    xf = x.astype(jnp.float32)
    x1, x2 = xf[..., :half], xf[..., half:]
    return jnp.concatenate([x1 * cos - x2 * sin, x2 * cos + x1 * sin], axis=-1).astype(x.dtype)


def dilated_branch(q, k, v, window, dilation):
    Bsz, H, S, hd = q.shape
    r = dilation
    n_side = window // (2 * dilation)
    QB = n_side
    L = S // r
    nb = -(-L // QB)
    Lp = nb * QB

    def to_phase(t):
        return t.reshape(Bsz, H, L, r, hd).transpose(0, 1, 3, 2, 4)

    qp = jnp.pad(to_phase(q), ((0, 0), (0, 0), (0, 0), (0, Lp - L), (0, 0)))
    qb = qp.reshape(Bsz, H, r, nb, QB, hd)
    pad_k = ((0, 0), (0, 0), (0, 0), (QB, Lp - L + QB), (0, 0))
    kb = jnp.pad(to_phase(k), pad_k).reshape(Bsz, H, r, nb + 2, QB, hd)
    vb = jnp.pad(to_phase(v), pad_k).reshape(Bsz, H, r, nb + 2, QB, hd)
    kb = jnp.concatenate([kb[:, :, :, 0:nb], kb[:, :, :, 1:nb + 1], kb[:, :, :, 2:nb + 2]], axis=-2)
    vb = jnp.concatenate([vb[:, :, :, 0:nb], vb[:, :, :, 1:nb + 1], vb[:, :, :, 2:nb + 2]], axis=-2)

    a = jnp.arange(QB)[:, None]
    c = jnp.arange(3 * QB)[None, :]
    band = jnp.abs(c - QB - a) <= n_side
    kl = jnp.arange(nb)[:, None] * QB + jnp.arange(3 * QB)[None, :] - QB
    in_range = (kl >= 0) & (kl < L)
    mask = band[None, :, :] & in_range[:, None, :]

    s = jnp.einsum('bhrnqd,bhrnkd->bhrnqk', qb, kb)
    s = jnp.where(mask, s, -jnp.inf)
    m = jnp.max(s, axis=-1, keepdims=True)
    p = jnp.exp(s - m)
    den = jnp.sum(p, axis=-1, keepdims=True)
    o = jnp.einsum('bhrnqk,bhrnkd->bhrnqd', p, vb) / den
    lse = (m + jnp.log(den))[..., 0]

    o = o.reshape(Bsz, H, r, Lp, hd)[:, :, :, :L].transpose(0, 1, 3, 2, 4).reshape(Bsz, H, S, hd)
    lse = lse.reshape(Bsz, H, r, Lp)[..., :L].transpose(0, 1, 3, 2).reshape(Bsz, H, S)
    return o, lse


def dilated_attention(q, k, v):
    outs, lses = [], []
    for window, dilation in DIL_PAIRS:
        o, l = dilated_branch(q, k, v, window, dilation)
        outs.append(o)
        lses.append(l)
    w = jax.nn.softmax(jnp.stack(lses, axis=0), axis=0)
    return jnp.sum(w[..., None] * jnp.stack(outs, axis=0), axis=0)


def even_mixer(x, w_in, conv_w, w_out):
    Bsz, S, _ = x.shape
    h = x @ w_in
    c = CONV_CH
    bg, cg, xv, q, k, v = jnp.split(
        h, [c, 2 * c, 3 * c, 3 * c + ATTN_W, 3 * c + 2 * ATTN_W], axis=-1)
    y_conv = bg * dwconv3(cg * xv, conv_w)
    q = rope(q.reshape(Bsz, S, N_HEADS, HEAD_DIM))
    k = rope(k.reshape(Bsz, S, N_HEADS, HEAD_DIM))
    v = v.reshape(Bsz, S, N_HEADS, HEAD_DIM)
    qf = q.transpose(0, 2, 1, 3).astype(jnp.float32) * (HEAD_DIM ** -0.5)
    kf = k.transpose(0, 2, 1, 3).astype(jnp.float32)
    vf = v.transpose(0, 2, 1, 3).astype(jnp.float32)
    o = dilated_attention(qf, kf, vf)
    y_attn = o.transpose(0, 2, 1, 3).reshape(Bsz, S, ATTN_W).astype(x.dtype)
    return jnp.concatenate([y_conv, y_attn], axis=-1) @ w_out


def fourier_mixer(x, w_out):
    Bsz, S, D = x.shape
    xg = x.astype(jnp.float32).reshape(Bsz, S, N_FOURIER_GROUPS, D // N_FOURIER_GROUPS)
    f = jnp.fft.fft2(xg, axes=(1, 3), norm="ortho").real
    return f.reshape(Bsz, S, D).astype(x.dtype) @ w_out


def conv_ffn(x, w_up, conv_w, w_down):
    h = dwconv3(x @ w_up, conv_w)
    g, u = jnp.split(h, 2, axis=-1)
    return (jax.nn.silu(g) * u) @ w_down


def trunk(x, w_in_mix, conv_short, w_out_mix, w_out_fourier, ln_mix_g, ln_mix_b,
          w_up, conv_ffn_w, w_down, ln_ffn_g, ln_ffn_b):
    for l in range(DEPTH):
        if l % 2 == 0:
            i = l // 2
            mix = even_mixer(x, w_in_mix[i], conv_short[i], w_out_mix[i])
        else:
            mix = fourier_mixer(x, w_out_fourier[l // 2])
        x = layer_norm(ALPHA * x + mix, ln_mix_g[l], ln_mix_b[l])
        x = layer_norm(ALPHA * x + conv_ffn(x, w_up[l], conv_ffn_w[l], w_down[l]), ln_ffn_g[l], ln_ffn_b[l])
    return x


def setup_inputs(seed: int = 0) -> dict:
    key = jax.random.key(seed)
    ks = jax.random.split(key, 13)
    f32 = jnp.float32
    D = D_MODEL
    nrm = lambda k, shape, scale: jax.random.normal(k, shape, f32) * scale
    return {
        "x_prompt": jax.random.normal(ks[0], (BATCH, SEQ, D), f32),
        "x_sample": jax.random.normal(ks[1], (DEC_BATCH, DEC_SEQ, D), f32),
        "w_in_mix": nrm(ks[2], (N_EVEN, D, D_IN_MIX), D ** -0.5),
        "conv_short": nrm(ks[3], (N_EVEN, 3, CONV_CH), 3 ** -0.5),
        "w_out_mix": nrm(ks[4], (N_EVEN, CONV_CH + ATTN_W, D), BETA * (CONV_CH + ATTN_W) ** -0.5),
        "w_out_fourier": nrm(ks[5], (N_ODD, D, D), BETA * D ** -0.5),
        "ln_mix_g": 1.0 + nrm(ks[6], (DEPTH, D), 0.02),
        "ln_mix_b": nrm(ks[7], (DEPTH, D), 0.02),
        "w_up": nrm(ks[8], (DEPTH, D, 2 * D_FF), D ** -0.5),
        "conv_ffn_w": nrm(ks[9], (DEPTH, 3, 2 * D_FF), 3 ** -0.5),
        "w_down": nrm(ks[10], (DEPTH, D_FF, D), BETA * D_FF ** -0.5),
        "ln_ffn_g": 1.0 + nrm(ks[11], (DEPTH, D), 0.02),
        "ln_ffn_b": nrm(ks[12], (DEPTH, D), 0.02),
    }


def reference(x_prompt, x_sample, w_in_mix, conv_short, w_out_mix, w_out_fourier, ln_mix_g, ln_mix_b,
              w_up, conv_ffn_w, w_down, ln_ffn_g, ln_ffn_b):
    y_prompt = trunk(x_prompt, w_in_mix, conv_short, w_out_mix, w_out_fourier, ln_mix_g, ln_mix_b,
                     w_up, conv_ffn_w, w_down, ln_ffn_g, ln_ffn_b)
    y_sample = trunk(x_sample, w_in_mix, conv_short, w_out_mix, w_out_fourier, ln_mix_g, ln_mix_b,
                     w_up, conv_ffn_w, w_down, ln_ffn_g, ln_ffn_b)
    return (y_prompt, y_sample)
```

```python
import math
from contextlib import ExitStack
import numpy as np
import ml_dtypes
import concourse.bass as bass
import concourse.mybir as mybir
from concourse.bass_utils import run_bass_kernel_spmd

F32 = mybir.dt.float32
BF16 = mybir.dt.bfloat16
AF = mybir.ActivationFunctionType
ALU = mybir.AluOpType

D = 2048
S = 2048
DEPTH = 4
DFF = 5632
NCH_FF = DFF // 128
DIN = 6144
ALPHA = (2 * DEPTH) ** 0.25
LN_EPS = 1e-5
NSEQ = 3
ENG = ("pe", "act", "dve", "pool", "sp")


class DSem:
    def __init__(self, h):
        self.h = h
        self.count = 0


class Rec:
    def __init__(self):
        self.call = None

    def __getattr__(self, name):
        def f(*a, **k):
            self.call = (name, a, k)
            return self
        return f


def freeze(fn):
    r = Rec()
    fn(r)
    name, a, k = r.call
    return lambda e: getattr(e, name)(*a, **k)


class Plan:
    def __init__(self, nc, stack):
        self.nc = nc
        self.stack = stack
        self.q = {e: [] for e in ENG}
        self.cnt = {e: 0 for e in ENG}
        self.sem = {}
        self.waited = {}
        self.nsem = 0
        self.last = {e: None for e in ENG}
        self.new_epoch()

    def mksem(self, name):
        self.nsem += 1
        return self.stack.enter_context(self.nc.semaphore(f"{name}_{self.nsem}"))

    def dsem(self, name):
        return DSem(self.mksem(name))

    def new_epoch(self):
        for e in ("pe", "act", "dve", "pool"):
            self.sem[e] = self.mksem("e" + e)
            self.cnt[e] = 0

    def op(self, eng, fn, sig=True):
        fn = freeze(fn)
        if sig:
            self.cnt[eng] += 1
            tok = (self.sem[eng], self.cnt[eng])
            self.q[eng].append(("op", fn, self.sem[eng]))
            self.last[eng] = tok
            return tok
        self.q[eng].append(("op", fn, None))
        return None

    def wait(self, eng, *toks):
        for tok in toks:
            if tok is None:
                continue
            if isinstance(tok, (list,)):
                self.wait(eng, *tok)
                continue
            sem, val = tok
            key = (eng, id(sem))
            if self.waited.get(key, 0) >= val:
                continue
            self.waited[key] = val
            self.q[eng].append(("wait", sem, val))

    def dma(self, eng, fn, ds):
        fn = freeze(fn)
        ds.count += 16
        self.q[eng].append(("dma", fn, ds.h))
        return (ds.h, ds.count)

    def barrier(self, extra=()):
        toks = [self.last[e] for e in ("pe", "act", "dve", "pool")] + list(extra)
        for e in ENG:
            self.wait(e, *toks)

    def emit(self, eng, e):
        for kind, a, b in self.q[eng]:
            if kind == "op":
                inst = a(e)
                if b is not None:
                    inst.then_inc(b, 1)
            elif kind == "wait":
                e.wait_ge(a, b)
            else:
                a(e).then_inc(b, 16)


def build(nseq=NSEQ, mode="full"):
    nc = bass.Bass("TRN2", target_bir_lowering=False)
    stack = ExitStack()
    with stack:
        _build(nc, stack, nseq, mode)
    return nc


def _build(nc, stack, nseq, mode):
    def din(name, shape, dt=F32):
        return nc.dram_tensor(name, list(shape), dt, kind="ExternalInput").ap()

    def dscr(name, shape, dt):
        return nc.dram_tensor(name, list(shape), dt, kind="Internal").ap()

    x_in = din("x", [nseq, S, D])
    y_out = nc.dram_tensor("y", [nseq, S, D], F32, kind="ExternalOutput").ap()
    w_in = din("w_in_mix", [2, D, DIN])
    conv_short = din("conv_short", [2, 3, 1024])
    w_om = din("w_out_mix", [2, D, D])
    w_of = din("w_out_fourier", [2, D, D])
    ln_mix_g = din("ln_mix_g", [4, D]); ln_mix_b = din("ln_mix_b", [4, D])
    w_up = din("w_up", [4, D, 2 * DFF])
    conv_ffn = din("conv_ffn_w", [4, 3, 2 * DFF])
    w_dn = din("w_down", [4, DFF, D])
    ln_ffn_g = din("ln_ffn_g", [4, D]); ln_ffn_b = din("ln_ffn_b", [4, D])
    c_ident = din("c_ident", [128, 128], BF16)
    c_perm = din("c_perm", [128, 128], BF16)
    c_identf = din("c_identf", [128, 128], F32)
    c_cos = din("c_cos", [128, S], F32)
    c_sin = din("c_sin", [128, S], F32)
    c_mask = din("c_mask", [7, 128, 512], BF16)
    c_cc = din("c_cc", [512, 512], BF16)
    c_sc = din("c_sc", [512, 512], BF16)
    c_cs = din("c_cs", [S, S], BF16)
    c_ns = din("c_ns", [S, S], BF16)

    dbg = nc.dram_tensor("dbg", [128, 8192], F32, kind="ExternalOutput").ap() if mode != "full" else None
    ds_dbg = None
    dbg16 = nc.dram_tensor("dbg16", [128, 8192], BF16, kind="ExternalOutput").ap() if mode != "full" else None
    wb_in = dscr("wb_in", [2, D, DIN], BF16)
    wb_om = dscr("wb_om", [2, D, D], BF16)
    wb_of = dscr("wb_of", [2, D, D], BF16)
    wb_up = dscr("wb_up", [4, D, 2 * DFF], BF16)
    wb_dn = dscr("wb_dn", [4, DFF, D], BF16)
    resid = dscr("resid", [nseq, S, D], F32)
    ycat = dscr("ycat", [nseq, D, S], BF16)

    sb = lambda name, shape, dt: stack.enter_context(nc.sbuf_tensor(name, list(shape), dt))
    xT = sb("xT", [128, 16, 4, 514], BF16)
    ident = sb("ident", [128, 128], BF16)
    perm = sb("perm", [128, 128], BF16)
    ones = sb("ones", [128, 128], BF16)
    identf = sb("identf", [128, 128], F32)
    cw = sb("cw", [128, 4, 3, 88], F32)
    cs = sb("cs", [128, 2, 3, 8], F32)
    small = sb("small", [128, 64], F32)
    NW = 34816
    WK = sb("WK", [128, NW], F32)
    ps = [stack.enter_context(nc.psum_tensor(f"ps{i}", [128, 1024], F32)) for i in range(4)]

    def cv(off, shape, dt):
        n = int(np.prod(shape))
        if dt == F32:
            assert off % 4 == 0
            ap = WK[:, off // 4: off // 4 + n]
        else:
            ap = WK[:, off // 4: off // 4 + (n + 1) // 2].bitcast(BF16)
        if len(shape) == 1:
            return ap
        if len(shape) == 2:
            return ap.rearrange("p (a b) -> p a b", a=shape[0])
        if len(shape) == 3:
            return ap.rearrange("p (a b c) -> p a b c", a=shape[0], b=shape[1])
        raise ValueError

    def bank(i):
        return ps[i // 2][:, (i % 2) * 512:(i % 2) * 512 + 512]

    def bank_bf(i):
        return ps[i // 2][:, (i % 2) * 512:(i % 2) * 512 + 512].bitcast(BF16)

    P = Plan(nc, stack)

    acc = cv(0, [4, 2048], F32)
    xres = cv(32768, [2, 2048], F32)
    ybf = cv(49152, [2048], BF16)
    gam = cv(53248, [2048], F32)
    bet = cv(61440, [2048], F32)
    wdn = cv(69632, [3, 2, 2048], BF16)
    wup = cv(94208, [3, 16, 256], BF16)
    aT = cv(118784, [4, 2, 512], BF16)
    ctmp = cv(126976, [2, 512], F32)
    sgb = cv(131072, [4, 512], F32)
    kopb = cv(94208, [2, 16, 512], BF16)
    assert 139264 <= NW * 4
    st6 = small[:, 0:48].rearrange("p (b c s) -> p b c s", b=2, c=4)
    mv = small[:, 48:52].rearrange("p (b s) -> p b s", b=2)
    rstd = small[:, 52:54]
    nmr = small[:, 54:56]
    sdv = small[:, 56:58]

    ds_const = P.dsem("const")
    ds_cast = {}
    ds_wdn = [P.dsem("wdn") for _ in range(3)]
    ds_wup = [P.dsem("wup") for _ in range(3)]
    ds_xres = [P.dsem("xres") for _ in range(2)]
    ds_yst = [P.dsem("yst") for _ in range(2)]
    ds_gb = P.dsem("gb")
    ds_kop = [P.dsem("kop") for _ in range(2)]
    ds_misc = P.dsem("misc")
    ds_dbg = P.dsem("dbg")
    ds_blk = [P.dsem("blk") for _ in range(4)]
    ds_f = P.dsem("fst")
    ds_win = [P.dsem("win") for _ in range(3)]
    ds_y = [P.dsem("ycs") for _ in range(2)]
    ds_ya = P.dsem("yat")

    def dump(ap, col0, n, *toks):
        if dbg is None:
            return
        P.wait("sp", *toks)
        tgt = dbg16 if ap.dtype == BF16 else dbg
        t = P.dma("sp", lambda e: e.dma_start(out=tgt[:, col0:col0 + n], in_=ap), ds_dbg)
        for e_ in ENG:
            P.wait(e_, t)

    ctoks = []
    ctoks.append(P.dma("sp", lambda e: e.dma_start(out=ident[:], in_=c_ident[:, :]), ds_const))
    ctoks.append(P.dma("sp", lambda e: e.dma_start(out=perm[:], in_=c_perm[:, :]), ds_const))
    ctoks.append(P.dma("sp", lambda e: e.dma_start(out=identf[:], in_=c_identf[:, :]), ds_const))
    cwraw = cv(0, [4, 3, 128], F32)
    csraw = cv(8192, [2, 3, 128], F32)
    for l in range(4):
        ctoks.append(P.dma("sp", lambda e, l=l: e.dma_start(
            out=cwraw[0:88, l], in_=conv_ffn[l].rearrange("k (c p) -> c k p", p=128)), ds_const))
    for i in range(2):
        ctoks.append(P.dma("sp", lambda e, i=i: e.dma_start(
            out=csraw[0:8, i], in_=conv_short[i].rearrange("k (c p) -> c k p", p=128)), ds_const))
    P.wait("pe", ctoks[-1]); P.wait("pool", ctoks[-1]); P.wait("dve", ctoks[-1]); P.wait("act", ctoks[-1])
    P.op("pool", lambda e: e.memset(ones[:], 1.0))
    P.op("pool", lambda e: e.memset(xT[:, :, 0, 0:1], 0.0))
    tz = P.op("pool", lambda e: e.memset(xT[:, :, 3, 513:514], 0.0))
    tcw = None
    for l in range(4):
        for k in range(3):
            idx = l * 3 + k
            bk, col = idx // 5, (idx % 5) * 88
            tcw = P.op("pe", lambda e, l=l, k=k, bk=bk, col=col: e.transpose(
                out=bank(bk)[:, col:col + 88], in_=cwraw[0:88, l, k, :], identity=identf[0:88, 0:88]),
                sig=(idx == 11))
    P.wait("dve", tcw)
    for l in range(4):
        for k in range(3):
            idx = l * 3 + k
            bk, col = idx // 5, (idx % 5) * 88
            P.op("dve", lambda e, l=l, k=k, bk=bk, col=col: e.tensor_copy(out=cw[:, l, k, :], in_=bank(bk)[:, col:col + 88]))
    tcs = None
    for i in range(2):
        for k in range(3):
            idx = i * 3 + k
            tcs = P.op("pe", lambda e, i=i, k=k, idx=idx: e.transpose(
                out=bank(3)[:, idx * 8:idx * 8 + 8], in_=csraw[0:8, i, k, :], identity=identf[0:8, 0:8]),
                sig=(idx == 5))
    P.wait("dve", tcs)
    for i in range(2):
        for k in range(3):
            idx = i * 3 + k
            P.op("dve", lambda e, i=i, k=k, idx=idx: e.tensor_copy(out=cs[:, i, k, :], in_=bank(3)[:, idx * 8:idx * 8 + 8]))

    def cast(name, src, dst, l):
        sflat = src[l].rearrange("k (j c) -> (k j) c", c=1024)
        dflat = dst[l].rearrange("k (j c) -> (k j) c", c=1024)
        rows = sflat.shape[0]
        dsx = P.dsem("cast" + name)
        tok = None
        r = 0
        while r < rows:
            n = min(8192, rows - r)
            tok = P.dma("pool", lambda e, r=r, n=n: e.dma_start(out=dflat[r:r + n, :], in_=sflat[r:r + n, :]), dsx)
            r += n
        ds_cast[(name, l)] = tok

    for l in range(4):
        if l % 2 == 0:
            cast("in", w_in, wb_in, l // 2); cast("om", w_om, wb_om, l // 2)
        else:
            cast("of", w_of, wb_of, l // 2)
        cast("up", w_up, wb_up, l); cast("dn", w_dn, wb_dn, l)

    st = dict(wdn_i=0, wup_i=0, bankC=0, xres_i=0, chunk_i=0)
    wdn_free = [None] * 3
    wup_free = [None] * 3
    bankC_free = [None, None]
    bankD_free = [None]
    acc_free = [None] * 4
    xres_free = [None, None]
    ybf_free = [None]
    xT_ready = []
    pair_free = [None, None]
    aT_free = [None] * 4
    sg_free = [None] * 4
    ct_free = [None, None]
    gb_tok = [None]

    def xcol(t):
        return t // 512, t % 512 + 1

    def writeback(ybf_tok, t0, ffn_inplace):
        slot, c0 = xcol(t0)
        P.wait("pe", ybf_tok, bankD_free[0])
        tp = None
        for d in range(16):
            tp = P.op("pe", lambda e, d=d: e.transpose(
                out=bank_bf(6 + d // 8)[:, (d % 8) * 128:(d % 8) * 128 + 128],
                in_=ybf[:, d * 128:(d + 1) * 128], identity=ident[:]), sig=(d == 15))
        ybf_free[0] = tp
        toks = []
        for half, eng in ((0, "dve"), (1, "act")):
            src = bank_bf(6 + half).rearrange("p (d c) -> p d c", d=8)
            P.wait(eng, tp)
            if eng == "dve":
                cp = lambda e, o, i: e.tensor_copy(out=o, in_=i)
            else:
                cp = lambda e, o, i: e.activation(out=o, in_=i, func=AF.Copy)
            dsl = slice(half * 8, half * 8 + 8)
            tk = P.op(eng, lambda e, cp=cp, dsl=dsl, src=src: cp(e, xT[:, dsl, slot, c0:c0 + 128], src))
            if t0 % 512 == 0 and slot > 0:
                tk = P.op(eng, lambda e, cp=cp, dsl=dsl, src=src: cp(e, xT[:, dsl, slot - 1, 513:514], src[:, :, 0:1]))
            if t0 % 512 == 384 and slot < 3 and not ffn_inplace:
                tk = P.op(eng, lambda e, cp=cp, dsl=dsl, src=src: cp(e, xT[:, dsl, slot + 1, 0:1], src[:, :, 127:128]))
            toks.append(tk)
        bankD_free[0] = toks
        xT_ready[:] = toks
        return toks

    def load_gb(g_ap, b_ap, l):
        P.wait("sp", P.last["pool"], P.last["act"], P.last["dve"])
        P.dma("sp", lambda e: e.dma_start(out=gam, in_=g_ap[l:l + 1, :].partition_broadcast(128)), ds_gb)
        gb_tok[0] = P.dma("sp", lambda e: e.dma_start(out=bet, in_=b_ap[l:l + 1, :].partition_broadcast(128)), ds_gb)

    def g2_group(q, nq, kops, wsrc, cast_tok, kop_toks):
        slot = st["wdn_i"] % 3
        st["wdn_i"] += 1
        P.wait("sp", wdn_free[slot], cast_tok)
        ld = P.dma("sp", lambda e: e.dma_start(out=wdn[:, slot], in_=wsrc), ds_wdn[slot])
        P.wait("pe", ld, *kop_toks)
        last_pe = None
        for s in range(4):
            for n in range(4):
                bi = st["bankC"] % 2
                st["bankC"] += 1
                P.wait("pe", bankC_free[bi])
                for g in range(2):
                    last_pe = P.op("pe", lambda e, s=s, n=n, g=g, bi=bi: e.matmul(
                        bank(4 + bi), lhsT=kops[g][:, s * 128:(s + 1) * 128],
                        rhs=wdn[:, slot, g, n * 512:(n + 1) * 512], start=(g == 0), stop=(g == 1)), sig=(g == 1))
                P.wait("dve", last_pe)
                dst = acc[:, s, n * 512:(n + 1) * 512]
                if q == 0:
                    P.wait("dve", acc_free[s])
                    bankC_free[bi] = P.op("dve", lambda e, dst=dst, bi=bi: e.tensor_copy(out=dst, in_=bank(4 + bi)))
                else:
                    bankC_free[bi] = P.op("dve", lambda e, dst=dst, bi=bi: e.tensor_tensor(
                        out=dst, in0=dst, in1=bank(4 + bi), op=ALU.add))
        wdn_free[slot] = last_pe
        return last_pe

    def g2_finish(seq, tt, rsrc, rdst, ffn_inplace):
        for s in range(4):
            t0 = tt * 512 + s * 128
            b = st["xres_i"] % 2
            st["xres_i"] += 1
            P.wait("sp", xres_free[b])
            ldx = P.dma("sp", lambda e, b=b, t0=t0: e.dma_start(out=xres[:, b], in_=rsrc[seq, t0:t0 + 128, :]), ds_xres[b])
            P.wait("dve", ldx, P.last["dve"], P.last["act"])
            P.op("dve", lambda e, b=b, s=s: e.scalar_tensor_tensor(
                out=acc[:, s], in0=xres[:, b], scalar=float(ALPHA), in1=acc[:, s], op0=ALU.mult, op1=ALU.add))
            P.wait("dve", P.last["dve"])
            for c in range(4):
                P.op("dve", lambda e, b=b, s=s, c=c: e.bn_stats(out=st6[:, b, c], in_=acc[:, s, c * 512:(c + 1) * 512]))
            P.wait("dve", P.last["dve"])
            P.op("dve", lambda e, b=b: e.bn_aggr(out=mv[:, b], in_=st6[:, b].rearrange("p c s -> p (c s)")))
            P.wait("dve", P.last["dve"])
            P.wait("act", P.last["dve"])
            tsd = P.op("act", lambda e, b=b: e.activation(
                out=sdv[:, b:b + 1], in_=mv[:, b, 1:2], func=AF.Sqrt, bias=float(LN_EPS), scale=1.0))
            P.wait("dve", tsd)
            P.op("dve", lambda e, b=b: e.reciprocal(out=rstd[:, b:b + 1], in_=sdv[:, b:b + 1]))
            P.wait("dve", P.last["dve"])
            tstat = P.op("dve", lambda e, b=b: e.scalar_tensor_tensor(
                out=nmr[:, b:b + 1], in0=mv[:, b, 0:1], scalar=-1.0, in1=rstd[:, b:b + 1], op0=ALU.mult, op1=ALU.mult))
            P.wait("act", tstat)
            txh = P.op("act", lambda e, b=b, s=s: e.activation(
                out=xres[:, b], in_=acc[:, s], func=AF.Identity, scale=rstd[:, b:b + 1], bias=nmr[:, b:b + 1]))
            acc_free[s] = txh
            P.wait("pool", txh, gb_tok[0])
            P.op("pool", lambda e, b=b: e.tensor_tensor(out=xres[:, b], in0=xres[:, b], in1=gam, op=ALU.mult))
            P.wait("pool", P.last["pool"])
            ty = P.op("pool", lambda e, b=b: e.tensor_tensor(out=xres[:, b], in0=xres[:, b], in1=bet, op=ALU.add))
            P.wait("act", ty, ybf_free[0])
            tyb = P.op("act", lambda e, b=b: e.activation(out=ybf, in_=xres[:, b], func=AF.Copy))
            P.wait("sp", ty, tyb)
            xres_free[b] = P.dma("sp", lambda e, b=b, t0=t0: e.dma_start(out=rdst[seq, t0:t0 + 128, :], in_=xres[:, b]), ds_yst[b])
            writeback(tyb, t0, ffn_inplace)

    def load_x(seq):
        for s in range(16):
            t0 = s * 128
            b = st["xres_i"] % 2
            st["xres_i"] += 1
            P.wait("sp", xres_free[b])
            ldx = P.dma("sp", lambda e, b=b, t0=t0: e.dma_start(out=xres[:, b], in_=x_in[seq, t0:t0 + 128, :]), ds_xres[b])
            P.wait("act", ldx, ybf_free[0])
            tyb = P.op("act", lambda e, b=b: e.activation(out=ybf, in_=xres[:, b], func=AF.Copy))
            xres_free[b] = tyb
            writeback(tyb, t0, False)

    def ffn(seq, l, rsrc, rdst):
        load_gb(ln_ffn_g, ln_ffn_b, l)
        wu = wb_up[l].rearrange("(kc p) n -> p kc n", p=128)
        wd = wb_dn[l].rearrange("(j p) n -> p j n", p=128)
        for tt in range(4):
            u_done = None
            grp_toks = {}
            for b in range(22):
                for half in range(2):
                    col0 = half * DFF + b * 256
                    slot = st["wup_i"] % 3
                    st["wup_i"] += 1
                    P.wait("sp", wup_free[slot], ds_cast[("up", l)])
                    ld = P.dma("sp", lambda e, slot=slot, col0=col0: e.dma_start(
                        out=wup[:, slot], in_=wu[:, :, col0:col0 + 256]), ds_wup[slot])
                    if seq == 0 and tt == 0 and b == 0 and half == 0:
                        dump(wup[:, slot].rearrange("p k c -> p (k c)"), 2048, 4096, ld)
                    P.wait("pe", ld)
                    for ci in range(2):
                        ch = half * NCH_FF + b * 2 + ci
                        pi = st["chunk_i"] % 2
                        st["chunk_i"] += 1
                        P.wait("pe", pair_free[pi])
                        lp = None
                        for kc in range(16):
                            for hh in range(2):
                                lp = P.op("pe", lambda e, kc=kc, hh=hh, pi=pi, slot=slot, ci=ci: e.matmul(
                                    bank(2 * pi + hh)[:, 0:258], lhsT=wup[:, slot, kc, ci * 128:(ci + 1) * 128],
                                    rhs=xT[:, kc, tt, hh * 256:hh * 256 + 258], start=(kc == 0), stop=(kc == 15)),
                                    sig=(kc == 15 and hh == 1))
                        u_done = lp
                        pv = ps[pi].rearrange("p (h c) -> p h c", h=2)
                        cti = pi
                        ct = ctmp[:, cti].rearrange("p (h c) -> p h c", h=2)
                        P.wait("act", lp, ct_free[cti])
                        ta = P.op("act", lambda e, pv=pv, ct=ct, ch=ch: e.activation(
                            out=ct, in_=pv[:, :, 1:257], func=AF.Identity, scale=cw[:, l, 1, ch:ch + 1]))
                        P.wait("dve", ta)
                        tb = P.op("dve", lambda e, pv=pv, ct=ct, ch=ch: e.scalar_tensor_tensor(
                            out=ct, in0=pv[:, :, 0:256], scalar=cw[:, l, 0, ch:ch + 1], in1=ct, op0=ALU.mult, op1=ALU.add))
                        P.wait("dve", tb)
                        tc = P.op("dve", lambda e, pv=pv, ct=ct, ch=ch: e.scalar_tensor_tensor(
                            out=ct, in0=pv[:, :, 2:258], scalar=cw[:, l, 2, ch:ch + 1], in1=ct, op0=ALU.mult, op1=ALU.add))
                        pair_free[pi] = tc
                        if seq == 0 and tt == 0 and b == 0 and ci == 0:
                            dump(ctmp[:, cti], 0 if half == 0 else 1024, 512, tc)
                        sgi = (b % 2) * 2 + ci
                        if half == 0:
                            P.wait("act", tc, sg_free[sgi])
                            ct_free[cti] = P.op("act", lambda e, cti=cti, sgi=sgi: e.activation(
                                out=sgb[:, sgi], in_=ctmp[:, cti], func=AF.Silu))
                            grp_toks[(b, ci, "sg")] = ct_free[cti]
                            if seq == 0 and tt == 0 and b == 0 and ci == 0:
                                dump(sgb[:, sgi], 512, 512, ct_free[cti])
                        else:
                            gi = b % 4
                            P.wait("pool", tc, grp_toks[(b, ci, "sg")], aT_free[gi] if ci == 0 else None)
                            tm = P.op("pool", lambda e, cti=cti, sgi=sgi, gi=gi, ci=ci: e.tensor_tensor(
                                out=aT[:, gi, ci], in0=sgb[:, sgi], in1=ctmp[:, cti], op=ALU.mult))
                            ct_free[cti] = tm
                            sg_free[sgi] = tm
                            grp_toks[(b, ci, "a")] = tm
                    wup_free[slot] = u_done
                if b >= 1:
                    q = b - 1
                    gi = q % 4
                    aT_free[gi] = g2_group(q, 22, [aT[:, gi, 0], aT[:, gi, 1]], wd[:, 2 * q:2 * q + 2, :],
                                           ds_cast[("dn", l)], [grp_toks[(q, 0, "a")], grp_toks[(q, 1, "a")]])
            q = 21
            gi = q % 4
            aT_free[gi] = g2_group(q, 22, [aT[:, gi, 0], aT[:, gi, 1]], wd[:, 2 * q:2 * q + 2, :],
                                   ds_cast[("dn", l)], [grp_toks[(q, 0, "a")], grp_toks[(q, 1, "a")]])
            if tt >= 1:
                P.wait("pool", u_done, *xT_ready)
                P.op("pool", lambda e, tt=tt: e.tensor_copy(out=xT[:, :, tt, 0:1], in_=xT[:, :, tt - 1, 512:513]))
            if seq == 0 and tt == 0:
                dump(acc[:, 0], 2048, 2048, P.last["dve"])
            g2_finish(seq, tt, rsrc, rdst, True)

    def m2(seq, wsrc_l, cast_tok, g_ap, b_ap, l, rsrc, rdst, ycat_tok):
        load_gb(g_ap, b_ap, l)
        wd = wsrc_l.rearrange("(j p) n -> p j n", p=128)
        yc = ycat[seq].rearrange("(kc p) t -> p kc t", p=128)
        kop_free = [None, None]
        for tt in range(4):
            kb = tt % 2
            P.wait("sp", kop_free[kb], ycat_tok)
            ldk = P.dma("sp", lambda e, kb=kb, tt=tt: e.dma_start(out=kopb[:, kb], in_=yc[:, :, tt * 512:(tt + 1) * 512]), ds_kop[kb])
            lp = None
            for q in range(8):
                lp = g2_group(q, 8, [kopb[:, kb, 2 * q], kopb[:, kb, 2 * q + 1]], wd[:, 2 * q:2 * q + 2, :], cast_tok, [ldk])
            kop_free[kb] = lp
            g2_finish(seq, tt, rsrc, rdst, False)

    def fourier_m1(seq):
        Yc = cv(0, [16, 512], BF16)
        Ys = cv(16384, [16, 512], BF16)
        CC = cv(32768, [4, 512], BF16)
        SC = cv(36864, [4, 512], BF16)
        blk = cv(40960, [4, 16, 256], BF16)
        fst = cv(73728, [4, 2048], BF16)
        blk_free = [None] * 4
        P.wait("sp", P.last["pe"])
        P.dma("sp", lambda e: e.dma_start(out=CC, in_=c_cc.rearrange("(k p) n -> p k n", p=128)), ds_misc)
        tcc = P.dma("sp", lambda e: e.dma_start(out=SC, in_=c_sc.rearrange("(k p) n -> p k n", p=128)), ds_misc)
        csv = c_cs.rearrange("(k p) n -> p k n", p=128)
        nsv = c_ns.rearrange("(k p) n -> p k n", p=128)
        bC = [None, None]
        y_free = [None]
        fst_free = [None]
        cnt = 0
        for g in range(4):
            P.wait("pe", tcc)
            ev = []
            for stl in range(16):
                slot, c0 = xcol(stl * 128)
                for which, (M, Yb, eng) in enumerate(((CC, Yc, "act"), (SC, Ys, "dve"))):
                    bi = which
                    P.wait("pe", bC[bi], y_free[0] if stl == 0 else None)
                    lp = None
                    for kc in range(4):
                        lp = P.op("pe", lambda e, kc=kc, slot=slot, c0=c0, M=M, bi=bi: e.matmul(
                            bank(4 + bi), lhsT=xT[:, 4 * g + kc, slot, c0:c0 + 128], rhs=M[:, kc, :],
                            start=(kc == 0), stop=(kc == 3)), sig=(kc == 3))
                    P.wait(eng, lp)
                    if eng == "act":
                        bC[bi] = P.op("act", lambda e, Yb=Yb, stl=stl, bi=bi: e.activation(out=Yb[:, stl], in_=bank(4 + bi), func=AF.Copy))
                    else:
                        bC[bi] = P.op("dve", lambda e, Yb=Yb, stl=stl, bi=bi: e.tensor_copy(out=Yb[:, stl], in_=bank(4 + bi)))
                    ev.append(bC[bi])
            P.wait("pe", ev[-1], ev[-2])
            P.wait("act", fst_free[0])
            lastpe = None
            for sp_ in range(8):
                bs = sp_ % 2
                P.wait("sp", blk_free[bs], blk_free[2 + bs])
                l1 = P.dma("sp", lambda e, bs=bs, sp_=sp_: e.dma_start(out=blk[:, bs], in_=csv[:, :, sp_ * 256:(sp_ + 1) * 256]), ds_blk[bs])
                l2 = P.dma("sp", lambda e, bs=bs, sp_=sp_: e.dma_start(out=blk[:, 2 + bs], in_=nsv[:, :, sp_ * 256:(sp_ + 1) * 256]), ds_blk[2 + bs])
                P.wait("pe", l1, l2)
                for cp_ in range(4):
                    bi = cnt % 2
                    cnt += 1
                    P.wait("pe", pair_free[bi])
                    lp = None
                    for which, Yb in enumerate((Yc, Ys)):
                        for sc in range(16):
                            lp = P.op("pe", lambda e, Yb=Yb, sc=sc, cp_=cp_, bs=bs, which=which, bi=bi: e.matmul(
                                bank(bi)[:, 0:256], lhsT=Yb[:, sc, cp_ * 128:(cp_ + 1) * 128], rhs=blk[:, 2 * which + bs, sc, :],
                                start=(which == 0 and sc == 0), stop=(which == 1 and sc == 15)),
                                sig=(which == 1 and sc == 15))
                    lastpe = lp
                    P.wait("act", lp)
                    pair_free[bi] = P.op("act", lambda e, cp_=cp_, sp_=sp_, bi=bi: e.activation(
                        out=fst[:, cp_, sp_ * 256:(sp_ + 1) * 256], in_=bank(bi)[:, 0:256], func=AF.Copy))
                blk_free[bs] = lastpe
                blk_free[2 + bs] = lastpe
            y_free[0] = lastpe
            P.wait("sp", P.last["act"])
            for cp_ in range(4):
                ch = 4 * g + cp_
                fst_free[0] = P.dma("sp", lambda e, cp_=cp_, ch=ch: e.dma_start(
                    out=ycat[seq, ch * 128:(ch + 1) * 128, :], in_=fst[:, cp_]), ds_f)
        return fst_free[0]

    def even_m1(seq, li):
        wi = wb_in[li].rearrange("(kc p) n -> p kc n", p=128)
        winb = cv(0, [3, 16, 256], BF16)
        cosb = cv(24576, [2048], F32)
        sinb = cv(32768, [2048], F32)
        maskb = cv(40960, [7, 512], BF16)
        stg = cv(49152, [6, 2050], F32)
        yst = cv(98352 + 16, [2, 2048], BF16)
        qb = cv(49152, [2, 2048], BF16)
        r1 = cv(57344, [512], F32)
        r2 = cv(59392, [512], F32)
        qT = cv(61440, [2, 2048], BF16)
        kT = cv(69632, [2, 2048], BF16)
        vT = cv(77824, [2, 2048], BF16)
        Vl = cv(86016, [3, 16, 128], BF16)
        Pb = cv(106560, [3, 512], BF16)
        rden = cv(109632, [512], F32)
        yat = cv(111680, [2048], BF16)
        win_free = [None] * 3
        wi_i = [0]
        P.wait("sp", P.last["pe"], P.last["act"], P.last["dve"], P.last["pool"])
        P.dma("sp", lambda e: e.dma_start(out=cosb, in_=c_cos[:, :]), ds_misc)
        P.dma("sp", lambda e: e.dma_start(out=sinb, in_=c_sin[:, :]), ds_misc)
        tconst = P.dma("sp", lambda e: e.dma_start(out=maskb, in_=c_mask.rearrange("m p c -> p m c")), ds_misc)
        for e_ in ("pe", "act", "dve", "pool"):
            P.wait(e_, tconst)
        for i in (4, 5):
            P.op("pool", lambda e, i=i: e.memset(stg[:, i, 0:1], 0.0))
            P.op("pool", lambda e, i=i: e.memset(stg[:, i, 2049:2050], 0.0))
        tpad = P.last["pool"]
        bk_free = [None] * 4
        bki = [0]
        ycat_tok = [None]

        def inproj_block(col0, epi):
            slot = wi_i[0] % 3
            wi_i[0] += 1
            P.wait("sp", win_free[slot], ds_cast[("in", li)])
            ld = P.dma("sp", lambda e: e.dma_start(out=winb[:, slot], in_=wi[:, :, col0:col0 + 256]), ds_win[slot])
            P.wait("pe", ld)
            lp = None
            for ci in range(2):
                for tt in range(4):
                    bi = bki[0] % 4
                    bki[0] += 1
                    P.wait("pe", bk_free[bi])
                    for kc in range(16):
                        lp = P.op("pe", lambda e, kc=kc, ci=ci, tt=tt, bi=bi: e.matmul(
                            bank(bi), lhsT=winb[:, slot, kc, ci * 128:(ci + 1) * 128], rhs=xT[:, kc, tt, 1:513],
                            start=(kc == 0), stop=(kc == 15)), sig=(kc == 15))
                    bk_free[bi] = epi(ci, tt, bi, lp)
            win_free[slot] = lp

        yst_free = [None, None]
        for c in range(4):
            def epi_copy(base):
                def f(ci, tt, bi, lp):
                    P.wait("act", lp, yst_free[ci] if tt == 0 else None, tpad)
                    return P.op("act", lambda e: e.activation(
                        out=stg[:, base + ci, 1 + tt * 512:1 + (tt + 1) * 512], in_=bank(bi), func=AF.Copy))
                return f
            inproj_block(0 * 1024 + c * 256, epi_copy(0))
            inproj_block(1 * 1024 + c * 256, epi_copy(2))

            def epi_u(ci, tt, bi, lp):
                P.wait("dve", lp, P.last["act"])
                return P.op("dve", lambda e: e.tensor_tensor(
                    out=stg[:, 4 + ci, 1 + tt * 512:1 + (tt + 1) * 512], in0=stg[:, 2 + ci, 1 + tt * 512:1 + (tt + 1) * 512],
                    in1=bank(bi), op=ALU.mult))
            inproj_block(2 * 1024 + c * 256, epi_u)
            for ci in range(2):
                ch = c * 2 + ci
                u = stg[:, 4 + ci]
                tmp = stg[:, 2 + ci, 1:2049]
                P.wait("act", P.last["dve"])
                t1 = P.op("act", lambda e, u=u, tmp=tmp, ch=ch: e.activation(
                    out=tmp, in_=u[:, 1:2049], func=AF.Identity, scale=cs[:, li, 1, ch:ch + 1]))
                P.wait("dve", t1)
                t2 = P.op("dve", lambda e, u=u, tmp=tmp, ch=ch: e.scalar_tensor_tensor(
                    out=tmp, in0=u[:, 0:2048], scalar=cs[:, li, 0, ch:ch + 1], in1=tmp, op0=ALU.mult, op1=ALU.add))
                P.wait("dve", t2)
                t3 = P.op("dve", lambda e, u=u, tmp=tmp, ch=ch: e.scalar_tensor_tensor(
                    out=tmp, in0=u[:, 2:2050], scalar=cs[:, li, 2, ch:ch + 1], in1=tmp, op0=ALU.mult, op1=ALU.add))
                P.wait("pool", t3, yst_free[ci])
                t4 = P.op("pool", lambda e, ci=ci, tmp=tmp: e.tensor_tensor(
                    out=yst[:, ci], in0=tmp, in1=stg[:, ci, 1:2049], op=ALU.mult))
                P.wait("sp", t4)
                yst_free[ci] = P.dma("sp", lambda e, ci=ci, ch=ch: e.dma_start(
                    out=ycat[seq, ch * 128:(ch + 1) * 128, :], in_=yst[:, ci]), ds_y[ci])
                ycat_tok[0] = yst_free[ci]
        last_conv = [yst_free[0], yst_free[1]]

        P.barrier(extra=last_conv)
        scale = 1.0 / math.sqrt(128.0)
        scb_free = [None] * 3
        pb_free = [None] * 3
        nd_free = [None]
        yat_free = [None]
        sci = [0]
        for hp in range(4):
            def epi_rope(dst):
                def f(ci, tt, bi, lp):
                    sl = slice(tt * 512, (tt + 1) * 512)
                    P.wait("act", lp, P.last["pe"] if tt == 0 else None)
                    ta = P.op("act", lambda e: e.activation(out=qb[:, ci, sl], in_=bank(bi), func=AF.Copy))
                    P.wait("pe", ta, nd_free[0])
                    tsw = P.op("pe", lambda e: e.matmul(bank(4), lhsT=perm[:], rhs=qb[:, ci, sl], start=True, stop=True))
                    P.wait("dve", tsw, P.last["pool"])
                    t2_ = P.op("dve", lambda e: e.tensor_tensor(out=r2, in0=bank(4), in1=sinb[:, sl], op=ALU.mult))
                    nd_free[0] = t2_
                    P.wait("pool", ta, P.last["pool"])
                    t1_ = P.op("pool", lambda e: e.tensor_tensor(out=r1, in0=qb[:, ci, sl], in1=cosb[:, sl], op=ALU.mult))
                    P.wait("pool", t1_, t2_)
                    return P.op("pool", lambda e: e.tensor_tensor(out=dst[:, ci, sl], in0=r1, in1=r2, op=ALU.add))
                return f
            inproj_block(3 * 1024 + hp * 256, epi_rope(qT))
            inproj_block(4 * 1024 + hp * 256, epi_rope(kT))

            def epi_v(ci, tt, bi, lp):
                P.wait("act", lp)
                return P.op("act", lambda e: e.activation(out=vT[:, ci, tt * 512:(tt + 1) * 512], in_=bank(bi), func=AF.Copy))
            inproj_block(5 * 1024 + hp * 256, epi_v)
            for ci in range(2):
                h = hp * 2 + ci
                P.wait("pe", P.last["act"], P.last["pool"], P.last["dve"])
                for li_, r in enumerate((1, 4, 16)):
                    ntile = 16 // r
                    for half in range(2):
                        tp = None
                        for jj in range(8):
                            j = half * 8 + jj
                            ph, kt = j // ntile, j % ntile
                            src = vT[:, ci, :].rearrange("p (l r) -> p r l", r=r)[:, ph, kt * 128:(kt + 1) * 128]
                            tp = P.op("pe", lambda e, src=src, jj=jj, half=half: e.transpose(
                                out=bank_bf(6 + half)[:, jj * 128:(jj + 1) * 128], in_=src, identity=ident[:]), sig=(jj == 7))
                        eng = "act" if half == 0 else "dve"
                        P.wait(eng, tp)
                        srcv = bank_bf(6 + half).rearrange("p (j c) -> p j c", j=8)
                        if eng == "act":
                            te = P.op("act", lambda e, li_=li_, half=half, srcv=srcv: e.activation(
                                out=Vl[:, li_, half * 8:half * 8 + 8, :], in_=srcv, func=AF.Copy))
                        else:
                            te = P.op("dve", lambda e, li_=li_, half=half, srcv=srcv: e.tensor_copy(
                                out=Vl[:, li_, half * 8:half * 8 + 8, :], in_=srcv))
                        P.wait("pe", te)
                for Qb in range(4):
                    i0 = Qb * 512
                    banks_ = []
                    segA, segB = [], []
                    col = 0
                    for (rel, n, mc, q0) in ((-1, 64, 192, 0), (0, 192, 64, 0), (1, 256, 0, 64)):
                        kt = 4 * Qb + rel
                        segA.append((0, kt, ("c", i0 + q0, n, 1), ("c", kt * 128, 1), col, n, ("c", q0, n, 1)))
                        col += n
                    col = 0
                    for (rel, n, q0) in ((2, 256, 192), (3, 192, 320), (4, 64, 448)):
                        kt = 4 * Qb + rel
                        segB.append((0, kt, ("c", i0 + q0, n, 1), ("c", kt * 128, 1), col, n, ("c", q0, n, 1)))
                        col += n
                    banks_.append((0, segA)); banks_.append((1, segB))
                    for pp in range(2):
                        seg = []
                        col = 0
                        for ph in (2 * pp, 2 * pp + 1):
                            for (rel, n, q0) in ((-1, 64, 0), (0, 128, 0), (1, 64, 64)):
                                kt = Qb + rel
                                seg.append((1, ph * 4 + kt if 0 <= kt < 4 else -1, ("c", 4 * (128 * Qb + q0) + ph, n, 4),
                                            ("c", 4 * (128 * kt) + ph, 4), col, n, ("c", 4 * q0 + ph, n, 4)))
                                col += n
                        banks_.append((2, seg))
                    seg = []
                    for ph in range(16):
                        seg.append((2, ph, ("c", 16 * (32 * Qb) + ph, 32, 16), ("c", ph, 16), ph * 32, 32, ("c", ph, 32, 16)))
                    banks_.append((3 + Qb, seg))
                    first = [True]
                    P.wait("pe", nd_free[0])
                    fin = None
                    for (mi, seg) in banks_:
                        sb_ = sci[0] % 3
                        sci[0] += 1
                        valid = []
                        for (lay, vt, qs, ks, col, n, oc) in seg:
                            if lay == 0 and not (0 <= vt < 16):
                                continue
                            if lay == 1 and vt < 0:
                                continue
                            valid.append((lay, vt, qs, ks, col, n, oc))
                        P.wait("pe", bk_free[sb_])
                        lp = None
                        for vi, (lay, vt, qs, ks, col, n, oc) in enumerate(valid):
                            qa = qT[:, ci, qs[1]:qs[1] + (qs[2] - 1) * qs[3] + 1:qs[3]]
                            ka = kT[:, ci, ks[1]:ks[1] + 127 * ks[2] + 1:ks[2]]
                            lp = P.op("pe", lambda e, qa=qa, ka=ka, col=col, n=n, sb_=sb_: e.matmul(
                                bank(sb_)[:, col:col + n], lhsT=ka, rhs=qa, start=True, stop=True),
                                sig=(vi == len(valid) - 1))
                        P.wait("act", lp, pb_free[sb_])
                        te = P.op("act", lambda e, sb_=sb_: e.activation(out=Pb[:, sb_], in_=bank(sb_), func=AF.Exp, scale=scale))
                        bk_free[sb_] = te
                        P.wait("pool", te)
                        tm = P.op("pool", lambda e, sb_=sb_, mi=mi: e.tensor_tensor(out=Pb[:, sb_], in0=Pb[:, sb_], in1=maskb[:, mi], op=ALU.mult))
                        P.wait("pe", tm)
                        for (lay, vt, qs, ks, col, n, oc) in valid:
                            oa = slice(oc[1], oc[1] + (oc[2] - 1) * oc[3] + 1, oc[3])
                            st_ = first[0]
                            first[0] = False
                            P.op("pe", lambda e, lay=lay, vt=vt, col=col, n=n, oa=oa, st_=st_, sb_=sb_: e.matmul(
                                bank(4)[:, oa], lhsT=Vl[:, lay, vt, :], rhs=Pb[:, sb_, col:col + n],
                                start=st_, stop=False, skip_group_check=True), sig=False)
                            fin = P.op("pe", lambda e, col=col, n=n, oa=oa, st_=st_, sb_=sb_: e.matmul(
                                bank(5)[:, oa], lhsT=ones[:], rhs=Pb[:, sb_, col:col + n],
                                start=st_, stop=False, skip_group_check=True), sig=True)
                        pb_free[sb_] = fin
                    P.wait("dve", fin, yat_free[0] if Qb == 0 else None)
                    td = P.op("dve", lambda e: e.reciprocal(out=rden, in_=bank(5)))
                    P.wait("dve", td)
                    nd_free[0] = P.op("dve", lambda e, i0=i0: e.tensor_tensor(out=yat[:, i0:i0 + 512], in0=bank(4), in1=rden, op=ALU.mult))
                P.wait("sp", nd_free[0])
                yat_free[0] = P.dma("sp", lambda e, h=h: e.dma_start(
                    out=ycat[seq, (8 + h) * 128:(9 + h) * 128, :], in_=yat), ds_ya)
                ycat_tok[0] = yat_free[0]
        return [yat_free[0]] + last_conv

    for seq in range(nseq):
        P.barrier(extra=[tz] + ctoks)
        load_x(seq)
        P.barrier()
        if mode == "ffn":
            dump(xT[:, 0, 0, :], 0, 514, P.last["dve"], P.last["act"])
            dump(xT[:, 5, 1, :], 514, 514, P.last["dve"], P.last["act"])
            dump(cw[:, 0].rearrange("p k c -> p (k c)"), 4096, 264, P.last["dve"])
            ffn(seq, 0, x_in, y_out)
            P.barrier(extra=xres_free)
            continue
        if mode == "four":
            tk = fourier_m1(seq)
            P.barrier(extra=[tk])
            m2(seq, wb_of[0], ds_cast[("of", 0)], ln_mix_g, ln_mix_b, 1, x_in, y_out, tk)
            P.barrier(extra=xres_free)
            continue
        if mode == "even":
            tks = even_m1(seq, 0)
            P.barrier(extra=tks)
            m2(seq, wb_om[0], ds_cast[("om", 0)], ln_mix_g, ln_mix_b, 0, x_in, y_out, tks[0])
            P.barrier(extra=xres_free)
            continue
        for l in range(DEPTH):
            P.new_epoch()
            rsrc = x_in if l == 0 else resid
            if l % 2 == 0:
                tks = even_m1(seq, l // 2)
                P.barrier(extra=tks)
                m2(seq, wb_om[l // 2], ds_cast[("om", l // 2)], ln_mix_g, ln_mix_b, l, rsrc, resid, tks[0])
            else:
                tk = fourier_m1(seq)
                P.barrier(extra=[tk])
                m2(seq, wb_of[l // 2], ds_cast[("of", l // 2)], ln_mix_g, ln_mix_b, l, rsrc, resid, tk)
            P.barrier(extra=xres_free)
            ffn(seq, l, resid, y_out if l == DEPTH - 1 else resid)
            P.barrier(extra=xres_free)
    P.barrier(extra=xres_free)

    block = stack.enter_context(nc.Block())

    @block.tensor
    def _(e):
        P.emit("pe", e)

    @block.scalar
    def _(e):
        P.emit("act", e)

    @block.vector
    def _(e):
        P.emit("dve", e)

    @block.gpsimd
    def _(e):
        P.emit("pool", e)

    @block.sync
    def _(e):
        P.emit("sp", e)


def _bf(a):
    return np.asarray(a, np.float32).astype(ml_dtypes.bfloat16)


def make_consts():
    c = {}
    c["c_ident"] = _bf(np.eye(128))
    pm = np.zeros((128, 128), np.float32)
    for m in range(128):
        pm[(m + 64) % 128, m] = 1.0
    c["c_perm"] = _bf(pm)
    c["c_identf"] = np.eye(128, dtype=np.float32)
    half = 64
    inv = (1.0 / (10000.0 ** (np.arange(half, dtype=np.float32) / np.float32(half)))).astype(np.float32)
    ang = (np.arange(S, dtype=np.float32)[:, None] * inv[None, :]).astype(np.float32)
    cos = np.cos(ang.astype(np.float64)).T
    sin = np.sin(ang.astype(np.float64)).T
    c["c_cos"] = np.concatenate([cos, cos], 0).astype(np.float32)
    c["c_sin"] = np.concatenate([-sin, sin], 0).astype(np.float32)
    a = np.arange(128)[:, None]
    cc = np.arange(256)[None, :]
    master = (np.abs(cc - 64 - a) <= 64).astype(np.float32)
    masks = np.zeros((7, 128, 512), np.float32)
    masks[0] = np.concatenate([master[:, 192:256], master[:, 64:256], master[:, 0:256]], 1)
    masks[1] = np.concatenate([master[:, 0:256], master[:, 0:192], master[:, 0:64]], 1)
    one = np.concatenate([master[:, 192:256], master[:, 64:192], master[:, 0:64]], 1)
    masks[2] = np.concatenate([one, one], 1)
    for Qb in range(4):
        masks[3 + Qb] = np.tile(master[:, 64 + 32 * Qb:64 + 32 * Qb + 32], (1, 16))
    c["c_mask"] = _bf(masks)
    k = np.arange(512, dtype=np.float64)
    th = 2 * np.pi * np.outer(k, k) / 512.0
    c["c_cc"] = _bf(np.cos(th) / math.sqrt(512.0))
    c["c_sc"] = _bf(np.sin(th) / math.sqrt(512.0))
    k = np.arange(S, dtype=np.float64)
    th = 2 * np.pi * ((np.outer(k, k)) % S) / S
    c["c_cs"] = _bf(np.cos(th) / math.sqrt(S))
    c["c_ns"] = _bf(-np.sin(th) / math.sqrt(S))
    return c


_NC_CACHE = {}


def kernel(x_prompt, x_sample, w_in_mix, conv_short, w_out_mix, w_out_fourier, ln_mix_g, ln_mix_b,
           w_up, conv_ffn_w, w_down, ln_ffn_g, ln_ffn_b):
    f = lambda a: np.ascontiguousarray(np.asarray(a, dtype=np.float32))
    xp, xs = f(x_prompt), f(x_sample)
    consts = make_consts()
    shared = dict(w_in_mix=f(w_in_mix), conv_short=f(conv_short), w_out_mix=f(w_out_mix),
                  w_out_fourier=f(w_out_fourier), ln_mix_g=f(ln_mix_g), ln_mix_b=f(ln_mix_b),
                  w_up=f(w_up), conv_ffn_w=f(conv_ffn_w), w_down=f(w_down), ln_ffn_g=f(ln_ffn_g),
                  ln_ffn_b=f(ln_ffn_b))
    shared.update(consts)
    in_maps = []
    for c in range(8):
        xs_c = np.ascontiguousarray(np.stack([xp[2 * c], xp[2 * c + 1], xs[c]], 0))
        m = dict(shared)
        m["x"] = xs_c
        in_maps.append(m)
    nc = build(NSEQ, "full")
    res = run_bass_kernel_spmd(nc, in_maps, core_ids=list(range(8)))
    yp = np.empty_like(xp)
    ys = np.empty_like(xs)
    for c in range(8):
        y = res.results[c]["y"]
        yp[2 * c] = y[0]
        yp[2 * c + 1] = y[1]
        ys[c] = y[2]
    return (yp, ys)
```

```python
import math
from contextlib import ExitStack
import numpy as np
import ml_dtypes
import concourse.bass as bass
import concourse.mybir as mybir
from concourse.bass_utils import run_bass_kernel_spmd

F32 = mybir.dt.float32
BF16 = mybir.dt.bfloat16
AF = mybir.ActivationFunctionType
ALU = mybir.AluOpType

D = 2048
S = 2048
DEPTH = 4
DFF = 5632
NCH_FF = DFF // 128
DIN = 6144
ALPHA = (2 * DEPTH) ** 0.25
LN_EPS = 1e-5
NSEQ = 3
ENG = ("pe", "act", "dve", "pool", "sp")


class DSem:
    def __init__(self, h):
        self.h = h
        self.count = 0


class Rec:
    def __init__(self):
        self.call = None

    def __getattr__(self, name):
        def f(*a, **k):
            self.call = (name, a, k)
            return self
        return f


def freeze(fn):
    r = Rec()
    fn(r)
    name, a, k = r.call
    return lambda e: getattr(e, name)(*a, **k)


class Plan:
    def __init__(self, nc, stack):
        self.nc = nc
        self.stack = stack
        self.q = {e: [] for e in ENG}
        self.cnt = {e: 0 for e in ENG}
        self.sem = {}
        self.waited = {}
        self.nsem = 0
        self.last = {e: None for e in ENG}
        self.new_epoch()

    def mksem(self, name):
        self.nsem += 1
        return self.stack.enter_context(self.nc.semaphore(f"{name}_{self.nsem}"))

    def dsem(self, name):
        return DSem(self.mksem(name))

    def new_epoch(self):
        for e in ("pe", "act", "dve", "pool"):
            self.sem[e] = self.mksem("e" + e)
            self.cnt[e] = 0

    def op(self, eng, fn, sig=True):
        fn = freeze(fn)
        if sig:
            self.cnt[eng] += 1
            tok = (self.sem[eng], self.cnt[eng])
            self.q[eng].append(("op", fn, self.sem[eng]))
            self.last[eng] = tok
            return tok
        self.q[eng].append(("op", fn, None))
        return None

    def wait(self, eng, *toks):
        for tok in toks:
            if tok is None:
                continue
            if isinstance(tok, (list,)):
                self.wait(eng, *tok)
                continue
            sem, val = tok
            key = (eng, id(sem))
            if self.waited.get(key, 0) >= val:
                continue
            self.waited[key] = val
            self.q[eng].append(("wait", sem, val))

    def dma(self, eng, fn, ds):
        fn = freeze(fn)
        ds.count += 16
        self.q[eng].append(("dma", fn, ds.h))
        return (ds.h, ds.count)

    def barrier(self, extra=()):
        toks = [self.last[e] for e in ("pe", "act", "dve", "pool")] + list(extra)
        for e in ENG:
            self.wait(e, *toks)

    def emit(self, eng, e):
        for kind, a, b in self.q[eng]:
            if kind == "op":
                inst = a(e)
                if b is not None:
                    inst.then_inc(b, 1)
            elif kind == "wait":
                e.wait_ge(a, b)
            else:
                a(e).then_inc(b, 16)


def build(nseq=NSEQ, mode="full"):
    nc = bass.Bass("TRN2", target_bir_lowering=False)
    stack = ExitStack()
    with stack:
        _build(nc, stack, nseq, mode)
    return nc


def _build(nc, stack, nseq, mode):
    def din(name, shape, dt=F32):
        return nc.dram_tensor(name, list(shape), dt, kind="ExternalInput").ap()

    def dscr(name, shape, dt):
        return nc.dram_tensor(name, list(shape), dt, kind="Internal").ap()

    x_in = din("x", [nseq, S, D])
    y_out = nc.dram_tensor("y", [nseq, S, D], F32, kind="ExternalOutput").ap()
    w_in = din("w_in_mix", [2, D, DIN])
    conv_short = din("conv_short", [2, 3, 1024])
    w_om = din("w_out_mix", [2, D, D])
    w_of = din("w_out_fourier", [2, D, D])
    ln_mix_g = din("ln_mix_g", [4, D]); ln_mix_b = din("ln_mix_b", [4, D])
    w_up = din("w_up", [4, D, 2 * DFF])
    conv_ffn = din("conv_ffn_w", [4, 3, 2 * DFF])
    w_dn = din("w_down", [4, DFF, D])
    ln_ffn_g = din("ln_ffn_g", [4, D]); ln_ffn_b = din("ln_ffn_b", [4, D])
    c_ident = din("c_ident", [128, 128], BF16)
    c_perm = din("c_perm", [128, 128], BF16)
    c_identf = din("c_identf", [128, 128], F32)
    c_cos = din("c_cos", [128, S], F32)
    c_sin = din("c_sin", [128, S], F32)
    c_mask = din("c_mask", [7, 128, 512], BF16)
    c_cc = din("c_cc", [512, 512], BF16)
    c_sc = din("c_sc", [512, 512], BF16)
    c_cs = din("c_cs", [S, S], BF16)
    c_ns = din("c_ns", [S, S], BF16)

    dbg = nc.dram_tensor("dbg", [128, 8192], F32, kind="ExternalOutput").ap() if mode != "full" else None
    ds_dbg = None
    dbg16 = nc.dram_tensor("dbg16", [128, 8192], BF16, kind="ExternalOutput").ap() if mode != "full" else None
    wb_in = dscr("wb_in", [2, D, DIN], BF16)
    wb_om = dscr("wb_om", [2, D, D], BF16)
    wb_of = dscr("wb_of", [2, D, D], BF16)
    wb_up = dscr("wb_up", [4, D, 2 * DFF], BF16)
    wb_dn = dscr("wb_dn", [4, DFF, D], BF16)
    resid = dscr("resid", [nseq, S, D], F32)
    ycat = dscr("ycat", [nseq, D, S], BF16)

    sb = lambda name, shape, dt: stack.enter_context(nc.sbuf_tensor(name, list(shape), dt))
    xT = sb("xT", [128, 16, 4, 514], BF16)
    ident = sb("ident", [128, 128], BF16)
    perm = sb("perm", [128, 128], BF16)
    ones = sb("ones", [128, 128], BF16)
    identf = sb("identf", [128, 128], F32)
    cw = sb("cw", [128, 4, 3, 88], F32)
    cs = sb("cs", [128, 2, 3, 8], F32)
    small = sb("small", [128, 64], F32)
    NW = 34816
    WK = sb("WK", [128, NW], F32)
    ps = [stack.enter_context(nc.psum_tensor(f"ps{i}", [128, 1024], F32)) for i in range(4)]

    def cv(off, shape, dt):
        n = int(np.prod(shape))
        if dt == F32:
            assert off % 4 == 0
            ap = WK[:, off // 4: off // 4 + n]
        else:
            ap = WK[:, off // 4: off // 4 + (n + 1) // 2].bitcast(BF16)
        if len(shape) == 1:
            return ap
        if len(shape) == 2:
            return ap.rearrange("p (a b) -> p a b", a=shape[0])
        if len(shape) == 3:
            return ap.rearrange("p (a b c) -> p a b c", a=shape[0], b=shape[1])
        raise ValueError

    def bank(i):
        return ps[i // 2][:, (i % 2) * 512:(i % 2) * 512 + 512]

    def bank_bf(i):
        return ps[i // 2][:, (i % 2) * 512:(i % 2) * 512 + 512].bitcast(BF16)

    P = Plan(nc, stack)

    acc = cv(0, [4, 2048], F32)
    xres = cv(32768, [2, 2048], F32)
    ybf = cv(49152, [2048], BF16)
    gam = cv(53248, [2048], F32)
    bet = cv(61440, [2048], F32)
    wdn = cv(69632, [3, 2, 2048], BF16)
    wup = cv(94208, [3, 16, 256], BF16)
    aT = cv(118784, [4, 2, 512], BF16)
    ctmp = cv(126976, [2, 512], F32)
    sgb = cv(131072, [4, 512], F32)
    kopb = cv(94208, [2, 16, 512], BF16)
    assert 139264 <= NW * 4
    st6 = small[:, 0:48].rearrange("p (b c s) -> p b c s", b=2, c=4)
    mv = small[:, 48:52].rearrange("p (b s) -> p b s", b=2)
    rstd = small[:, 52:54]
    nmr = small[:, 54:56]
    sdv = small[:, 56:58]

    ds_const = P.dsem("const")
    ds_cast = {}
    ds_wdn = [P.dsem("wdn") for _ in range(3)]
    ds_wup = [P.dsem("wup") for _ in range(3)]
    ds_xres = [P.dsem("xres") for _ in range(2)]
    ds_yst = [P.dsem("yst") for _ in range(2)]
    ds_gb = P.dsem("gb")
    ds_kop = [P.dsem("kop") for _ in range(2)]
    ds_misc = P.dsem("misc")
    ds_dbg = P.dsem("dbg")
    ds_blk = [P.dsem("blk") for _ in range(4)]
    ds_f = P.dsem("fst")
    ds_win = [P.dsem("win") for _ in range(3)]
    ds_y = [P.dsem("ycs") for _ in range(2)]
    ds_ya = P.dsem("yat")

    def dump(ap, col0, n, *toks):
        if dbg is None:
            return
        P.wait("sp", *toks)
        tgt = dbg16 if ap.dtype == BF16 else dbg
        t = P.dma("sp", lambda e: e.dma_start(out=tgt[:, col0:col0 + n], in_=ap), ds_dbg)
        for e_ in ENG:
            P.wait(e_, t)

    ctoks = []
    ctoks.append(P.dma("sp", lambda e: e.dma_start(out=ident[:], in_=c_ident[:, :]), ds_const))
    ctoks.append(P.dma("sp", lambda e: e.dma_start(out=perm[:], in_=c_perm[:, :]), ds_const))
    ctoks.append(P.dma("sp", lambda e: e.dma_start(out=identf[:], in_=c_identf[:, :]), ds_const))
    cwraw = cv(0, [4, 3, 128], F32)
    csraw = cv(8192, [2, 3, 128], F32)
    for l in range(4):
        ctoks.append(P.dma("sp", lambda e, l=l: e.dma_start(
            out=cwraw[0:88, l], in_=conv_ffn[l].rearrange("k (c p) -> c k p", p=128)), ds_const))
    for i in range(2):
        ctoks.append(P.dma("sp", lambda e, i=i: e.dma_start(
            out=csraw[0:8, i], in_=conv_short[i].rearrange("k (c p) -> c k p", p=128)), ds_const))
    P.wait("pe", ctoks[-1]); P.wait("pool", ctoks[-1]); P.wait("dve", ctoks[-1]); P.wait("act", ctoks[-1])
    P.op("pool", lambda e: e.memset(ones[:], 1.0))
    P.op("pool", lambda e: e.memset(xT[:, :, 0, 0:1], 0.0))
    tz = P.op("pool", lambda e: e.memset(xT[:, :, 3, 513:514], 0.0))
    tcw = None
    for l in range(4):
        for k in range(3):
            idx = l * 3 + k
            bk, col = idx // 5, (idx % 5) * 88
            tcw = P.op("pe", lambda e, l=l, k=k, bk=bk, col=col: e.transpose(
                out=bank(bk)[:, col:col + 88], in_=cwraw[0:88, l, k, :], identity=identf[0:88, 0:88]),
                sig=(idx == 11))
    P.wait("dve", tcw)
    for l in range(4):
        for k in range(3):
            idx = l * 3 + k
            bk, col = idx // 5, (idx % 5) * 88
            P.op("dve", lambda e, l=l, k=k, bk=bk, col=col: e.tensor_copy(out=cw[:, l, k, :], in_=bank(bk)[:, col:col + 88]))
    tcs = None
    for i in range(2):
        for k in range(3):
            idx = i * 3 + k
            tcs = P.op("pe", lambda e, i=i, k=k, idx=idx: e.transpose(
                out=bank(3)[:, idx * 8:idx * 8 + 8], in_=csraw[0:8, i, k, :], identity=identf[0:8, 0:8]),
                sig=(idx == 5))
    P.wait("dve", tcs)
    for i in range(2):
        for k in range(3):
            idx = i * 3 + k
            P.op("dve", lambda e, i=i, k=k, idx=idx: e.tensor_copy(out=cs[:, i, k, :], in_=bank(3)[:, idx * 8:idx * 8 + 8]))

    def cast(name, src, dst, l):
        sflat = src[l].rearrange("k (j c) -> (k j) c", c=1024)
        dflat = dst[l].rearrange("k (j c) -> (k j) c", c=1024)
        rows = sflat.shape[0]
        dsx = P.dsem("cast" + name)
        tok = None
        r = 0
        while r < rows:
            n = min(8192, rows - r)
            tok = P.dma("pool", lambda e, r=r, n=n: e.dma_start(out=dflat[r:r + n, :], in_=sflat[r:r + n, :]), dsx)
            r += n
        ds_cast[(name, l)] = tok

    for l in range(4):
        if l % 2 == 0:
            cast("in", w_in, wb_in, l // 2); cast("om", w_om, wb_om, l // 2)
        else:
            cast("of", w_of, wb_of, l // 2)
        cast("up", w_up, wb_up, l); cast("dn", w_dn, wb_dn, l)

    st = dict(wdn_i=0, wup_i=0, bankC=0, xres_i=0, chunk_i=0)
    wdn_free = [None] * 3
    wup_free = [None] * 3
    bankC_free = [None, None]
    bankD_free = [None]
    acc_free = [None] * 4
    xres_free = [None, None]
    ybf_free = [None]
    xT_ready = []
    pair_free = [None, None]
    aT_free = [None] * 4
    sg_free = [None] * 4
    ct_free = [None, None]
    gb_tok = [None]

    def xcol(t):
        return t // 512, t % 512 + 1

    def writeback(ybf_tok, t0, ffn_inplace):
        slot, c0 = xcol(t0)
        P.wait("pe", ybf_tok, bankD_free[0])
        tp = None
        for d in range(16):
            tp = P.op("pe", lambda e, d=d: e.transpose(
                out=bank_bf(6 + d // 8)[:, (d % 8) * 128:(d % 8) * 128 + 128],
                in_=ybf[:, d * 128:(d + 1) * 128], identity=ident[:]), sig=(d == 15))
        ybf_free[0] = tp
        toks = []
        for half, eng in ((0, "dve"), (1, "act")):
            src = bank_bf(6 + half).rearrange("p (d c) -> p d c", d=8)
            P.wait(eng, tp)
            if eng == "dve":
                cp = lambda e, o, i: e.tensor_copy(out=o, in_=i)
            else:
                cp = lambda e, o, i: e.activation(out=o, in_=i, func=AF.Copy)
            dsl = slice(half * 8, half * 8 + 8)
            tk = P.op(eng, lambda e, cp=cp, dsl=dsl, src=src: cp(e, xT[:, dsl, slot, c0:c0 + 128], src))
            if t0 % 512 == 0 and slot > 0:
                tk = P.op(eng, lambda e, cp=cp, dsl=dsl, src=src: cp(e, xT[:, dsl, slot - 1, 513:514], src[:, :, 0:1]))
            if t0 % 512 == 384 and slot < 3 and not ffn_inplace:
                tk = P.op(eng, lambda e, cp=cp, dsl=dsl, src=src: cp(e, xT[:, dsl, slot + 1, 0:1], src[:, :, 127:128]))
            toks.append(tk)
        bankD_free[0] = toks
        xT_ready[:] = toks
        return toks

    def load_gb(g_ap, b_ap, l):
        P.wait("sp", P.last["pool"], P.last["act"], P.last["dve"])
        P.dma("sp", lambda e: e.dma_start(out=gam, in_=g_ap[l:l + 1, :].partition_broadcast(128)), ds_gb)
        gb_tok[0] = P.dma("sp", lambda e: e.dma_start(out=bet, in_=b_ap[l:l + 1, :].partition_broadcast(128)), ds_gb)

    def d_prefetch(wsrc, cast_tok):
        slot = st["wdn_i"] % 3
        st["wdn_i"] += 1
        P.wait("sp", wdn_free[slot], cast_tok)
        ld = P.dma("sp", lambda e: e.dma_start(out=wdn[:, slot], in_=wsrc), ds_wdn[slot])
        return slot, ld

    def d_part(q, k, kops, slot, ld, kop_toks):
        P.wait("pe", ld, *kop_toks)
        last_pe = None
        s = k
        for n in range(4):
            bi = st["bankC"] % 2
            st["bankC"] += 1
            P.wait("pe", bankC_free[bi])
            for g in range(2):
                last_pe = P.op("pe", lambda e, g=g: e.matmul(
                    bank(4 + bi), lhsT=kops[g][:, s * 128:(s + 1) * 128],
                    rhs=wdn[:, slot, g, n * 512:(n + 1) * 512], start=(g == 0), stop=(g == 1)), sig=(g == 1))
            P.wait("dve", last_pe)
            dst = acc[:, s, n * 512:(n + 1) * 512]
            if q == 0:
                P.wait("dve", acc_free[s])
                bankC_free[bi] = P.op("dve", lambda e: e.tensor_copy(out=dst, in_=bank(4 + bi)))
            else:
                bankC_free[bi] = P.op("dve", lambda e: e.tensor_tensor(out=dst, in0=dst, in1=bank(4 + bi), op=ALU.add))
        if k == 3:
            wdn_free[slot] = last_pe
        return last_pe

    fin = {}

    def fin_chain(seq, tt, s, rsrc, rdst):
        t0 = tt * 512 + s * 128
        b = st["xres_i"] % 2
        st["xres_i"] += 1
        P.wait("sp", xres_free[b])
        ldx = P.dma("sp", lambda e: e.dma_start(out=xres[:, b], in_=rsrc[seq, t0:t0 + 128, :]), ds_xres[b])
        P.wait("dve", ldx, P.last["dve"], P.last["act"])
        P.op("dve", lambda e: e.scalar_tensor_tensor(
            out=acc[:, s], in0=xres[:, b], scalar=float(ALPHA), in1=acc[:, s], op0=ALU.mult, op1=ALU.add))
        P.wait("dve", P.last["dve"])
        for c in range(4):
            P.op("dve", lambda e, c=c: e.bn_stats(out=st6[:, b, c], in_=acc[:, s, c * 512:(c + 1) * 512]))
        P.wait("dve", P.last["dve"])
        P.op("dve", lambda e: e.bn_aggr(out=mv[:, b], in_=st6[:, b].rearrange("p c s -> p (c s)")))
        P.wait("act", P.last["dve"])
        tsd = P.op("act", lambda e: e.activation(
            out=sdv[:, b:b + 1], in_=mv[:, b, 1:2], func=AF.Sqrt, bias=float(LN_EPS), scale=1.0))
        P.wait("dve", tsd)
        P.op("dve", lambda e: e.reciprocal(out=rstd[:, b:b + 1], in_=sdv[:, b:b + 1]))
        P.wait("dve", P.last["dve"])
        tstat = P.op("dve", lambda e: e.scalar_tensor_tensor(
            out=nmr[:, b:b + 1], in0=mv[:, b, 0:1], scalar=-1.0, in1=rstd[:, b:b + 1], op0=ALU.mult, op1=ALU.mult))
        P.wait("act", tstat)
        txh = P.op("act", lambda e: e.activation(
            out=xres[:, b], in_=acc[:, s], func=AF.Identity, scale=rstd[:, b:b + 1], bias=nmr[:, b:b + 1]))
        acc_free[s] = txh
        P.wait("pool", txh, gb_tok[0])
        P.op("pool", lambda e: e.tensor_tensor(out=xres[:, b], in0=xres[:, b], in1=gam, op=ALU.mult))
        P.wait("pool", P.last["pool"])
        ty = P.op("pool", lambda e: e.tensor_tensor(out=xres[:, b], in0=xres[:, b], in1=bet, op=ALU.add))
        fin[(tt, s)] = (b, ty, t0, seq, rdst)

    def fin_cast(tt, s):
        b, ty, t0, seq, rdst = fin[(tt, s)]
        P.wait("act", ty, ybf_free[0])
        tyb = P.op("act", lambda e: e.activation(out=ybf, in_=xres[:, b], func=AF.Copy))
        P.wait("sp", ty, tyb)
        xres_free[b] = P.dma("sp", lambda e: e.dma_start(out=rdst[seq, t0:t0 + 128, :], in_=xres[:, b]), ds_yst[b])
        fin[(tt, s)] = (tyb, t0)

    def fin_wb(tt, s, ffn_inplace):
        tyb, t0 = fin.pop((tt, s))
        writeback(tyb, t0, ffn_inplace)

    def g2_finish(seq, tt, rsrc, rdst, ffn_inplace):
        fin_chain(seq, tt, 0, rsrc, rdst)
        for s in range(4):
            if s + 1 < 4:
                fin_chain(seq, tt, s + 1, rsrc, rdst)
            fin_cast(tt, s)
            fin_wb(tt, s, ffn_inplace)

    def load_x(seq):
        for s in range(16):
            t0 = s * 128
            b = st["xres_i"] % 2
            st["xres_i"] += 1
            P.wait("sp", xres_free[b])
            ldx = P.dma("sp", lambda e, b=b, t0=t0: e.dma_start(out=xres[:, b], in_=x_in[seq, t0:t0 + 128, :]), ds_xres[b])
            P.wait("act", ldx, ybf_free[0])
            tyb = P.op("act", lambda e, b=b: e.activation(out=ybf, in_=xres[:, b], func=AF.Copy))
            xres_free[b] = tyb
            writeback(tyb, t0, False)

    def ffn(seq, l, rsrc, rdst):
        load_gb(ln_ffn_g, ln_ffn_b, l)
        wu = wb_up[l].rearrange("(kc p) n -> p kc n", p=128)
        wd = wb_dn[l].rearrange("(j p) n -> p j n", p=128)
        LAG = 3
        items = [(tt, b, half) for tt in range(4) for b in range(22) for half in range(2)]
        wl = {}

        def up_prefetch(i):
            if i >= len(items) or i in wl:
                return
            tt_, b_, half_ = items[i]
            col0 = half_ * DFF + b_ * 256
            slot = st["wup_i"] % 3
            st["wup_i"] += 1
            P.wait("sp", wup_free[slot], ds_cast[("up", l)])
            wl[i] = (slot, P.dma("sp", lambda e: e.dma_start(out=wup[:, slot], in_=wu[:, :, col0:col0 + 256]), ds_wup[slot]))

        up_prefetch(0)
        up_prefetch(1)
        it = 0
        for tt in range(4):
            u_done = None
            grp_toks = {}
            dgrp = {}

            def dpre(q):
                if 0 <= q < 22 and q not in dgrp:
                    dgrp[q] = d_prefetch(wd[:, 2 * q:2 * q + 2, :], ds_cast[("dn", l)])

            def dpart(q, k):
                gi = q % 4
                slot, ld = dgrp[q]
                tk = d_part(q, k, [aT[:, gi, 0], aT[:, gi, 1]], slot, ld,
                            [grp_toks[(q, 0, "a")], grp_toks[(q, 1, "a")]])
                if k == 3:
                    aT_free[gi] = tk

            for b in range(22):
                dpre(b - LAG)
                dpre(b - LAG + 1)
                for half in range(2):
                    slot, ld = wl.pop(it)
                    up_prefetch(it + 2)
                    it += 1
                    if seq == 0 and tt == 0 and b == 0 and half == 0:
                        dump(wup[:, slot].rearrange("p k c -> p (k c)"), 2048, 4096, ld)
                    P.wait("pe", ld)
                    for ci in range(2):
                        ch = half * NCH_FF + b * 2 + ci
                        pi = st["chunk_i"] % 2
                        st["chunk_i"] += 1
                        P.wait("pe", pair_free[pi])
                        lp = None
                        for kc in range(16):
                            for hh in range(2):
                                lp = P.op("pe", lambda e: e.matmul(
                                    bank(2 * pi + hh)[:, 0:258], lhsT=wup[:, slot, kc, ci * 128:(ci + 1) * 128],
                                    rhs=xT[:, kc, tt, hh * 256:hh * 256 + 258], start=(kc == 0), stop=(kc == 15)),
                                    sig=(kc == 15 and hh == 1))
                        u_done = lp
                        if b >= LAG:
                            dpart(b - LAG, half * 2 + ci)
                        pv = ps[pi].rearrange("p (h c) -> p h c", h=2)
                        cti = pi
                        ct = ctmp[:, cti].rearrange("p (h c) -> p h c", h=2)
                        P.wait("act", lp, ct_free[cti])
                        ta = P.op("act", lambda e: e.activation(
                            out=ct, in_=pv[:, :, 1:257], func=AF.Identity, scale=cw[:, l, 1, ch:ch + 1]))
                        P.wait("dve", ta)
                        tb = P.op("dve", lambda e: e.scalar_tensor_tensor(
                            out=ct, in0=pv[:, :, 0:256], scalar=cw[:, l, 0, ch:ch + 1], in1=ct, op0=ALU.mult, op1=ALU.add))
                        P.wait("dve", tb)
                        tc = P.op("dve", lambda e: e.scalar_tensor_tensor(
                            out=ct, in0=pv[:, :, 2:258], scalar=cw[:, l, 2, ch:ch + 1], in1=ct, op0=ALU.mult, op1=ALU.add))
                        pair_free[pi] = tc
                        if seq == 0 and tt == 0 and b == 0 and ci == 0:
                            dump(ctmp[:, cti], 0 if half == 0 else 1024, 512, tc)
                        sgi = (b % 2) * 2 + ci
                        if half == 0:
                            P.wait("act", tc, sg_free[sgi])
                            ct_free[cti] = P.op("act", lambda e: e.activation(
                                out=sgb[:, sgi], in_=ctmp[:, cti], func=AF.Silu))
                            grp_toks[(b, ci, "sg")] = ct_free[cti]
                        else:
                            gi = b % 4
                            P.wait("pool", tc, grp_toks[(b, ci, "sg")], aT_free[gi])
                            tm = P.op("pool", lambda e: e.tensor_tensor(
                                out=aT[:, gi, ci], in0=sgb[:, sgi], in1=ctmp[:, cti], op=ALU.mult))
                            ct_free[cti] = tm
                            sg_free[sgi] = tm
                            grp_toks[(b, ci, "a")] = tm
                    wup_free[slot] = u_done
                if tt >= 1:
                    pt = tt - 1
                    if b == 0:
                        fin_chain(seq, pt, 0, rsrc, rdst); fin_cast(pt, 0); fin_chain(seq, pt, 1, rsrc, rdst)
                    elif b == 1:
                        fin_wb(pt, 0, True); fin_cast(pt, 1); fin_chain(seq, pt, 2, rsrc, rdst)
                    elif b == 2:
                        fin_wb(pt, 1, True); fin_cast(pt, 2); fin_chain(seq, pt, 3, rsrc, rdst)
                    elif b == 3:
                        fin_wb(pt, 2, True); fin_cast(pt, 3)
                    elif b == 4:
                        fin_wb(pt, 3, True)
            for q in range(22 - LAG, 22):
                dpre(q)
                for k in range(4):
                    dpart(q, k)
            if tt >= 1:
                P.wait("pool", u_done, *xT_ready)
                P.op("pool", lambda e: e.tensor_copy(out=xT[:, :, tt, 0:1], in_=xT[:, :, tt - 1, 512:513]))
            if seq == 0 and tt == 0:
                dump(acc[:, 0], 2048, 2048, P.last["dve"])
            if tt == 3:
                g2_finish(seq, tt, rsrc, rdst, True)

    def m2(seq, wsrc_l, cast_tok, g_ap, b_ap, l, rsrc, rdst, ycat_tok):
        load_gb(g_ap, b_ap, l)
        wv = wsrc_l.rearrange("(kc p) n -> p kc n", p=128)
        yc = ycat[seq].rearrange("(kc p) t -> p kc t", p=128)
        wblk = cv(69632, [2, 16, 512], BF16)
        kop2 = cv(102400, [2, 16, 512], BF16)
        kop_free = [None, None]
        wb_free = [None, None]
        kl = {}
        wq = {}
        wi_ = [0]

        def kpre(tt):
            if tt < 4 and tt not in kl:
                kb = tt % 2
                P.wait("sp", kop_free[kb], ycat_tok)
                kl[tt] = P.dma("sp", lambda e: e.dma_start(out=kop2[:, kb], in_=yc[:, :, tt * 512:(tt + 1) * 512]), ds_kop[kb])

        def wpre(i):
            if i < 16 and i not in wq:
                n = i % 4
                slot = wi_[0] % 2
                wi_[0] += 1
                P.wait("sp", wb_free[slot], cast_tok)
                wq[i] = (slot, P.dma("sp", lambda e: e.dma_start(out=wblk[:, slot], in_=wv[:, :, n * 512:(n + 1) * 512]), ds_wdn[slot]))

        kpre(0)
        wpre(0)
        ev = 0
        for tt in range(4):
            kb = tt % 2
            kpre(tt + 1)
            lp = None
            for n in range(4):
                slot, ld = wq[tt * 4 + n]
                wpre(tt * 4 + n + 1)
                P.wait("pe", ld, kl[tt])
                for s in range(4):
                    bi = st["bankC"] % 2
                    st["bankC"] += 1
                    P.wait("pe", bankC_free[bi])
                    for kc in range(16):
                        lp = P.op("pe", lambda e: e.matmul(
                            bank(4 + bi), lhsT=kop2[:, kb, kc, s * 128:(s + 1) * 128], rhs=wblk[:, slot, kc, :],
                            start=(kc == 0), stop=(kc == 15)), sig=(kc == 15))
                    eng = "dve" if ev % 2 == 0 else "act"
                    ev += 1
                    dst = acc[:, s, n * 512:(n + 1) * 512]
                    P.wait(eng, lp, acc_free[s])
                    if eng == "dve":
                        bankC_free[bi] = P.op("dve", lambda e: e.tensor_copy(out=dst, in_=bank(4 + bi)))
                    else:
                        bankC_free[bi] = P.op("act", lambda e: e.activation(out=dst, in_=bank(4 + bi), func=AF.Copy))
                wb_free[slot] = lp
            kop_free[kb] = lp
            g2_finish(seq, tt, rsrc, rdst, False)

    def fourier_m1(seq):
        Yc = cv(0, [16, 512], BF16)
        Ys = cv(16384, [16, 512], BF16)
        CC = cv(32768, [4, 512], BF16)
        SC = cv(36864, [4, 512], BF16)
        blk = cv(40960, [4, 16, 256], BF16)
        fst = cv(73728, [4, 2048], BF16)
        blk_free = [None] * 4
        P.wait("sp", P.last["pe"])
        P.dma("sp", lambda e: e.dma_start(out=CC, in_=c_cc.rearrange("(k p) n -> p k n", p=128)), ds_misc)
        tcc = P.dma("sp", lambda e: e.dma_start(out=SC, in_=c_sc.rearrange("(k p) n -> p k n", p=128)), ds_misc)
        csv = c_cs.rearrange("(k p) n -> p k n", p=128)
        nsv = c_ns.rearrange("(k p) n -> p k n", p=128)
        bC = [None, None]
        y_free = [None]
        fst_free = [None]
        cnt = 0
        for g in range(4):
            P.wait("pe", tcc)
            ev = []
            for stl in range(16):
                slot, c0 = xcol(stl * 128)
                for which, (M, Yb, eng) in enumerate(((CC, Yc, "act"), (SC, Ys, "dve"))):
                    bi = which
                    P.wait("pe", bC[bi], y_free[0] if stl == 0 else None)
                    lp = None
                    for kc in range(4):
                        lp = P.op("pe", lambda e, kc=kc, slot=slot, c0=c0, M=M, bi=bi: e.matmul(
                            bank(4 + bi), lhsT=xT[:, 4 * g + kc, slot, c0:c0 + 128], rhs=M[:, kc, :],
                            start=(kc == 0), stop=(kc == 3)), sig=(kc == 3))
                    P.wait(eng, lp)
                    if eng == "act":
                        bC[bi] = P.op("act", lambda e, Yb=Yb, stl=stl, bi=bi: e.activation(out=Yb[:, stl], in_=bank(4 + bi), func=AF.Copy))
                    else:
                        bC[bi] = P.op("dve", lambda e, Yb=Yb, stl=stl, bi=bi: e.tensor_copy(out=Yb[:, stl], in_=bank(4 + bi)))
                    ev.append(bC[bi])
            P.wait("pe", ev[-1], ev[-2])
            P.wait("act", fst_free[0])
            lastpe = None
            for sp_ in range(8):
                bs = sp_ % 2
                P.wait("sp", blk_free[bs], blk_free[2 + bs])
                l1 = P.dma("sp", lambda e, bs=bs, sp_=sp_: e.dma_start(out=blk[:, bs], in_=csv[:, :, sp_ * 256:(sp_ + 1) * 256]), ds_blk[bs])
                l2 = P.dma("sp", lambda e, bs=bs, sp_=sp_: e.dma_start(out=blk[:, 2 + bs], in_=nsv[:, :, sp_ * 256:(sp_ + 1) * 256]), ds_blk[2 + bs])
                P.wait("pe", l1, l2)
                for cp_ in range(4):
                    bi = cnt % 2
                    cnt += 1
                    P.wait("pe", pair_free[bi])
                    lp = None
                    for which, Yb in enumerate((Yc, Ys)):
                        for sc in range(16):
                            lp = P.op("pe", lambda e, Yb=Yb, sc=sc, cp_=cp_, bs=bs, which=which, bi=bi: e.matmul(
                                bank(bi)[:, 0:256], lhsT=Yb[:, sc, cp_ * 128:(cp_ + 1) * 128], rhs=blk[:, 2 * which + bs, sc, :],
                                start=(which == 0 and sc == 0), stop=(which == 1 and sc == 15)),
                                sig=(which == 1 and sc == 15))
                    lastpe = lp
                    P.wait("act", lp)
                    pair_free[bi] = P.op("act", lambda e, cp_=cp_, sp_=sp_, bi=bi: e.activation(
                        out=fst[:, cp_, sp_ * 256:(sp_ + 1) * 256], in_=bank(bi)[:, 0:256], func=AF.Copy))
                blk_free[bs] = lastpe
                blk_free[2 + bs] = lastpe
            y_free[0] = lastpe
            P.wait("sp", P.last["act"])
            for cp_ in range(4):
                ch = 4 * g + cp_
                fst_free[0] = P.dma("sp", lambda e, cp_=cp_, ch=ch: e.dma_start(
                    out=ycat[seq, ch * 128:(ch + 1) * 128, :], in_=fst[:, cp_]), ds_f)
        return fst_free[0]

    def even_m1(seq, li):
        wi = wb_in[li].rearrange("(kc p) n -> p kc n", p=128)
        winb = cv(0, [3, 16, 256], BF16)
        cosb = cv(24576, [2048], F32)
        sinb = cv(32768, [2048], F32)
        maskb = cv(40960, [7, 512], BF16)
        stg = cv(49152, [6, 2050], F32)
        yst = cv(98352 + 16, [2, 2048], BF16)
        qb = cv(49152, [2, 2048], BF16)
        r1 = cv(57344, [512], F32)
        r2 = cv(59392, [512], F32)
        qT = cv(61440, [2, 2048], BF16)
        kT = cv(69632, [2, 2048], BF16)
        vT = cv(77824, [2, 2048], BF16)
        Vl = cv(86016, [3, 16, 128], BF16)
        Pb = cv(106560, [3, 512], BF16)
        rden = cv(109632, [512], F32)
        yat = cv(111680, [2048], BF16)
        win_free = [None] * 3
        wi_i = [0]
        P.wait("sp", P.last["pe"], P.last["act"], P.last["dve"], P.last["pool"])
        P.dma("sp", lambda e: e.dma_start(out=cosb, in_=c_cos[:, :]), ds_misc)
        P.dma("sp", lambda e: e.dma_start(out=sinb, in_=c_sin[:, :]), ds_misc)
        tconst = P.dma("sp", lambda e: e.dma_start(out=maskb, in_=c_mask.rearrange("m p c -> p m c")), ds_misc)
        for e_ in ("pe", "act", "dve", "pool"):
            P.wait(e_, tconst)
        for i in (4, 5):
            P.op("pool", lambda e, i=i: e.memset(stg[:, i, 0:1], 0.0))
            P.op("pool", lambda e, i=i: e.memset(stg[:, i, 2049:2050], 0.0))
        tpad = P.last["pool"]
        bk_free = [None] * 4
        bki = [0]
        ycat_tok = [None]

        def inproj_block(col0, epi):
            slot = wi_i[0] % 3
            wi_i[0] += 1
            P.wait("sp", win_free[slot], ds_cast[("in", li)])
            ld = P.dma("sp", lambda e: e.dma_start(out=winb[:, slot], in_=wi[:, :, col0:col0 + 256]), ds_win[slot])
            P.wait("pe", ld)
            lp = None
            for ci in range(2):
                for tt in range(4):
                    bi = bki[0] % 4
                    bki[0] += 1
                    P.wait("pe", bk_free[bi])
                    for kc in range(16):
                        lp = P.op("pe", lambda e, kc=kc, ci=ci, tt=tt, bi=bi: e.matmul(
                            bank(bi), lhsT=winb[:, slot, kc, ci * 128:(ci + 1) * 128], rhs=xT[:, kc, tt, 1:513],
                            start=(kc == 0), stop=(kc == 15)), sig=(kc == 15))
                    bk_free[bi] = epi(ci, tt, bi, lp)
            win_free[slot] = lp

        yst_free = [None, None]
        for c in range(4):
            def epi_copy(base):
                def f(ci, tt, bi, lp):
                    P.wait("act", lp, yst_free[ci] if tt == 0 else None, tpad)
                    return P.op("act", lambda e: e.activation(
                        out=stg[:, base + ci, 1 + tt * 512:1 + (tt + 1) * 512], in_=bank(bi), func=AF.Copy))
                return f
            inproj_block(0 * 1024 + c * 256, epi_copy(0))
            inproj_block(1 * 1024 + c * 256, epi_copy(2))

            def epi_u(ci, tt, bi, lp):
                P.wait("dve", lp, P.last["act"])
                return P.op("dve", lambda e: e.tensor_tensor(
                    out=stg[:, 4 + ci, 1 + tt * 512:1 + (tt + 1) * 512], in0=stg[:, 2 + ci, 1 + tt * 512:1 + (tt + 1) * 512],
                    in1=bank(bi), op=ALU.mult))
            inproj_block(2 * 1024 + c * 256, epi_u)
            for ci in range(2):
                ch = c * 2 + ci
                u = stg[:, 4 + ci]
                tmp = stg[:, 2 + ci, 1:2049]
                P.wait("act", P.last["dve"])
                t1 = P.op("act", lambda e, u=u, tmp=tmp, ch=ch: e.activation(
                    out=tmp, in_=u[:, 1:2049], func=AF.Identity, scale=cs[:, li, 1, ch:ch + 1]))
                P.wait("dve", t1)
                t2 = P.op("dve", lambda e, u=u, tmp=tmp, ch=ch: e.scalar_tensor_tensor(
                    out=tmp, in0=u[:, 0:2048], scalar=cs[:, li, 0, ch:ch + 1], in1=tmp, op0=ALU.mult, op1=ALU.add))
                P.wait("dve", t2)
                t3 = P.op("dve", lambda e, u=u, tmp=tmp, ch=ch: e.scalar_tensor_tensor(
                    out=tmp, in0=u[:, 2:2050], scalar=cs[:, li, 2, ch:ch + 1], in1=tmp, op0=ALU.mult, op1=ALU.add))
                P.wait("pool", t3, yst_free[ci])
                t4 = P.op("pool", lambda e, ci=ci, tmp=tmp: e.tensor_tensor(
                    out=yst[:, ci], in0=tmp, in1=stg[:, ci, 1:2049], op=ALU.mult))
                P.wait("sp", t4)
                yst_free[ci] = P.dma("sp", lambda e, ci=ci, ch=ch: e.dma_start(
                    out=ycat[seq, ch * 128:(ch + 1) * 128, :], in_=yst[:, ci]), ds_y[ci])
                ycat_tok[0] = yst_free[ci]
        last_conv = [yst_free[0], yst_free[1]]

        P.barrier(extra=last_conv)
        scale = 1.0 / math.sqrt(128.0)
        scb_free = [None] * 3
        pb_free = [None] * 3
        nd_free = [None]
        yat_free = [None]
        sci = [0]
        for hp in range(4):
            def epi_rope(dst):
                def f(ci, tt, bi, lp):
                    sl = slice(tt * 512, (tt + 1) * 512)
                    P.wait("act", lp, P.last["pe"] if tt == 0 else None)
                    ta = P.op("act", lambda e: e.activation(out=qb[:, ci, sl], in_=bank(bi), func=AF.Copy))
                    P.wait("pe", ta, nd_free[0])
                    tsw = P.op("pe", lambda e: e.matmul(bank(4), lhsT=perm[:], rhs=qb[:, ci, sl], start=True, stop=True))
                    P.wait("dve", tsw, P.last["pool"])
                    t2_ = P.op("dve", lambda e: e.tensor_tensor(out=r2, in0=bank(4), in1=sinb[:, sl], op=ALU.mult))
                    nd_free[0] = t2_
                    P.wait("pool", ta, P.last["pool"])
                    t1_ = P.op("pool", lambda e: e.tensor_tensor(out=r1, in0=qb[:, ci, sl], in1=cosb[:, sl], op=ALU.mult))
                    P.wait("pool", t1_, t2_)
                    return P.op("pool", lambda e: e.tensor_tensor(out=dst[:, ci, sl], in0=r1, in1=r2, op=ALU.add))
                return f
            inproj_block(3 * 1024 + hp * 256, epi_rope(qT))
            inproj_block(4 * 1024 + hp * 256, epi_rope(kT))

            def epi_v(ci, tt, bi, lp):
                P.wait("act", lp)
                return P.op("act", lambda e: e.activation(out=vT[:, ci, tt * 512:(tt + 1) * 512], in_=bank(bi), func=AF.Copy))
            inproj_block(5 * 1024 + hp * 256, epi_v)
            for ci in range(2):
                h = hp * 2 + ci
                P.wait("pe", P.last["act"], P.last["pool"], P.last["dve"])
                for li_, r in enumerate((1, 4, 16)):
                    ntile = 16 // r
                    for half in range(2):
                        tp = None
                        for jj in range(8):
                            j = half * 8 + jj
                            ph, kt = j // ntile, j % ntile
                            src = vT[:, ci, :].rearrange("p (l r) -> p r l", r=r)[:, ph, kt * 128:(kt + 1) * 128]
                            tp = P.op("pe", lambda e, src=src, jj=jj, half=half: e.transpose(
                                out=bank_bf(6 + half)[:, jj * 128:(jj + 1) * 128], in_=src, identity=ident[:]), sig=(jj == 7))
                        eng = "act" if half == 0 else "dve"
                        P.wait(eng, tp)
                        srcv = bank_bf(6 + half).rearrange("p (j c) -> p j c", j=8)
                        if eng == "act":
                            te = P.op("act", lambda e, li_=li_, half=half, srcv=srcv: e.activation(
                                out=Vl[:, li_, half * 8:half * 8 + 8, :], in_=srcv, func=AF.Copy))
                        else:
                            te = P.op("dve", lambda e, li_=li_, half=half, srcv=srcv: e.tensor_copy(
                                out=Vl[:, li_, half * 8:half * 8 + 8, :], in_=srcv))
                        P.wait("pe", te)
                for Qb in range(4):
                    i0 = Qb * 512
                    banks_ = []
                    segA, segB = [], []
                    col = 0
                    for (rel, n, mc, q0) in ((-1, 64, 192, 0), (0, 192, 64, 0), (1, 256, 0, 64)):
                        kt = 4 * Qb + rel
                        segA.append((0, kt, ("c", i0 + q0, n, 1), ("c", kt * 128, 1), col, n, ("c", q0, n, 1)))
                        col += n
                    col = 0
                    for (rel, n, q0) in ((2, 256, 192), (3, 192, 320), (4, 64, 448)):
                        kt = 4 * Qb + rel
                        segB.append((0, kt, ("c", i0 + q0, n, 1), ("c", kt * 128, 1), col, n, ("c", q0, n, 1)))
                        col += n
                    banks_.append((0, segA)); banks_.append((1, segB))
                    for pp in range(2):
                        seg = []
                        col = 0
                        for ph in (2 * pp, 2 * pp + 1):
                            for (rel, n, q0) in ((-1, 64, 0), (0, 128, 0), (1, 64, 64)):
                                kt = Qb + rel
                                seg.append((1, ph * 4 + kt if 0 <= kt < 4 else -1, ("c", 4 * (128 * Qb + q0) + ph, n, 4),
                                            ("c", 4 * (128 * kt) + ph, 4), col, n, ("c", 4 * q0 + ph, n, 4)))
                                col += n
                        banks_.append((2, seg))
                    seg = []
                    for ph in range(16):
                        seg.append((2, ph, ("c", 16 * (32 * Qb) + ph, 32, 16), ("c", ph, 16), ph * 32, 32, ("c", ph, 32, 16)))
                    banks_.append((3 + Qb, seg))
                    first = [True]
                    P.wait("pe", nd_free[0])
                    fin = None
                    for (mi, seg) in banks_:
                        sb_ = sci[0] % 3
                        sci[0] += 1
                        valid = []
                        for (lay, vt, qs, ks, col, n, oc) in seg:
                            if lay == 0 and not (0 <= vt < 16):
                                continue
                            if lay == 1 and vt < 0:
                                continue
                            valid.append((lay, vt, qs, ks, col, n, oc))
                        P.wait("pe", bk_free[sb_])
                        lp = None
                        for vi, (lay, vt, qs, ks, col, n, oc) in enumerate(valid):
                            qa = qT[:, ci, qs[1]:qs[1] + (qs[2] - 1) * qs[3] + 1:qs[3]]
                            ka = kT[:, ci, ks[1]:ks[1] + 127 * ks[2] + 1:ks[2]]
                            lp = P.op("pe", lambda e, qa=qa, ka=ka, col=col, n=n, sb_=sb_: e.matmul(
                                bank(sb_)[:, col:col + n], lhsT=ka, rhs=qa, start=True, stop=True),
                                sig=(vi == len(valid) - 1))
                        P.wait("act", lp, pb_free[sb_])
                        te = P.op("act", lambda e, sb_=sb_: e.activation(out=Pb[:, sb_], in_=bank(sb_), func=AF.Exp, scale=scale))
                        bk_free[sb_] = te
                        P.wait("pool", te)
                        tm = P.op("pool", lambda e, sb_=sb_, mi=mi: e.tensor_tensor(out=Pb[:, sb_], in0=Pb[:, sb_], in1=maskb[:, mi], op=ALU.mult))
                        P.wait("pe", tm)
                        for (lay, vt, qs, ks, col, n, oc) in valid:
                            oa = slice(oc[1], oc[1] + (oc[2] - 1) * oc[3] + 1, oc[3])
                            st_ = first[0]
                            first[0] = False
                            P.op("pe", lambda e, lay=lay, vt=vt, col=col, n=n, oa=oa, st_=st_, sb_=sb_: e.matmul(
                                bank(4)[:, oa], lhsT=Vl[:, lay, vt, :], rhs=Pb[:, sb_, col:col + n],
                                start=st_, stop=False, skip_group_check=True), sig=False)
                            fin = P.op("pe", lambda e, col=col, n=n, oa=oa, st_=st_, sb_=sb_: e.matmul(
                                bank(5)[:, oa], lhsT=ones[:], rhs=Pb[:, sb_, col:col + n],
                                start=st_, stop=False, skip_group_check=True), sig=True)
                        pb_free[sb_] = fin
                    P.wait("dve", fin, yat_free[0] if Qb == 0 else None)
                    td = P.op("dve", lambda e: e.reciprocal(out=rden, in_=bank(5)))
                    P.wait("dve", td)
                    nd_free[0] = P.op("dve", lambda e, i0=i0: e.tensor_tensor(out=yat[:, i0:i0 + 512], in0=bank(4), in1=rden, op=ALU.mult))
                P.wait("sp", nd_free[0])
                yat_free[0] = P.dma("sp", lambda e, h=h: e.dma_start(
                    out=ycat[seq, (8 + h) * 128:(9 + h) * 128, :], in_=yat), ds_ya)
                ycat_tok[0] = yat_free[0]
        return [yat_free[0]] + last_conv

    for seq in range(nseq):
        P.barrier(extra=[tz] + ctoks)
        load_x(seq)
        P.barrier()
        if mode == "ffn":
            dump(xT[:, 0, 0, :], 0, 514, P.last["dve"], P.last["act"])
            dump(xT[:, 5, 1, :], 514, 514, P.last["dve"], P.last["act"])
            dump(cw[:, 0].rearrange("p k c -> p (k c)"), 4096, 264, P.last["dve"])
            ffn(seq, 0, x_in, y_out)
            P.barrier(extra=xres_free)
            continue
        if mode == "four":
            tk = fourier_m1(seq)
            P.barrier(extra=[tk])
            m2(seq, wb_of[0], ds_cast[("of", 0)], ln_mix_g, ln_mix_b, 1, x_in, y_out, tk)
            P.barrier(extra=xres_free)
            continue
        if mode == "even":
            tks = even_m1(seq, 0)
            P.barrier(extra=tks)
            m2(seq, wb_om[0], ds_cast[("om", 0)], ln_mix_g, ln_mix_b, 0, x_in, y_out, tks[0])
            P.barrier(extra=xres_free)
            continue
        for l in range(DEPTH):
            P.new_epoch()
            rsrc = x_in if l == 0 else resid
            if l % 2 == 0:
                tks = even_m1(seq, l // 2)
                P.barrier(extra=tks)
                m2(seq, wb_om[l // 2], ds_cast[("om", l // 2)], ln_mix_g, ln_mix_b, l, rsrc, resid, tks[0])
            else:
                tk = fourier_m1(seq)
                P.barrier(extra=[tk])
                m2(seq, wb_of[l // 2], ds_cast[("of", l // 2)], ln_mix_g, ln_mix_b, l, rsrc, resid, tk)
            P.barrier(extra=xres_free)
            ffn(seq, l, resid, y_out if l == DEPTH - 1 else resid)
            P.barrier(extra=xres_free)
    P.barrier(extra=xres_free)

    block = stack.enter_context(nc.Block())

    @block.tensor
    def _(e):
        P.emit("pe", e)

    @block.scalar
    def _(e):
        P.emit("act", e)

    @block.vector
    def _(e):
        P.emit("dve", e)

    @block.gpsimd
    def _(e):
        P.emit("pool", e)

    @block.sync
    def _(e):
        P.emit("sp", e)


def _bf(a):
    return np.asarray(a, np.float32).astype(ml_dtypes.bfloat16)


def make_consts():
    c = {}
    c["c_ident"] = _bf(np.eye(128))
    pm = np.zeros((128, 128), np.float32)
    for m in range(128):
        pm[(m + 64) % 128, m] = 1.0
    c["c_perm"] = _bf(pm)
    c["c_identf"] = np.eye(128, dtype=np.float32)
    half = 64
    inv = (1.0 / (10000.0 ** (np.arange(half, dtype=np.float32) / np.float32(half)))).astype(np.float32)
    ang = (np.arange(S, dtype=np.float32)[:, None] * inv[None, :]).astype(np.float32)
    cos = np.cos(ang.astype(np.float64)).T
    sin = np.sin(ang.astype(np.float64)).T
    c["c_cos"] = np.concatenate([cos, cos], 0).astype(np.float32)
    c["c_sin"] = np.concatenate([-sin, sin], 0).astype(np.float32)
    a = np.arange(128)[:, None]
    cc = np.arange(256)[None, :]
    master = (np.abs(cc - 64 - a) <= 64).astype(np.float32)
    masks = np.zeros((7, 128, 512), np.float32)
    masks[0] = np.concatenate([master[:, 192:256], master[:, 64:256], master[:, 0:256]], 1)
    masks[1] = np.concatenate([master[:, 0:256], master[:, 0:192], master[:, 0:64]], 1)
    one = np.concatenate([master[:, 192:256], master[:, 64:192], master[:, 0:64]], 1)
    masks[2] = np.concatenate([one, one], 1)
    for Qb in range(4):
        masks[3 + Qb] = np.tile(master[:, 64 + 32 * Qb:64 + 32 * Qb + 32], (1, 16))
    c["c_mask"] = _bf(masks)
    k = np.arange(512, dtype=np.float64)
    th = 2 * np.pi * np.outer(k, k) / 512.0
    c["c_cc"] = _bf(np.cos(th) / math.sqrt(512.0))
    c["c_sc"] = _bf(np.sin(th) / math.sqrt(512.0))
    k = np.arange(S, dtype=np.float64)
    th = 2 * np.pi * ((np.outer(k, k)) % S) / S
    c["c_cs"] = _bf(np.cos(th) / math.sqrt(S))
    c["c_ns"] = _bf(-np.sin(th) / math.sqrt(S))
    return c


_NC_CACHE = {}


def kernel(x_prompt, x_sample, w_in_mix, conv_short, w_out_mix, w_out_fourier, ln_mix_g, ln_mix_b,
           w_up, conv_ffn_w, w_down, ln_ffn_g, ln_ffn_b):
    f = lambda a: np.ascontiguousarray(np.asarray(a, dtype=np.float32))
    xp, xs = f(x_prompt), f(x_sample)
    consts = make_consts()
    shared = dict(w_in_mix=f(w_in_mix), conv_short=f(conv_short), w_out_mix=f(w_out_mix),
                  w_out_fourier=f(w_out_fourier), ln_mix_g=f(ln_mix_g), ln_mix_b=f(ln_mix_b),
                  w_up=f(w_up), conv_ffn_w=f(conv_ffn_w), w_down=f(w_down), ln_ffn_g=f(ln_ffn_g),
                  ln_ffn_b=f(ln_ffn_b))
    shared.update(consts)
    in_maps = []
    for c in range(8):
        xs_c = np.ascontiguousarray(np.stack([xp[2 * c], xp[2 * c + 1], xs[c]], 0))
        m = dict(shared)
        m["x"] = xs_c
        in_maps.append(m)
    nc = build(NSEQ, "full")
    res = run_bass_kernel_spmd(nc, in_maps, core_ids=list(range(8)))
    yp = np.empty_like(xp)
    ys = np.empty_like(xs)
    for c in range(8):
        y = res.results[c]["y"]
        yp[2 * c] = y[0]
        yp[2 * c + 1] = y[1]
        ys[c] = y[2]
    return (yp, ys)
```

```python
import math
from contextlib import ExitStack
import numpy as np
import ml_dtypes
import concourse.bass as bass
import concourse.mybir as mybir
from concourse.bass_utils import run_bass_kernel_spmd

F32 = mybir.dt.float32
BF16 = mybir.dt.bfloat16
AF = mybir.ActivationFunctionType
ALU = mybir.AluOpType

D = 2048
S = 2048
DEPTH = 4
DFF = 5632
NCH_FF = DFF // 128
DIN = 6144
ALPHA = (2 * DEPTH) ** 0.25
LN_EPS = 1e-5
NSEQ = 3
ENG = ("pe", "act", "dve", "pool", "sp")


class DSem:
    def __init__(self, h):
        self.h = h
        self.count = 0


class Rec:
    def __init__(self):
        self.call = None

    def __getattr__(self, name):
        def f(*a, **k):
            self.call = (name, a, k)
            return self
        return f


def freeze(fn):
    r = Rec()
    fn(r)
    name, a, k = r.call
    return lambda e: getattr(e, name)(*a, **k)


class Plan:
    def __init__(self, nc, stack):
        self.nc = nc
        self.stack = stack
        self.q = {e: [] for e in ENG}
        self.cnt = {e: 0 for e in ENG}
        self.sem = {}
        self.waited = {}
        self.nsem = 0
        self.last = {e: None for e in ENG}
        self.new_epoch()

    def mksem(self, name):
        self.nsem += 1
        return self.stack.enter_context(self.nc.semaphore(f"{name}_{self.nsem}"))

    def dsem(self, name):
        return DSem(self.mksem(name))

    def new_epoch(self):
        for e in ("pe", "act", "dve", "pool"):
            self.sem[e] = self.mksem("e" + e)
            self.cnt[e] = 0

    def op(self, eng, fn, sig=True):
        fn = freeze(fn)
        if sig:
            self.cnt[eng] += 1
            tok = (self.sem[eng], self.cnt[eng])
            self.q[eng].append(("op", fn, self.sem[eng]))
            self.last[eng] = tok
            return tok
        self.q[eng].append(("op", fn, None))
        return None

    def wait(self, eng, *toks):
        for tok in toks:
            if tok is None:
                continue
            if isinstance(tok, (list,)):
                self.wait(eng, *tok)
                continue
            sem, val = tok
            key = (eng, id(sem))
            if self.waited.get(key, 0) >= val:
                continue
            self.waited[key] = val
            self.q[eng].append(("wait", sem, val))

    def dma(self, eng, fn, ds):
        fn = freeze(fn)
        ds.count += 16
        self.q[eng].append(("dma", fn, ds.h))
        return (ds.h, ds.count)

    def barrier(self, extra=()):
        toks = [self.last[e] for e in ("pe", "act", "dve", "pool")] + list(extra)
        for e in ENG:
            self.wait(e, *toks)

    def emit(self, eng, e):
        for kind, a, b in self.q[eng]:
            if kind == "op":
                inst = a(e)
                if b is not None:
                    inst.then_inc(b, 1)
            elif kind == "wait":
                e.wait_ge(a, b)
            else:
                a(e).then_inc(b, 16)


def build(nseq=NSEQ, mode="full"):
    nc = bass.Bass("TRN2", target_bir_lowering=False)
    stack = ExitStack()
    with stack:
        _build(nc, stack, nseq, mode)
    return nc


def _build(nc, stack, nseq, mode):
    def din(name, shape, dt=F32):
        return nc.dram_tensor(name, list(shape), dt, kind="ExternalInput").ap()

    def dscr(name, shape, dt):
        return nc.dram_tensor(name, list(shape), dt, kind="Internal").ap()

    x_in = din("x", [nseq, S, D])
    y_out = nc.dram_tensor("y", [nseq, S, D], F32, kind="ExternalOutput").ap()
    w_in = din("w_in_mix", [2, D, DIN])
    conv_short = din("conv_short", [2, 3, 1024])
    w_om = din("w_out_mix", [2, D, D])
    w_of = din("w_out_fourier", [2, D, D])
    ln_mix_g = din("ln_mix_g", [4, D]); ln_mix_b = din("ln_mix_b", [4, D])
    w_up = din("w_up", [4, D, 2 * DFF])
    conv_ffn = din("conv_ffn_w", [4, 3, 2 * DFF])
    w_dn = din("w_down", [4, DFF, D])
    ln_ffn_g = din("ln_ffn_g", [4, D]); ln_ffn_b = din("ln_ffn_b", [4, D])
    c_ident = din("c_ident", [128, 128], BF16)
    c_perm = din("c_perm", [128, 128], BF16)
    c_identf = din("c_identf", [128, 128], F32)
    c_cos = din("c_cos", [128, S], F32)
    c_sin = din("c_sin", [128, S], F32)
    c_mask = din("c_mask", [7, 128, 512], BF16)
    c_cc = din("c_cc", [512, 512], BF16)
    c_sc = din("c_sc", [512, 512], BF16)
    c_cs = din("c_cs", [S, S], BF16)
    c_ns = din("c_ns", [S, S], BF16)

    dbg = nc.dram_tensor("dbg", [128, 8192], F32, kind="ExternalOutput").ap() if mode != "full" else None
    ds_dbg = None
    dbg16 = nc.dram_tensor("dbg16", [128, 8192], BF16, kind="ExternalOutput").ap() if mode != "full" else None
    wb_in = dscr("wb_in", [2, D, DIN], BF16)
    wb_om = dscr("wb_om", [2, D, D], BF16)
    wb_of = dscr("wb_of", [2, D, D], BF16)
    wb_up = dscr("wb_up", [4, D, 2 * DFF], BF16)
    wb_dn = dscr("wb_dn", [4, DFF, D], BF16)
    resid = dscr("resid", [nseq, S, D], F32)
    ycat = dscr("ycat", [nseq, D, S], BF16)

    sb = lambda name, shape, dt: stack.enter_context(nc.sbuf_tensor(name, list(shape), dt))
    xT = sb("xT", [128, 16, 4, 514], BF16)
    ident = sb("ident", [128, 128], BF16)
    perm = sb("perm", [128, 128], BF16)
    ones = sb("ones", [128, 128], BF16)
    identf = sb("identf", [128, 128], F32)
    cw = sb("cw", [128, 4, 3, 88], F32)
    cs = sb("cs", [128, 2, 3, 8], F32)
    small = sb("small", [128, 64], F32)
    NW = 34816
    WK = sb("WK", [128, NW], F32)
    ps = [stack.enter_context(nc.psum_tensor(f"ps{i}", [128, 1024], F32)) for i in range(4)]

    def cv(off, shape, dt):
        n = int(np.prod(shape))
        if dt == F32:
            assert off % 4 == 0
            ap = WK[:, off // 4: off // 4 + n]
        else:
            ap = WK[:, off // 4: off // 4 + (n + 1) // 2].bitcast(BF16)
        if len(shape) == 1:
            return ap
        if len(shape) == 2:
            return ap.rearrange("p (a b) -> p a b", a=shape[0])
        if len(shape) == 3:
            return ap.rearrange("p (a b c) -> p a b c", a=shape[0], b=shape[1])
        raise ValueError

    def bank(i):
        return ps[i // 2][:, (i % 2) * 512:(i % 2) * 512 + 512]

    def bank_bf(i):
        return ps[i // 2][:, (i % 2) * 512:(i % 2) * 512 + 512].bitcast(BF16)

    P = Plan(nc, stack)

    acc = cv(0, [4, 2048], F32)
    xres = cv(32768, [2, 2048], F32)
    ybf = cv(49152, [2048], BF16)
    gam = cv(53248, [2048], F32)
    bet = cv(61440, [2048], F32)
    wdn = cv(69632, [3, 2, 2048], BF16)
    wup = cv(94208, [3, 16, 256], BF16)
    aT = cv(118784, [4, 2, 512], BF16)
    ctmp = cv(126976, [2, 512], F32)
    sgb = cv(131072, [4, 512], F32)
    kopb = cv(94208, [2, 16, 512], BF16)
    assert 139264 <= NW * 4
    st6 = small[:, 0:48].rearrange("p (b c s) -> p b c s", b=2, c=4)
    mv = small[:, 48:52].rearrange("p (b s) -> p b s", b=2)
    rstd = small[:, 52:54]
    nmr = small[:, 54:56]
    sdv = small[:, 56:58]

    ds_const = P.dsem("const")
    ds_cast = {}
    ds_wdn = [P.dsem("wdn") for _ in range(3)]
    ds_wup = [P.dsem("wup") for _ in range(3)]
    ds_xres = [P.dsem("xres") for _ in range(2)]
    ds_yst = [P.dsem("yst") for _ in range(2)]
    ds_gb = P.dsem("gb")
    ds_kop = [P.dsem("kop") for _ in range(2)]
    ds_misc = P.dsem("misc")
    ds_dbg = P.dsem("dbg")
    ds_blk = [P.dsem("blk") for _ in range(4)]
    ds_f = P.dsem("fst")
    ds_win = [P.dsem("win") for _ in range(3)]
    ds_y = [P.dsem("ycs") for _ in range(2)]
    ds_ya = P.dsem("yat")

    def dump(ap, col0, n, *toks):
        if dbg is None:
            return
        P.wait("sp", *toks)
        tgt = dbg16 if ap.dtype == BF16 else dbg
        t = P.dma("sp", lambda e: e.dma_start(out=tgt[:, col0:col0 + n], in_=ap), ds_dbg)
        for e_ in ENG:
            P.wait(e_, t)

    ctoks = []
    ctoks.append(P.dma("sp", lambda e: e.dma_start(out=ident[:], in_=c_ident[:, :]), ds_const))
    ctoks.append(P.dma("sp", lambda e: e.dma_start(out=perm[:], in_=c_perm[:, :]), ds_const))
    ctoks.append(P.dma("sp", lambda e: e.dma_start(out=identf[:], in_=c_identf[:, :]), ds_const))
    cwraw = cv(0, [4, 3, 128], F32)
    csraw = cv(8192, [2, 3, 128], F32)
    for l in range(4):
        ctoks.append(P.dma("sp", lambda e, l=l: e.dma_start(
            out=cwraw[0:88, l], in_=conv_ffn[l].rearrange("k (c p) -> c k p", p=128)), ds_const))
    for i in range(2):
        ctoks.append(P.dma("sp", lambda e, i=i: e.dma_start(
            out=csraw[0:8, i], in_=conv_short[i].rearrange("k (c p) -> c k p", p=128)), ds_const))
    P.wait("pe", ctoks[-1]); P.wait("pool", ctoks[-1]); P.wait("dve", ctoks[-1]); P.wait("act", ctoks[-1])
    P.op("pool", lambda e: e.memset(ones[:], 1.0))
    P.op("pool", lambda e: e.memset(xT[:, :, 0, 0:1], 0.0))
    tz = P.op("pool", lambda e: e.memset(xT[:, :, 3, 513:514], 0.0))
    tcw = None
    for l in range(4):
        for k in range(3):
            idx = l * 3 + k
            bk, col = idx // 5, (idx % 5) * 88
            tcw = P.op("pe", lambda e, l=l, k=k, bk=bk, col=col: e.transpose(
                out=bank(bk)[:, col:col + 88], in_=cwraw[0:88, l, k, :], identity=identf[0:88, 0:88]),
                sig=(idx == 11))
    P.wait("dve", tcw)
    for l in range(4):
        for k in range(3):
            idx = l * 3 + k
            bk, col = idx // 5, (idx % 5) * 88
            P.op("dve", lambda e, l=l, k=k, bk=bk, col=col: e.tensor_copy(out=cw[:, l, k, :], in_=bank(bk)[:, col:col + 88]))
    tcs = None
    for i in range(2):
        for k in range(3):
            idx = i * 3 + k
            tcs = P.op("pe", lambda e, i=i, k=k, idx=idx: e.transpose(
                out=bank(3)[:, idx * 8:idx * 8 + 8], in_=csraw[0:8, i, k, :], identity=identf[0:8, 0:8]),
                sig=(idx == 5))
    P.wait("dve", tcs)
    for i in range(2):
        for k in range(3):
            idx = i * 3 + k
            P.op("dve", lambda e, i=i, k=k, idx=idx: e.tensor_copy(out=cs[:, i, k, :], in_=bank(3)[:, idx * 8:idx * 8 + 8]))

    def cast(name, src, dst, l):
        sflat = src[l].rearrange("k (j c) -> (k j) c", c=1024)
        dflat = dst[l].rearrange("k (j c) -> (k j) c", c=1024)
        rows = sflat.shape[0]
        dsx = P.dsem("cast" + name)
        tok = None
        r = 0
        while r < rows:
            n = min(8192, rows - r)
            tok = P.dma("pool", lambda e, r=r, n=n: e.dma_start(out=dflat[r:r + n, :], in_=sflat[r:r + n, :]), dsx)
            r += n
        ds_cast[(name, l)] = tok

    for l in range(4):
        if l % 2 == 0:
            cast("in", w_in, wb_in, l // 2); cast("om", w_om, wb_om, l // 2)
        else:
            cast("of", w_of, wb_of, l // 2)
        cast("up", w_up, wb_up, l); cast("dn", w_dn, wb_dn, l)

    st = dict(wdn_i=0, wup_i=0, bankC=0, xres_i=0, chunk_i=0)
    wdn_free = [None] * 3
    wup_free = [None] * 3
    bankC_free = [None, None]
    bankD_free = [None]
    acc_free = [None] * 4
    xres_free = [None, None]
    ybf_free = [None]
    xT_ready = []
    pair_free = [None, None]
    aT_free = [None] * 4
    sg_free = [None] * 4
    ct_free = [None, None]
    gb_tok = [None]

    def xcol(t):
        return t // 512, t % 512 + 1

    def writeback(ybf_tok, t0, ffn_inplace):
        slot, c0 = xcol(t0)
        P.wait("pe", ybf_tok, bankD_free[0])
        tp = None
        for d in range(16):
            tp = P.op("pe", lambda e, d=d: e.transpose(
                out=bank_bf(6 + d // 8)[:, (d % 8) * 128:(d % 8) * 128 + 128],
                in_=ybf[:, d * 128:(d + 1) * 128], identity=ident[:]), sig=(d == 15))
        ybf_free[0] = tp
        toks = []
        for half, eng in ((0, "dve"), (1, "act")):
            src = bank_bf(6 + half).rearrange("p (d c) -> p d c", d=8)
            P.wait(eng, tp)
            if eng == "dve":
                cp = lambda e, o, i: e.tensor_copy(out=o, in_=i)
            else:
                cp = lambda e, o, i: e.activation(out=o, in_=i, func=AF.Copy)
            dsl = slice(half * 8, half * 8 + 8)
            tk = P.op(eng, lambda e, cp=cp, dsl=dsl, src=src: cp(e, xT[:, dsl, slot, c0:c0 + 128], src))
            if t0 % 512 == 0 and slot > 0:
                tk = P.op(eng, lambda e, cp=cp, dsl=dsl, src=src: cp(e, xT[:, dsl, slot - 1, 513:514], src[:, :, 0:1]))
            if t0 % 512 == 384 and slot < 3 and not ffn_inplace:
                tk = P.op(eng, lambda e, cp=cp, dsl=dsl, src=src: cp(e, xT[:, dsl, slot + 1, 0:1], src[:, :, 127:128]))
            toks.append(tk)
        bankD_free[0] = toks
        xT_ready[:] = toks
        return toks

    def load_gb(g_ap, b_ap, l):
        P.wait("sp", P.last["pool"], P.last["act"], P.last["dve"])
        P.dma("sp", lambda e: e.dma_start(out=gam, in_=g_ap[l:l + 1, :].partition_broadcast(128)), ds_gb)
        gb_tok[0] = P.dma("sp", lambda e: e.dma_start(out=bet, in_=b_ap[l:l + 1, :].partition_broadcast(128)), ds_gb)

    def d_prefetch(wsrc, cast_tok):
        slot = st["wdn_i"] % 3
        st["wdn_i"] += 1
        P.wait("sp", wdn_free[slot], cast_tok)
        ld = P.dma("sp", lambda e: e.dma_start(out=wdn[:, slot], in_=wsrc), ds_wdn[slot])
        return slot, ld

    def d_part(q, k, kops, slot, ld, kop_toks):
        P.wait("pe", ld, *kop_toks)
        last_pe = None
        s = k
        for n in range(4):
            bi = st["bankC"] % 2
            st["bankC"] += 1
            P.wait("pe", bankC_free[bi])
            for g in range(2):
                last_pe = P.op("pe", lambda e, g=g: e.matmul(
                    bank(4 + bi), lhsT=kops[g][:, s * 128:(s + 1) * 128],
                    rhs=wdn[:, slot, g, n * 512:(n + 1) * 512], start=(g == 0), stop=(g == 1)), sig=(g == 1))
            P.wait("dve", last_pe)
            dst = acc[:, s, n * 512:(n + 1) * 512]
            if q == 0:
                P.wait("dve", acc_free[s])
                bankC_free[bi] = P.op("dve", lambda e: e.tensor_copy(out=dst, in_=bank(4 + bi)))
            else:
                bankC_free[bi] = P.op("dve", lambda e: e.tensor_tensor(out=dst, in0=dst, in1=bank(4 + bi), op=ALU.add))
        if k == 3:
            wdn_free[slot] = last_pe
        return last_pe

    fin = {}

    def fin_chain(seq, tt, s, rsrc, rdst):
        t0 = tt * 512 + s * 128
        b = st["xres_i"] % 2
        st["xres_i"] += 1
        P.wait("sp", xres_free[b])
        ldx = P.dma("sp", lambda e: e.dma_start(out=xres[:, b], in_=rsrc[seq, t0:t0 + 128, :]), ds_xres[b])
        P.wait("dve", ldx, P.last["dve"], P.last["act"])
        P.op("dve", lambda e: e.scalar_tensor_tensor(
            out=acc[:, s], in0=xres[:, b], scalar=float(ALPHA), in1=acc[:, s], op0=ALU.mult, op1=ALU.add))
        P.wait("dve", P.last["dve"])
        for c in range(4):
            P.op("dve", lambda e, c=c: e.bn_stats(out=st6[:, b, c], in_=acc[:, s, c * 512:(c + 1) * 512]))
        P.wait("dve", P.last["dve"])
        P.op("dve", lambda e: e.bn_aggr(out=mv[:, b], in_=st6[:, b].rearrange("p c s -> p (c s)")))
        P.wait("act", P.last["dve"])
        tsd = P.op("act", lambda e: e.activation(
            out=sdv[:, b:b + 1], in_=mv[:, b, 1:2], func=AF.Sqrt, bias=float(LN_EPS), scale=1.0))
        P.wait("dve", tsd)
        P.op("dve", lambda e: e.reciprocal(out=rstd[:, b:b + 1], in_=sdv[:, b:b + 1]))
        P.wait("dve", P.last["dve"])
        tstat = P.op("dve", lambda e: e.scalar_tensor_tensor(
            out=nmr[:, b:b + 1], in0=mv[:, b, 0:1], scalar=-1.0, in1=rstd[:, b:b + 1], op0=ALU.mult, op1=ALU.mult))
        P.wait("act", tstat)
        txh = P.op("act", lambda e: e.activation(
            out=xres[:, b], in_=acc[:, s], func=AF.Identity, scale=rstd[:, b:b + 1], bias=nmr[:, b:b + 1]))
        acc_free[s] = txh
        P.wait("pool", txh, gb_tok[0])
        P.op("pool", lambda e: e.tensor_tensor(out=xres[:, b], in0=xres[:, b], in1=gam, op=ALU.mult))
        P.wait("pool", P.last["pool"])
        ty = P.op("pool", lambda e: e.tensor_tensor(out=xres[:, b], in0=xres[:, b], in1=bet, op=ALU.add))
        fin[(tt, s)] = (b, ty, t0, seq, rdst)

    def fin_cast(tt, s):
        b, ty, t0, seq, rdst = fin[(tt, s)]
        P.wait("act", ty, ybf_free[0])
        tyb = P.op("act", lambda e: e.activation(out=ybf, in_=xres[:, b], func=AF.Copy))
        P.wait("sp", ty, tyb)
        xres_free[b] = P.dma("sp", lambda e: e.dma_start(out=rdst[seq, t0:t0 + 128, :], in_=xres[:, b]), ds_yst[b])
        fin[(tt, s)] = (tyb, t0)

    def fin_wb(tt, s, ffn_inplace):
        tyb, t0 = fin.pop((tt, s))
        writeback(tyb, t0, ffn_inplace)

    def g2_finish(seq, tt, rsrc, rdst, ffn_inplace):
        fin_chain(seq, tt, 0, rsrc, rdst)
        for s in range(4):
            if s + 1 < 4:
                fin_chain(seq, tt, s + 1, rsrc, rdst)
            fin_cast(tt, s)
            fin_wb(tt, s, ffn_inplace)

    def load_x(seq):
        for s in range(16):
            t0 = s * 128
            b = st["xres_i"] % 2
            st["xres_i"] += 1
            P.wait("sp", xres_free[b])
            ldx = P.dma("sp", lambda e, b=b, t0=t0: e.dma_start(out=xres[:, b], in_=x_in[seq, t0:t0 + 128, :]), ds_xres[b])
            P.wait("act", ldx, ybf_free[0])
            tyb = P.op("act", lambda e, b=b: e.activation(out=ybf, in_=xres[:, b], func=AF.Copy))
            xres_free[b] = tyb
            writeback(tyb, t0, False)

    def ffn(seq, l, rsrc, rdst):
        load_gb(ln_ffn_g, ln_ffn_b, l)
        wu = wb_up[l].rearrange("(kc p) n -> p kc n", p=128)
        wd = wb_dn[l].rearrange("(j p) n -> p j n", p=128)
        LAG = 3
        items = [(tt, b, half) for tt in range(4) for b in range(22) for half in range(2)]
        wl = {}

        def up_prefetch(i):
            if i >= len(items) or i in wl:
                return
            tt_, b_, half_ = items[i]
            col0 = half_ * DFF + b_ * 256
            slot = st["wup_i"] % 3
            st["wup_i"] += 1
            P.wait("sp", wup_free[slot], ds_cast[("up", l)])
            wl[i] = (slot, P.dma("sp", lambda e: e.dma_start(out=wup[:, slot], in_=wu[:, :, col0:col0 + 256]), ds_wup[slot]))

        up_prefetch(0)
        up_prefetch(1)
        it = 0
        for tt in range(4):
            u_done = None
            grp_toks = {}
            dgrp = {}

            def dpre(q):
                if 0 <= q < 22 and q not in dgrp:
                    dgrp[q] = d_prefetch(wd[:, 2 * q:2 * q + 2, :], ds_cast[("dn", l)])

            def dpart(q, k):
                gi = q % 4
                slot, ld = dgrp[q]
                tk = d_part(q, k, [aT[:, gi, 0], aT[:, gi, 1]], slot, ld,
                            [grp_toks[(q, 0, "a")], grp_toks[(q, 1, "a")]])
                if k == 3:
                    aT_free[gi] = tk

            for b in range(22):
                dpre(b - LAG)
                dpre(b - LAG + 1)
                for half in range(2):
                    slot, ld = wl.pop(it)
                    up_prefetch(it + 2)
                    it += 1
                    if seq == 0 and tt == 0 and b == 0 and half == 0:
                        dump(wup[:, slot].rearrange("p k c -> p (k c)"), 2048, 4096, ld)
                    P.wait("pe", ld)
                    for ci in range(2):
                        ch = half * NCH_FF + b * 2 + ci
                        pi = st["chunk_i"] % 2
                        st["chunk_i"] += 1
                        P.wait("pe", pair_free[pi])
                        lp = None
                        for kc in range(16):
                            for hh in range(2):
                                lp = P.op("pe", lambda e: e.matmul(
                                    bank(2 * pi + hh)[:, 0:258], lhsT=wup[:, slot, kc, ci * 128:(ci + 1) * 128],
                                    rhs=xT[:, kc, tt, hh * 256:hh * 256 + 258], start=(kc == 0), stop=(kc == 15)),
                                    sig=(kc == 15 and hh == 1))
                        u_done = lp
                        if b >= LAG:
                            dpart(b - LAG, half * 2 + ci)
                        pv = ps[pi].rearrange("p (h c) -> p h c", h=2)
                        cti = pi
                        ct = ctmp[:, cti].rearrange("p (h c) -> p h c", h=2)
                        P.wait("act", lp, ct_free[cti])
                        ta = P.op("act", lambda e: e.activation(
                            out=ct, in_=pv[:, :, 1:257], func=AF.Identity, scale=cw[:, l, 1, ch:ch + 1]))
                        P.wait("dve", ta)
                        tb = P.op("dve", lambda e: e.scalar_tensor_tensor(
                            out=ct, in0=pv[:, :, 0:256], scalar=cw[:, l, 0, ch:ch + 1], in1=ct, op0=ALU.mult, op1=ALU.add))
                        P.wait("dve", tb)
                        tc = P.op("dve", lambda e: e.scalar_tensor_tensor(
                            out=ct, in0=pv[:, :, 2:258], scalar=cw[:, l, 2, ch:ch + 1], in1=ct, op0=ALU.mult, op1=ALU.add))
                        pair_free[pi] = tc
                        if seq == 0 and tt == 0 and b == 0 and ci == 0:
                            dump(ctmp[:, cti], 0 if half == 0 else 1024, 512, tc)
                        sgi = (b % 2) * 2 + ci
                        if half == 0:
                            P.wait("act", tc, sg_free[sgi])
                            ct_free[cti] = P.op("act", lambda e: e.activation(
                                out=sgb[:, sgi], in_=ctmp[:, cti], func=AF.Silu))
                            grp_toks[(b, ci, "sg")] = ct_free[cti]
                        else:
                            gi = b % 4
                            P.wait("pool", tc, grp_toks[(b, ci, "sg")], aT_free[gi])
                            tm = P.op("pool", lambda e: e.tensor_tensor(
                                out=aT[:, gi, ci], in0=sgb[:, sgi], in1=ctmp[:, cti], op=ALU.mult))
                            ct_free[cti] = tm
                            sg_free[sgi] = tm
                            grp_toks[(b, ci, "a")] = tm
                    wup_free[slot] = u_done
                if tt >= 1:
                    pt = tt - 1
                    if b == 0:
                        fin_chain(seq, pt, 0, rsrc, rdst); fin_cast(pt, 0); fin_chain(seq, pt, 1, rsrc, rdst)
                    elif b == 1:
                        fin_wb(pt, 0, True); fin_cast(pt, 1); fin_chain(seq, pt, 2, rsrc, rdst)
                    elif b == 2:
                        fin_wb(pt, 1, True); fin_cast(pt, 2); fin_chain(seq, pt, 3, rsrc, rdst)
                    elif b == 3:
                        fin_wb(pt, 2, True); fin_cast(pt, 3)
                    elif b == 4:
                        fin_wb(pt, 3, True)
            for q in range(22 - LAG, 22):
                dpre(q)
                for k in range(4):
                    dpart(q, k)
            if tt >= 1:
                P.wait("pool", u_done, *xT_ready)
                P.op("pool", lambda e: e.tensor_copy(out=xT[:, :, tt, 0:1], in_=xT[:, :, tt - 1, 512:513]))
            if seq == 0 and tt == 0:
                dump(acc[:, 0], 2048, 2048, P.last["dve"])
            if tt == 3:
                g2_finish(seq, tt, rsrc, rdst, True)

    def m2(seq, wsrc_l, cast_tok, g_ap, b_ap, l, rsrc, rdst, ycat_tok):
        load_gb(g_ap, b_ap, l)
        wv = wsrc_l.rearrange("(kc p) n -> p kc n", p=128)
        yc = ycat[seq].rearrange("(kc p) t -> p kc t", p=128)
        wblk = cv(69632, [2, 16, 512], BF16)
        kop2 = cv(102400, [2, 16, 512], BF16)
        kop_free = [None, None]
        wb_free = [None, None]
        kl = {}
        wq = {}
        wi_ = [0]

        def kpre(tt):
            if tt < 4 and tt not in kl:
                kb = tt % 2
                P.wait("sp", kop_free[kb], ycat_tok)
                kl[tt] = P.dma("sp", lambda e: e.dma_start(out=kop2[:, kb], in_=yc[:, :, tt * 512:(tt + 1) * 512]), ds_kop[kb])

        def wpre(i):
            if i < 16 and i not in wq:
                n = i % 4
                slot = wi_[0] % 2
                wi_[0] += 1
                P.wait("sp", wb_free[slot], cast_tok)
                wq[i] = (slot, P.dma("sp", lambda e: e.dma_start(out=wblk[:, slot], in_=wv[:, :, n * 512:(n + 1) * 512]), ds_wdn[slot]))

        kpre(0)
        wpre(0)
        ev = 0
        for tt in range(4):
            kb = tt % 2
            kpre(tt + 1)
            lp = None
            for n in range(4):
                slot, ld = wq[tt * 4 + n]
                wpre(tt * 4 + n + 1)
                P.wait("pe", ld, kl[tt])
                for s in range(4):
                    bi = st["bankC"] % 2
                    st["bankC"] += 1
                    P.wait("pe", bankC_free[bi])
                    for kc in range(16):
                        lp = P.op("pe", lambda e: e.matmul(
                            bank(4 + bi), lhsT=kop2[:, kb, kc, s * 128:(s + 1) * 128], rhs=wblk[:, slot, kc, :],
                            start=(kc == 0), stop=(kc == 15)), sig=(kc == 15))
                    eng = "dve" if ev % 2 == 0 else "act"
                    ev += 1
                    dst = acc[:, s, n * 512:(n + 1) * 512]
                    P.wait(eng, lp, acc_free[s])
                    if eng == "dve":
                        bankC_free[bi] = P.op("dve", lambda e: e.tensor_copy(out=dst, in_=bank(4 + bi)))
                    else:
                        bankC_free[bi] = P.op("act", lambda e: e.activation(out=dst, in_=bank(4 + bi), func=AF.Copy))
                wb_free[slot] = lp
            kop_free[kb] = lp
            g2_finish(seq, tt, rsrc, rdst, False)

    def fourier_m1(seq):
        Yc = cv(0, [16, 512], BF16)
        Ys = cv(16384, [16, 512], BF16)
        CC = cv(32768, [4, 512], BF16)
        SC = cv(36864, [4, 512], BF16)
        blk = cv(40960, [4, 16, 256], BF16)
        fst = cv(73728, [4, 2048], BF16)
        blk_free = [None] * 4
        P.wait("sp", P.last["pe"])
        P.dma("sp", lambda e: e.dma_start(out=CC, in_=c_cc.rearrange("(k p) n -> p k n", p=128)), ds_misc)
        tcc = P.dma("sp", lambda e: e.dma_start(out=SC, in_=c_sc.rearrange("(k p) n -> p k n", p=128)), ds_misc)
        csv = c_cs.rearrange("(k p) n -> p k n", p=128)
        nsv = c_ns.rearrange("(k p) n -> p k n", p=128)
        bC = [None, None]
        y_free = [None]
        fst_free = [None]
        cnt = 0
        for g in range(4):
            P.wait("pe", tcc)
            ev = []
            for stl in range(16):
                slot, c0 = xcol(stl * 128)
                for which, (M, Yb, eng) in enumerate(((CC, Yc, "act"), (SC, Ys, "dve"))):
                    bi = which
                    P.wait("pe", bC[bi], y_free[0] if stl == 0 else None)
                    lp = None
                    for kc in range(4):
                        lp = P.op("pe", lambda e, kc=kc, slot=slot, c0=c0, M=M, bi=bi: e.matmul(
                            bank(4 + bi), lhsT=xT[:, 4 * g + kc, slot, c0:c0 + 128], rhs=M[:, kc, :],
                            start=(kc == 0), stop=(kc == 3)), sig=(kc == 3))
                    P.wait(eng, lp)
                    if eng == "act":
                        bC[bi] = P.op("act", lambda e, Yb=Yb, stl=stl, bi=bi: e.activation(out=Yb[:, stl], in_=bank(4 + bi), func=AF.Copy))
                    else:
                        bC[bi] = P.op("dve", lambda e, Yb=Yb, stl=stl, bi=bi: e.tensor_copy(out=Yb[:, stl], in_=bank(4 + bi)))
                    ev.append(bC[bi])
            P.wait("pe", ev[-1], ev[-2])
            P.wait("act", fst_free[0])
            lastpe = None
            for sp_ in range(8):
                bs = sp_ % 2
                P.wait("sp", blk_free[bs], blk_free[2 + bs])
                l1 = P.dma("sp", lambda e, bs=bs, sp_=sp_: e.dma_start(out=blk[:, bs], in_=csv[:, :, sp_ * 256:(sp_ + 1) * 256]), ds_blk[bs])
                l2 = P.dma("sp", lambda e, bs=bs, sp_=sp_: e.dma_start(out=blk[:, 2 + bs], in_=nsv[:, :, sp_ * 256:(sp_ + 1) * 256]), ds_blk[2 + bs])
                P.wait("pe", l1, l2)
                for cp_ in range(4):
                    bi = cnt % 2
                    cnt += 1
                    P.wait("pe", pair_free[bi])
                    lp = None
                    for which, Yb in enumerate((Yc, Ys)):
                        for sc in range(16):
                            lp = P.op("pe", lambda e, Yb=Yb, sc=sc, cp_=cp_, bs=bs, which=which, bi=bi: e.matmul(
                                bank(bi)[:, 0:256], lhsT=Yb[:, sc, cp_ * 128:(cp_ + 1) * 128], rhs=blk[:, 2 * which + bs, sc, :],
                                start=(which == 0 and sc == 0), stop=(which == 1 and sc == 15)),
                                sig=(which == 1 and sc == 15))
                    lastpe = lp
                    P.wait("act", lp)
                    pair_free[bi] = P.op("act", lambda e, cp_=cp_, sp_=sp_, bi=bi: e.activation(
                        out=fst[:, cp_, sp_ * 256:(sp_ + 1) * 256], in_=bank(bi)[:, 0:256], func=AF.Copy))
                blk_free[bs] = lastpe
                blk_free[2 + bs] = lastpe
            y_free[0] = lastpe
            P.wait("sp", P.last["act"])
            for cp_ in range(4):
                ch = 4 * g + cp_
                fst_free[0] = P.dma("sp", lambda e, cp_=cp_, ch=ch: e.dma_start(
                    out=ycat[seq, ch * 128:(ch + 1) * 128, :], in_=fst[:, cp_]), ds_f)
        return fst_free[0]

    def even_m1(seq, li):
        wi = wb_in[li].rearrange("(kc p) n -> p kc n", p=128)
        winb = cv(0, [3, 16, 256], BF16)
        cosb = cv(24576, [2048], F32)
        sinb = cv(32768, [2048], F32)
        maskb = cv(40960, [7, 512], BF16)
        stg = cv(49152, [6, 2050], F32)
        yst = cv(98352 + 16, [2, 2048], BF16)
        qb = cv(49152, [2, 2048], BF16)
        r1 = cv(57344, [512], F32)
        r2 = cv(59392, [512], F32)
        qT = cv(61440, [2, 2048], BF16)
        kT = cv(69632, [2, 2048], BF16)
        vT = cv(77824, [2, 2048], BF16)
        Vl = cv(86016, [3, 16, 128], BF16)
        Pb = cv(106560, [3, 512], BF16)
        rden = cv(109632, [512], F32)
        yat = cv(111680, [2048], BF16)
        win_free = [None] * 3
        wi_i = [0]
        P.wait("sp", P.last["pe"], P.last["act"], P.last["dve"], P.last["pool"])
        P.dma("sp", lambda e: e.dma_start(out=cosb, in_=c_cos[:, :]), ds_misc)
        P.dma("sp", lambda e: e.dma_start(out=sinb, in_=c_sin[:, :]), ds_misc)
        tconst = P.dma("sp", lambda e: e.dma_start(out=maskb, in_=c_mask.rearrange("m p c -> p m c")), ds_misc)
        for e_ in ("pe", "act", "dve", "pool"):
            P.wait(e_, tconst)
        for i in (4, 5):
            P.op("pool", lambda e, i=i: e.memset(stg[:, i, 0:1], 0.0))
            P.op("pool", lambda e, i=i: e.memset(stg[:, i, 2049:2050], 0.0))
        tpad = P.last["pool"]
        bk_free = [None] * 4
        bki = [0]
        ycat_tok = [None]

        col_order = []
        for c_ in range(4):
            col_order += [0 * 1024 + c_ * 256, 1 * 1024 + c_ * 256, 2 * 1024 + c_ * 256]
        for hp_ in range(4):
            col_order += [3 * 1024 + hp_ * 256, 4 * 1024 + hp_ * 256, 5 * 1024 + hp_ * 256]
        wl_in = {}
        blk_i = [0]

        def in_prefetch(i):
            if i >= len(col_order) or i in wl_in:
                return
            c0_ = col_order[i]
            slot_ = i % 3
            P.wait("sp", win_free[slot_], ds_cast[("in", li)])
            wl_in[i] = (slot_, P.dma("sp", lambda e: e.dma_start(out=winb[:, slot_], in_=wi[:, :, c0_:c0_ + 256]), ds_win[slot_]))

        def inproj_block(col0, epi):
            i = blk_i[0]
            blk_i[0] += 1
            assert col_order[i] == col0
            in_prefetch(i)
            slot, ld = wl_in.pop(i)
            in_prefetch(i + 1)
            P.wait("pe", ld)
            lp = None
            for ci in range(2):
                for tt in range(4):
                    bi = bki[0] % 4
                    bki[0] += 1
                    P.wait("pe", bk_free[bi])
                    for kc in range(16):
                        lp = P.op("pe", lambda e, kc=kc, ci=ci, tt=tt, bi=bi: e.matmul(
                            bank(bi), lhsT=winb[:, slot, kc, ci * 128:(ci + 1) * 128], rhs=xT[:, kc, tt, 1:513],
                            start=(kc == 0), stop=(kc == 15)), sig=(kc == 15))
                    bk_free[bi] = epi(ci, tt, bi, lp)
            win_free[slot] = lp

        yst_free = [None, None]
        for c in range(4):
            def epi_copy(base):
                def f(ci, tt, bi, lp):
                    P.wait("act", lp, yst_free[ci] if tt == 0 else None, tpad)
                    return P.op("act", lambda e: e.activation(
                        out=stg[:, base + ci, 1 + tt * 512:1 + (tt + 1) * 512], in_=bank(bi), func=AF.Copy))
                return f
            inproj_block(0 * 1024 + c * 256, epi_copy(0))
            inproj_block(1 * 1024 + c * 256, epi_copy(2))

            def epi_u(ci, tt, bi, lp):
                P.wait("dve", lp, P.last["act"])
                return P.op("dve", lambda e: e.tensor_tensor(
                    out=stg[:, 4 + ci, 1 + tt * 512:1 + (tt + 1) * 512], in0=stg[:, 2 + ci, 1 + tt * 512:1 + (tt + 1) * 512],
                    in1=bank(bi), op=ALU.mult))
            inproj_block(2 * 1024 + c * 256, epi_u)
            for ci in range(2):
                ch = c * 2 + ci
                u = stg[:, 4 + ci]
                tmp = stg[:, 2 + ci, 1:2049]
                P.wait("act", P.last["dve"])
                t1 = P.op("act", lambda e, u=u, tmp=tmp, ch=ch: e.activation(
                    out=tmp, in_=u[:, 1:2049], func=AF.Identity, scale=cs[:, li, 1, ch:ch + 1]))
                P.wait("dve", t1)
                t2 = P.op("dve", lambda e, u=u, tmp=tmp, ch=ch: e.scalar_tensor_tensor(
                    out=tmp, in0=u[:, 0:2048], scalar=cs[:, li, 0, ch:ch + 1], in1=tmp, op0=ALU.mult, op1=ALU.add))
                P.wait("dve", t2)
                t3 = P.op("dve", lambda e, u=u, tmp=tmp, ch=ch: e.scalar_tensor_tensor(
                    out=tmp, in0=u[:, 2:2050], scalar=cs[:, li, 2, ch:ch + 1], in1=tmp, op0=ALU.mult, op1=ALU.add))
                P.wait("pool", t3, yst_free[ci])
                t4 = P.op("pool", lambda e, ci=ci, tmp=tmp: e.tensor_tensor(
                    out=yst[:, ci], in0=tmp, in1=stg[:, ci, 1:2049], op=ALU.mult))
                P.wait("sp", t4)
                yst_free[ci] = P.dma("sp", lambda e, ci=ci, ch=ch: e.dma_start(
                    out=ycat[seq, ch * 128:(ch + 1) * 128, :], in_=yst[:, ci]), ds_y[ci])
                ycat_tok[0] = yst_free[ci]
        last_conv = [yst_free[0], yst_free[1]]

        P.barrier(extra=last_conv)
        scale = 1.0 / math.sqrt(128.0)
        scb_free = [None] * 3
        pb_free = [None] * 3
        nd_free = [None, None]
        yat_free = [None]
        sci = [0]
        for hp in range(4):
            def epi_rope(dst):
                def f(ci, tt, bi, lp):
                    sl = slice(tt * 512, (tt + 1) * 512)
                    P.wait("act", lp, P.last["pe"] if tt == 0 else None)
                    ta = P.op("act", lambda e: e.activation(out=qb[:, ci, sl], in_=bank(bi), func=AF.Copy))
                    P.wait("pe", ta, nd_free[0])
                    tsw = P.op("pe", lambda e: e.matmul(bank(4), lhsT=perm[:], rhs=qb[:, ci, sl], start=True, stop=True))
                    P.wait("dve", tsw, P.last["pool"])
                    t2_ = P.op("dve", lambda e: e.tensor_tensor(out=r2, in0=bank(4), in1=sinb[:, sl], op=ALU.mult))
                    nd_free[0] = t2_
                    P.wait("pool", ta, P.last["pool"])
                    t1_ = P.op("pool", lambda e: e.tensor_tensor(out=r1, in0=qb[:, ci, sl], in1=cosb[:, sl], op=ALU.mult))
                    P.wait("pool", t1_, t2_)
                    return P.op("pool", lambda e: e.tensor_tensor(out=dst[:, ci, sl], in0=r1, in1=r2, op=ALU.add))
                return f
            inproj_block(3 * 1024 + hp * 256, epi_rope(qT))
            inproj_block(4 * 1024 + hp * 256, epi_rope(kT))

            def epi_v(ci, tt, bi, lp):
                P.wait("act", lp)
                return P.op("act", lambda e: e.activation(out=vT[:, ci, tt * 512:(tt + 1) * 512], in_=bank(bi), func=AF.Copy))
            inproj_block(5 * 1024 + hp * 256, epi_v)
            for ci in range(2):
                h = hp * 2 + ci
                P.wait("pe", P.last["act"], P.last["pool"], P.last["dve"], nd_free[1])
                for li_, r in enumerate((1, 4, 16)):
                    ntile = 16 // r
                    for half in range(2):
                        tp = None
                        for jj in range(8):
                            j = half * 8 + jj
                            ph, kt = j // ntile, j % ntile
                            src = vT[:, ci, :].rearrange("p (l r) -> p r l", r=r)[:, ph, kt * 128:(kt + 1) * 128]
                            tp = P.op("pe", lambda e, src=src, jj=jj, half=half: e.transpose(
                                out=bank_bf(6 + half)[:, jj * 128:(jj + 1) * 128], in_=src, identity=ident[:]), sig=(jj == 7))
                        eng = "act" if half == 0 else "dve"
                        P.wait(eng, tp)
                        srcv = bank_bf(6 + half).rearrange("p (j c) -> p j c", j=8)
                        if eng == "act":
                            te = P.op("act", lambda e, li_=li_, half=half, srcv=srcv: e.activation(
                                out=Vl[:, li_, half * 8:half * 8 + 8, :], in_=srcv, func=AF.Copy))
                        else:
                            te = P.op("dve", lambda e, li_=li_, half=half, srcv=srcv: e.tensor_copy(
                                out=Vl[:, li_, half * 8:half * 8 + 8, :], in_=srcv))
                        P.wait("pe", te)
                allb = []
                for Qb in range(4):
                    i0 = Qb * 512
                    banks_ = []
                    segA, segB = [], []
                    col = 0
                    for (rel, n, mc, q0) in ((-1, 64, 192, 0), (0, 192, 64, 0), (1, 256, 0, 64)):
                        kt = 4 * Qb + rel
                        segA.append((0, kt, ("c", i0 + q0, n, 1), ("c", kt * 128, 1), col, n, ("c", q0, n, 1)))
                        col += n
                    col = 0
                    for (rel, n, q0) in ((2, 256, 192), (3, 192, 320), (4, 64, 448)):
                        kt = 4 * Qb + rel
                        segB.append((0, kt, ("c", i0 + q0, n, 1), ("c", kt * 128, 1), col, n, ("c", q0, n, 1)))
                        col += n
                    banks_.append((0, segA)); banks_.append((1, segB))
                    for pp in range(2):
                        seg = []
                        col = 0
                        for ph in (2 * pp, 2 * pp + 1):
                            for (rel, n, q0) in ((-1, 64, 0), (0, 128, 0), (1, 64, 64)):
                                kt = Qb + rel
                                seg.append((1, ph * 4 + kt if 0 <= kt < 4 else -1, ("c", 4 * (128 * Qb + q0) + ph, n, 4),
                                            ("c", 4 * (128 * kt) + ph, 4), col, n, ("c", 4 * q0 + ph, n, 4)))
                                col += n
                        banks_.append((2, seg))
                    seg = []
                    for ph in range(16):
                        seg.append((2, ph, ("c", 16 * (32 * Qb) + ph, 32, 16), ("c", ph, 16), ph * 32, 32, ("c", ph, 32, 16)))
                    banks_.append((3 + Qb, seg))
                    for bi_, (mi, seg) in enumerate(banks_):
                        valid = []
                        for (lay, vt, qs, ks, col, n, oc) in seg:
                            if lay == 0 and not (0 <= vt < 16):
                                continue
                            if lay == 1 and vt < 0:
                                continue
                            valid.append((lay, vt, qs, ks, col, n, oc))
                        allb.append((Qb, mi, valid, bi_ == 0, bi_ == len(banks_) - 1))

                def emit_scores(item):
                    Qb, mi, valid, isfirst, islast = item
                    sb_ = sci[0] % 3
                    sci[0] += 1
                    P.wait("pe", bk_free[sb_])
                    lp = None
                    for vi, (lay, vt, qs, ks, col, n, oc) in enumerate(valid):
                        qa = qT[:, ci, qs[1]:qs[1] + (qs[2] - 1) * qs[3] + 1:qs[3]]
                        ka = kT[:, ci, ks[1]:ks[1] + 127 * ks[2] + 1:ks[2]]
                        lp = P.op("pe", lambda e: e.matmul(
                            bank(sb_)[:, col:col + n], lhsT=ka, rhs=qa, start=True, stop=True),
                            sig=(vi == len(valid) - 1))
                    P.wait("act", lp, pb_free[sb_])
                    te = P.op("act", lambda e: e.activation(out=Pb[:, sb_], in_=bank(sb_), func=AF.Exp, scale=scale))
                    bk_free[sb_] = te
                    P.wait("pool", te)
                    tm = P.op("pool", lambda e: e.tensor_tensor(out=Pb[:, sb_], in0=Pb[:, sb_], in1=maskb[:, mi], op=ALU.mult))
                    return sb_, tm

                def emit_pv(item, sb_, tm):
                    Qb, mi, valid, isfirst, islast = item
                    pr = Qb % 2
                    nb, db = (4, 5) if pr == 0 else (6, 7)
                    if isfirst:
                        P.wait("pe", nd_free[pr])
                    P.wait("pe", tm)
                    ftok = None
                    for vi, (lay, vt, qs, ks, col, n, oc) in enumerate(valid):
                        oa = slice(oc[1], oc[1] + (oc[2] - 1) * oc[3] + 1, oc[3])
                        st_ = isfirst and vi == 0
                        P.op("pe", lambda e: e.matmul(
                            bank(nb)[:, oa], lhsT=Vl[:, lay, vt, :], rhs=Pb[:, sb_, col:col + n],
                            start=st_, stop=False, skip_group_check=True), sig=False)
                        ftok = P.op("pe", lambda e: e.matmul(
                            bank(db)[:, oa], lhsT=ones[:], rhs=Pb[:, sb_, col:col + n],
                            start=st_, stop=False, skip_group_check=True), sig=True)
                    pb_free[sb_] = ftok
                    if islast:
                        i0 = Qb * 512
                        P.wait("dve", ftok, yat_free[0] if Qb == 0 else None, P.last["dve"])
                        td = P.op("dve", lambda e: e.reciprocal(out=rden, in_=bank(db)))
                        P.wait("dve", td)
                        nd_free[pr] = P.op("dve", lambda e: e.tensor_tensor(out=yat[:, i0:i0 + 512], in0=bank(nb), in1=rden, op=ALU.mult))

                pend = None
                for item in allb + [None]:
                    cur = None
                    if item is not None:
                        cur = (item,) + emit_scores(item)
                    if pend is not None:
                        emit_pv(*pend)
                    pend = cur
                P.wait("sp", nd_free[0], nd_free[1])
                yat_free[0] = P.dma("sp", lambda e, h=h: e.dma_start(
                    out=ycat[seq, (8 + h) * 128:(9 + h) * 128, :], in_=yat), ds_ya)
                ycat_tok[0] = yat_free[0]
        return [yat_free[0]] + last_conv

    for seq in range(nseq):
        P.barrier(extra=[tz] + ctoks)
        load_x(seq)
        P.barrier()
        if mode == "ffn":
            dump(xT[:, 0, 0, :], 0, 514, P.last["dve"], P.last["act"])
            dump(xT[:, 5, 1, :], 514, 514, P.last["dve"], P.last["act"])
            dump(cw[:, 0].rearrange("p k c -> p (k c)"), 4096, 264, P.last["dve"])
            ffn(seq, 0, x_in, y_out)
            P.barrier(extra=xres_free)
            continue
        if mode == "four":
            tk = fourier_m1(seq)
            P.barrier(extra=[tk])
            m2(seq, wb_of[0], ds_cast[("of", 0)], ln_mix_g, ln_mix_b, 1, x_in, y_out, tk)
            P.barrier(extra=xres_free)
            continue
        if mode == "even":
            tks = even_m1(seq, 0)
            P.barrier(extra=tks)
            m2(seq, wb_om[0], ds_cast[("om", 0)], ln_mix_g, ln_mix_b, 0, x_in, y_out, tks[0])
            P.barrier(extra=xres_free)
            continue
        for l in range(DEPTH):
            P.new_epoch()
            rsrc = x_in if l == 0 else resid
            if l % 2 == 0:
                tks = even_m1(seq, l // 2)
                P.barrier(extra=tks)
                m2(seq, wb_om[l // 2], ds_cast[("om", l // 2)], ln_mix_g, ln_mix_b, l, rsrc, resid, tks[0])
            else:
                tk = fourier_m1(seq)
                P.barrier(extra=[tk])
                m2(seq, wb_of[l // 2], ds_cast[("of", l // 2)], ln_mix_g, ln_mix_b, l, rsrc, resid, tk)
            P.barrier(extra=xres_free)
            ffn(seq, l, resid, y_out if l == DEPTH - 1 else resid)
            P.barrier(extra=xres_free)
    P.barrier(extra=xres_free)

    block = stack.enter_context(nc.Block())

    @block.tensor
    def _(e):
        P.emit("pe", e)

    @block.scalar
    def _(e):
        P.emit("act", e)

    @block.vector
    def _(e):
        P.emit("dve", e)

    @block.gpsimd
    def _(e):
        P.emit("pool", e)

    @block.sync
    def _(e):
        P.emit("sp", e)


def _bf(a):
    return np.asarray(a, np.float32).astype(ml_dtypes.bfloat16)


def make_consts():
    c = {}
    c["c_ident"] = _bf(np.eye(128))
    pm = np.zeros((128, 128), np.float32)
    for m in range(128):
        pm[(m + 64) % 128, m] = 1.0
    c["c_perm"] = _bf(pm)
    c["c_identf"] = np.eye(128, dtype=np.float32)
    half = 64
    inv = (1.0 / (10000.0 ** (np.arange(half, dtype=np.float32) / np.float32(half)))).astype(np.float32)
    ang = (np.arange(S, dtype=np.float32)[:, None] * inv[None, :]).astype(np.float32)
    cos = np.cos(ang.astype(np.float64)).T
    sin = np.sin(ang.astype(np.float64)).T
    c["c_cos"] = np.concatenate([cos, cos], 0).astype(np.float32)
    c["c_sin"] = np.concatenate([-sin, sin], 0).astype(np.float32)
    a = np.arange(128)[:, None]
    cc = np.arange(256)[None, :]
    master = (np.abs(cc - 64 - a) <= 64).astype(np.float32)
    masks = np.zeros((7, 128, 512), np.float32)
    masks[0] = np.concatenate([master[:, 192:256], master[:, 64:256], master[:, 0:256]], 1)
    masks[1] = np.concatenate([master[:, 0:256], master[:, 0:192], master[:, 0:64]], 1)
    one = np.concatenate([master[:, 192:256], master[:, 64:192], master[:, 0:64]], 1)
    masks[2] = np.concatenate([one, one], 1)
    for Qb in range(4):
        masks[3 + Qb] = np.tile(master[:, 64 + 32 * Qb:64 + 32 * Qb + 32], (1, 16))
    c["c_mask"] = _bf(masks)
    k = np.arange(512, dtype=np.float64)
    th = 2 * np.pi * np.outer(k, k) / 512.0
    c["c_cc"] = _bf(np.cos(th) / math.sqrt(512.0))
    c["c_sc"] = _bf(np.sin(th) / math.sqrt(512.0))
    k = np.arange(S, dtype=np.float64)
    th = 2 * np.pi * ((np.outer(k, k)) % S) / S
    c["c_cs"] = _bf(np.cos(th) / math.sqrt(S))
    c["c_ns"] = _bf(-np.sin(th) / math.sqrt(S))
    return c


_NC_CACHE = {}


def kernel(x_prompt, x_sample, w_in_mix, conv_short, w_out_mix, w_out_fourier, ln_mix_g, ln_mix_b,
           w_up, conv_ffn_w, w_down, ln_ffn_g, ln_ffn_b):
    f = lambda a: np.ascontiguousarray(np.asarray(a, dtype=np.float32))
    xp, xs = f(x_prompt), f(x_sample)
    consts = make_consts()
    shared = dict(w_in_mix=f(w_in_mix), conv_short=f(conv_short), w_out_mix=f(w_out_mix),
                  w_out_fourier=f(w_out_fourier), ln_mix_g=f(ln_mix_g), ln_mix_b=f(ln_mix_b),
                  w_up=f(w_up), conv_ffn_w=f(conv_ffn_w), w_down=f(w_down), ln_ffn_g=f(ln_ffn_g),
                  ln_ffn_b=f(ln_ffn_b))
    shared.update(consts)
    in_maps = []
    for c in range(8):
        xs_c = np.ascontiguousarray(np.stack([xp[2 * c], xp[2 * c + 1], xs[c]], 0))
        m = dict(shared)
        m["x"] = xs_c
        in_maps.append(m)
    nc = build(NSEQ, "full")
    res = run_bass_kernel_spmd(nc, in_maps, core_ids=list(range(8)))
    yp = np.empty_like(xp)
    ys = np.empty_like(xs)
    for c in range(8):
        y = res.results[c]["y"]
        yp[2 * c] = y[0]
        yp[2 * c + 1] = y[1]
        ys[c] = y[2]
    return (yp, ys)
```

```python
import math
from contextlib import ExitStack
import numpy as np
import ml_dtypes
import concourse.bass as bass
import concourse.mybir as mybir
from concourse.bass_utils import run_bass_kernel_spmd

F32 = mybir.dt.float32
BF16 = mybir.dt.bfloat16
AF = mybir.ActivationFunctionType
ALU = mybir.AluOpType

D = 2048
S = 2048
DEPTH = 4
DFF = 5632
NCH_FF = DFF // 128
DIN = 6144
ALPHA = (2 * DEPTH) ** 0.25
LN_EPS = 1e-5
NSEQ = 3
ENG = ("pe", "act", "dve", "pool", "sp")


class DSem:
    def __init__(self, h):
        self.h = h
        self.count = 0


class Rec:
    def __init__(self):
        self.call = None

    def __getattr__(self, name):
        def f(*a, **k):
            self.call = (name, a, k)
            return self
        return f


def freeze(fn):
    r = Rec()
    fn(r)
    name, a, k = r.call
    return lambda e: getattr(e, name)(*a, **k)


class Plan:
    def __init__(self, nc, stack):
        self.nc = nc
        self.stack = stack
        self.q = {e: [] for e in ENG}
        self.cnt = {e: 0 for e in ENG}
        self.sem = {}
        self.waited = {}
        self.nsem = 0
        self.last = {e: None for e in ENG}
        self.new_epoch()

    def mksem(self, name):
        self.nsem += 1
        return self.stack.enter_context(self.nc.semaphore(f"{name}_{self.nsem}"))

    def dsem(self, name):
        return DSem(self.mksem(name))

    def new_epoch(self):
        for e in ("pe", "act", "dve", "pool"):
            self.sem[e] = self.mksem("e" + e)
            self.cnt[e] = 0

    def op(self, eng, fn, sig=True):
        fn = freeze(fn)
        if sig:
            self.cnt[eng] += 1
            tok = (self.sem[eng], self.cnt[eng])
            self.q[eng].append(("op", fn, self.sem[eng]))
            self.last[eng] = tok
            return tok
        self.q[eng].append(("op", fn, None))
        return None

    def wait(self, eng, *toks):
        for tok in toks:
            if tok is None:
                continue
            if isinstance(tok, (list,)):
                self.wait(eng, *tok)
                continue
            sem, val = tok
            key = (eng, id(sem))
            if self.waited.get(key, 0) >= val:
                continue
            self.waited[key] = val
            self.q[eng].append(("wait", sem, val))

    def dma(self, eng, fn, ds):
        fn = freeze(fn)
        ds.count += 16
        self.q[eng].append(("dma", fn, ds.h))
        return (ds.h, ds.count)

    def barrier(self, extra=()):
        toks = [self.last[e] for e in ("pe", "act", "dve", "pool")] + list(extra)
        for e in ENG:
            self.wait(e, *toks)

    def emit(self, eng, e):
        for kind, a, b in self.q[eng]:
            if kind == "op":
                inst = a(e)
                if b is not None:
                    inst.then_inc(b, 1)
            elif kind == "wait":
                e.wait_ge(a, b)
            else:
                a(e).then_inc(b, 16)


def build(nseq=NSEQ, mode="full"):
    nc = bass.Bass("TRN2", target_bir_lowering=False)
    stack = ExitStack()
    with stack:
        _build(nc, stack, nseq, mode)
    return nc


def _build(nc, stack, nseq, mode):
    def din(name, shape, dt=F32):
        return nc.dram_tensor(name, list(shape), dt, kind="ExternalInput").ap()

    def dscr(name, shape, dt):
        return nc.dram_tensor(name, list(shape), dt, kind="Internal").ap()

    x_in = din("x", [nseq, S, D])
    y_out = nc.dram_tensor("y", [nseq, S, D], F32, kind="ExternalOutput").ap()
    w_in = din("w_in_mix", [2, D, DIN])
    conv_short = din("conv_short", [2, 3, 1024])
    w_om = din("w_out_mix", [2, D, D])
    w_of = din("w_out_fourier", [2, D, D])
    ln_mix_g = din("ln_mix_g", [4, D]); ln_mix_b = din("ln_mix_b", [4, D])
    w_up = din("w_up", [4, D, 2 * DFF])
    conv_ffn = din("conv_ffn_w", [4, 3, 2 * DFF])
    w_dn = din("w_down", [4, DFF, D])
    ln_ffn_g = din("ln_ffn_g", [4, D]); ln_ffn_b = din("ln_ffn_b", [4, D])
    c_ident = din("c_ident", [128, 128], BF16)
    c_perm = din("c_perm", [128, 128], BF16)
    c_identf = din("c_identf", [128, 128], F32)
    c_cos = din("c_cos", [128, S], F32)
    c_sin = din("c_sin", [128, S], F32)
    c_mask = din("c_mask", [7, 128, 512], BF16)
    c_cc = din("c_cc", [512, 512], BF16)
    c_sc = din("c_sc", [512, 512], BF16)
    c_cs = din("c_cs", [S, S], BF16)
    c_ns = din("c_ns", [S, S], BF16)

    dbg = nc.dram_tensor("dbg", [128, 8192], F32, kind="ExternalOutput").ap() if mode != "full" else None
    ds_dbg = None
    dbg16 = nc.dram_tensor("dbg16", [128, 8192], BF16, kind="ExternalOutput").ap() if mode != "full" else None
    wb_in = dscr("wb_in", [2, D, DIN], BF16)
    wb_om = dscr("wb_om", [2, D, D], BF16)
    wb_of = dscr("wb_of", [2, D, D], BF16)
    wb_up = dscr("wb_up", [4, D, 2 * DFF], BF16)
    wb_dn = dscr("wb_dn", [4, DFF, D], BF16)
    resid = dscr("resid", [nseq, S, D], F32)
    ycat = dscr("ycat", [nseq, D, S], BF16)

    sb = lambda name, shape, dt: stack.enter_context(nc.sbuf_tensor(name, list(shape), dt))
    xT = sb("xT", [128, 16, 4, 514], BF16)
    ident = sb("ident", [128, 128], BF16)
    perm = sb("perm", [128, 128], BF16)
    ones = sb("ones", [128, 128], BF16)
    identf = sb("identf", [128, 128], F32)
    cw = sb("cw", [128, 4, 3, 88], F32)
    cs = sb("cs", [128, 2, 3, 8], F32)
    small = sb("small", [128, 64], F32)
    NW = 34816
    WK = sb("WK", [128, NW], F32)
    ps = [stack.enter_context(nc.psum_tensor(f"ps{i}", [128, 1024], F32)) for i in range(4)]

    def cv(off, shape, dt):
        n = int(np.prod(shape))
        if dt == F32:
            assert off % 4 == 0
            ap = WK[:, off // 4: off // 4 + n]
        else:
            ap = WK[:, off // 4: off // 4 + (n + 1) // 2].bitcast(BF16)
        if len(shape) == 1:
            return ap
        if len(shape) == 2:
            return ap.rearrange("p (a b) -> p a b", a=shape[0])
        if len(shape) == 3:
            return ap.rearrange("p (a b c) -> p a b c", a=shape[0], b=shape[1])
        raise ValueError

    def bank(i):
        return ps[i // 2][:, (i % 2) * 512:(i % 2) * 512 + 512]

    def bank_bf(i):
        return ps[i // 2][:, (i % 2) * 512:(i % 2) * 512 + 512].bitcast(BF16)

    P = Plan(nc, stack)

    acc = cv(0, [4, 2048], F32)
    xres = cv(32768, [2, 2048], F32)
    ybf = cv(49152, [2048], BF16)
    gam = cv(53248, [2048], F32)
    bet = cv(61440, [2048], F32)
    wdn = cv(69632, [3, 2, 2048], BF16)
    wup = cv(94208, [3, 16, 256], BF16)
    aT = cv(118784, [4, 2, 512], BF16)
    ctmp = cv(126976, [2, 512], F32)
    sgb = cv(131072, [4, 512], F32)
    kopb = cv(94208, [2, 16, 512], BF16)
    assert 139264 <= NW * 4
    st6 = small[:, 0:48].rearrange("p (b c s) -> p b c s", b=2, c=4)
    mv = small[:, 48:52].rearrange("p (b s) -> p b s", b=2)
    rstd = small[:, 52:54]
    nmr = small[:, 54:56]
    sdv = small[:, 56:58]

    ds_const = P.dsem("const")
    ds_cast = {}
    ds_wdn = [P.dsem("wdn") for _ in range(3)]
    ds_wup = [P.dsem("wup") for _ in range(3)]
    ds_xres = [P.dsem("xres") for _ in range(2)]
    ds_yst = [P.dsem("yst") for _ in range(2)]
    ds_gb = P.dsem("gb")
    ds_kop = [P.dsem("kop") for _ in range(2)]
    ds_misc = P.dsem("misc")
    ds_dbg = P.dsem("dbg")
    ds_blk = [P.dsem("blk") for _ in range(4)]
    ds_f = P.dsem("fst")
    ds_win = [P.dsem("win") for _ in range(3)]
    ds_y = [P.dsem("ycs") for _ in range(2)]
    ds_ya = P.dsem("yat")

    def dump(ap, col0, n, *toks):
        if dbg is None:
            return
        P.wait("sp", *toks)
        tgt = dbg16 if ap.dtype == BF16 else dbg
        t = P.dma("sp", lambda e: e.dma_start(out=tgt[:, col0:col0 + n], in_=ap), ds_dbg)
        for e_ in ENG:
            P.wait(e_, t)

    ctoks = []
    ctoks.append(P.dma("sp", lambda e: e.dma_start(out=ident[:], in_=c_ident[:, :]), ds_const))
    ctoks.append(P.dma("sp", lambda e: e.dma_start(out=perm[:], in_=c_perm[:, :]), ds_const))
    ctoks.append(P.dma("sp", lambda e: e.dma_start(out=identf[:], in_=c_identf[:, :]), ds_const))
    cwraw = cv(0, [4, 3, 128], F32)
    csraw = cv(8192, [2, 3, 128], F32)
    for l in range(4):
        ctoks.append(P.dma("sp", lambda e, l=l: e.dma_start(
            out=cwraw[0:88, l], in_=conv_ffn[l].rearrange("k (c p) -> c k p", p=128)), ds_const))
    for i in range(2):
        ctoks.append(P.dma("sp", lambda e, i=i: e.dma_start(
            out=csraw[0:8, i], in_=conv_short[i].rearrange("k (c p) -> c k p", p=128)), ds_const))
    P.wait("pe", ctoks[-1]); P.wait("pool", ctoks[-1]); P.wait("dve", ctoks[-1]); P.wait("act", ctoks[-1])
    P.op("pool", lambda e: e.memset(ones[:], 1.0))
    P.op("pool", lambda e: e.memset(xT[:, :, 0, 0:1], 0.0))
    tz = P.op("pool", lambda e: e.memset(xT[:, :, 3, 513:514], 0.0))
    tcw = None
    for l in range(4):
        for k in range(3):
            idx = l * 3 + k
            bk, col = idx // 5, (idx % 5) * 88
            tcw = P.op("pe", lambda e, l=l, k=k, bk=bk, col=col: e.transpose(
                out=bank(bk)[:, col:col + 88], in_=cwraw[0:88, l, k, :], identity=identf[0:88, 0:88]),
                sig=(idx == 11))
    P.wait("dve", tcw)
    for l in range(4):
        for k in range(3):
            idx = l * 3 + k
            bk, col = idx // 5, (idx % 5) * 88
            P.op("dve", lambda e, l=l, k=k, bk=bk, col=col: e.tensor_copy(out=cw[:, l, k, :], in_=bank(bk)[:, col:col + 88]))
    tcs = None
    for i in range(2):
        for k in range(3):
            idx = i * 3 + k
            tcs = P.op("pe", lambda e, i=i, k=k, idx=idx: e.transpose(
                out=bank(3)[:, idx * 8:idx * 8 + 8], in_=csraw[0:8, i, k, :], identity=identf[0:8, 0:8]),
                sig=(idx == 5))
    P.wait("dve", tcs)
    for i in range(2):
        for k in range(3):
            idx = i * 3 + k
            P.op("dve", lambda e, i=i, k=k, idx=idx: e.tensor_copy(out=cs[:, i, k, :], in_=bank(3)[:, idx * 8:idx * 8 + 8]))

    def cast(name, src, dst, l):
        sflat = src[l].rearrange("k (j c) -> (k j) c", c=1024)
        dflat = dst[l].rearrange("k (j c) -> (k j) c", c=1024)
        rows = sflat.shape[0]
        dsx = P.dsem("cast" + name)
        tok = None
        r = 0
        while r < rows:
            n = min(8192, rows - r)
            tok = P.dma("pool", lambda e, r=r, n=n: e.dma_start(out=dflat[r:r + n, :], in_=sflat[r:r + n, :]), dsx)
            r += n
        ds_cast[(name, l)] = tok

    for l in range(4):
        if l % 2 == 0:
            cast("in", w_in, wb_in, l // 2); cast("om", w_om, wb_om, l // 2)
        else:
            cast("of", w_of, wb_of, l // 2)
        cast("up", w_up, wb_up, l); cast("dn", w_dn, wb_dn, l)

    st = dict(wdn_i=0, wup_i=0, bankC=0, xres_i=0, chunk_i=0)
    wdn_free = [None] * 3
    wup_free = [None] * 3
    bankC_free = [None, None]
    bankD_free = [None]
    acc_free = [None] * 4
    xres_free = [None, None]
    ybf_free = [None]
    xT_ready = []
    pair_free = [None, None]
    aT_free = [None] * 4
    sg_free = [None] * 4
    ct_free = [None, None]
    gb_tok = [None]

    def xcol(t):
        return t // 512, t % 512 + 1

    def writeback(ybf_tok, t0, ffn_inplace):
        slot, c0 = xcol(t0)
        P.wait("pe", ybf_tok, bankD_free[0])
        tp = None
        for d in range(16):
            tp = P.op("pe", lambda e, d=d: e.transpose(
                out=bank_bf(6 + d // 8)[:, (d % 8) * 128:(d % 8) * 128 + 128],
                in_=ybf[:, d * 128:(d + 1) * 128], identity=ident[:]), sig=(d == 15))
        ybf_free[0] = tp
        toks = []
        for half, eng in ((0, "dve"), (1, "act")):
            src = bank_bf(6 + half).rearrange("p (d c) -> p d c", d=8)
            P.wait(eng, tp)
            if eng == "dve":
                cp = lambda e, o, i: e.tensor_copy(out=o, in_=i)
            else:
                cp = lambda e, o, i: e.activation(out=o, in_=i, func=AF.Copy)
            dsl = slice(half * 8, half * 8 + 8)
            tk = P.op(eng, lambda e, cp=cp, dsl=dsl, src=src: cp(e, xT[:, dsl, slot, c0:c0 + 128], src))
            if t0 % 512 == 0 and slot > 0:
                tk = P.op(eng, lambda e, cp=cp, dsl=dsl, src=src: cp(e, xT[:, dsl, slot - 1, 513:514], src[:, :, 0:1]))
            if t0 % 512 == 384 and slot < 3 and not ffn_inplace:
                tk = P.op(eng, lambda e, cp=cp, dsl=dsl, src=src: cp(e, xT[:, dsl, slot + 1, 0:1], src[:, :, 127:128]))
            toks.append(tk)
        bankD_free[0] = toks
        xT_ready[:] = toks
        return toks

    def load_gb(g_ap, b_ap, l):
        P.wait("sp", P.last["pool"], P.last["act"], P.last["dve"])
        P.dma("sp", lambda e: e.dma_start(out=gam, in_=g_ap[l:l + 1, :].partition_broadcast(128)), ds_gb)
        gb_tok[0] = P.dma("sp", lambda e: e.dma_start(out=bet, in_=b_ap[l:l + 1, :].partition_broadcast(128)), ds_gb)

    def d_prefetch(wsrc, cast_tok):
        slot = st["wdn_i"] % 3
        st["wdn_i"] += 1
        P.wait("sp", wdn_free[slot], cast_tok)
        ld = P.dma("sp", lambda e: e.dma_start(out=wdn[:, slot], in_=wsrc), ds_wdn[slot])
        return slot, ld

    def d_part(q, k, kops, slot, ld, kop_toks, nr=(0, 1, 2, 3)):
        P.wait("pe", ld, *kop_toks)
        last_pe = None
        s = k
        for n in nr:
            bi = st["bankC"] % 2
            st["bankC"] += 1
            P.wait("pe", bankC_free[bi])
            for g in range(2):
                last_pe = P.op("pe", lambda e, g=g: e.matmul(
                    bank(4 + bi), lhsT=kops[g][:, s * 128:(s + 1) * 128],
                    rhs=wdn[:, slot, g, n * 512:(n + 1) * 512], start=(g == 0), stop=(g == 1)), sig=(g == 1))
            P.wait("dve", last_pe)
            dst = acc[:, s, n * 512:(n + 1) * 512]
            if q == 0:
                P.wait("dve", acc_free[s])
                bankC_free[bi] = P.op("dve", lambda e: e.tensor_copy(out=dst, in_=bank(4 + bi)))
            else:
                bankC_free[bi] = P.op("dve", lambda e: e.tensor_tensor(out=dst, in0=dst, in1=bank(4 + bi), op=ALU.add))
        if k == 3 and nr[-1] == 3:
            wdn_free[slot] = last_pe
        return last_pe

    fin = {}

    def fin_chain(seq, tt, s, rsrc, rdst):
        t0 = tt * 512 + s * 128
        b = st["xres_i"] % 2
        st["xres_i"] += 1
        P.wait("sp", xres_free[b])
        ldx = P.dma("sp", lambda e: e.dma_start(out=xres[:, b], in_=rsrc[seq, t0:t0 + 128, :]), ds_xres[b])
        P.wait("dve", ldx, P.last["dve"], P.last["act"])
        P.op("dve", lambda e: e.scalar_tensor_tensor(
            out=acc[:, s], in0=xres[:, b], scalar=float(ALPHA), in1=acc[:, s], op0=ALU.mult, op1=ALU.add))
        P.wait("dve", P.last["dve"])
        for c in range(4):
            P.op("dve", lambda e, c=c: e.bn_stats(out=st6[:, b, c], in_=acc[:, s, c * 512:(c + 1) * 512]))
        P.wait("dve", P.last["dve"])
        P.op("dve", lambda e: e.bn_aggr(out=mv[:, b], in_=st6[:, b].rearrange("p c s -> p (c s)")))
        P.wait("act", P.last["dve"])
        tsd = P.op("act", lambda e: e.activation(
            out=sdv[:, b:b + 1], in_=mv[:, b, 1:2], func=AF.Sqrt, bias=float(LN_EPS), scale=1.0))
        P.wait("dve", tsd)
        P.op("dve", lambda e: e.reciprocal(out=rstd[:, b:b + 1], in_=sdv[:, b:b + 1]))
        P.wait("dve", P.last["dve"])
        tstat = P.op("dve", lambda e: e.scalar_tensor_tensor(
            out=nmr[:, b:b + 1], in0=mv[:, b, 0:1], scalar=-1.0, in1=rstd[:, b:b + 1], op0=ALU.mult, op1=ALU.mult))
        P.wait("act", tstat)
        txh = P.op("act", lambda e: e.activation(
            out=xres[:, b], in_=acc[:, s], func=AF.Identity, scale=rstd[:, b:b + 1], bias=nmr[:, b:b + 1]))
        acc_free[s] = txh
        P.wait("pool", txh, gb_tok[0])
        P.op("pool", lambda e: e.tensor_tensor(out=xres[:, b, 0:1024], in0=xres[:, b, 0:1024], in1=gam[:, 0:1024], op=ALU.mult))
        P.wait("pool", P.last["pool"])
        ty1 = P.op("pool", lambda e: e.tensor_tensor(out=xres[:, b, 0:1024], in0=xres[:, b, 0:1024], in1=bet[:, 0:1024], op=ALU.add))
        P.wait("dve", txh, gb_tok[0])
        P.op("dve", lambda e: e.tensor_tensor(out=xres[:, b, 1024:2048], in0=xres[:, b, 1024:2048], in1=gam[:, 1024:2048], op=ALU.mult))
        P.wait("dve", P.last["dve"])
        ty2 = P.op("dve", lambda e: e.tensor_tensor(out=xres[:, b, 1024:2048], in0=xres[:, b, 1024:2048], in1=bet[:, 1024:2048], op=ALU.add))
        fin[(tt, s)] = (b, [ty1, ty2], t0, seq, rdst)

    def fin_cast(tt, s):
        b, ty, t0, seq, rdst = fin[(tt, s)]
        P.wait("act", ty, ybf_free[0])
        tyb = P.op("act", lambda e: e.activation(out=ybf, in_=xres[:, b], func=AF.Copy))
        P.wait("sp", ty, tyb)
        xres_free[b] = P.dma("sp", lambda e: e.dma_start(out=rdst[seq, t0:t0 + 128, :], in_=xres[:, b]), ds_yst[b])
        fin[(tt, s)] = (tyb, t0)

    def fin_wb(tt, s, ffn_inplace):
        tyb, t0 = fin.pop((tt, s))
        writeback(tyb, t0, ffn_inplace)

    def g2_finish(seq, tt, rsrc, rdst, ffn_inplace):
        fin_chain(seq, tt, 0, rsrc, rdst)
        for s in range(4):
            if s + 1 < 4:
                fin_chain(seq, tt, s + 1, rsrc, rdst)
            fin_cast(tt, s)
            fin_wb(tt, s, ffn_inplace)

    def load_x(seq):
        for s in range(16):
            t0 = s * 128
            b = st["xres_i"] % 2
            st["xres_i"] += 1
            P.wait("sp", xres_free[b])
            ldx = P.dma("sp", lambda e, b=b, t0=t0: e.dma_start(out=xres[:, b], in_=x_in[seq, t0:t0 + 128, :]), ds_xres[b])
            P.wait("act", ldx, ybf_free[0])
            tyb = P.op("act", lambda e, b=b: e.activation(out=ybf, in_=xres[:, b], func=AF.Copy))
            xres_free[b] = tyb
            writeback(tyb, t0, False)

    def ffn(seq, l, rsrc, rdst):
        load_gb(ln_ffn_g, ln_ffn_b, l)
        wu = wb_up[l].rearrange("(kc p) n -> p kc n", p=128)
        wd = wb_dn[l].rearrange("(j p) n -> p j n", p=128)
        LAG = 3
        items = [(tt, b, half) for tt in range(4) for b in range(22) for half in range(2)]
        wl = {}

        def up_prefetch(i):
            if i >= len(items) or i in wl:
                return
            tt_, b_, half_ = items[i]
            col0 = half_ * DFF + b_ * 256
            slot = st["wup_i"] % 3
            st["wup_i"] += 1
            P.wait("sp", wup_free[slot], ds_cast[("up", l)])
            wl[i] = (slot, P.dma("sp", lambda e: e.dma_start(out=wup[:, slot], in_=wu[:, :, col0:col0 + 256]), ds_wup[slot]))

        up_prefetch(0)
        up_prefetch(1)
        it = 0
        for tt in range(4):
            u_done = None
            grp_toks = {}
            dgrp = {}

            def dpre(q):
                if 0 <= q < 22 and q not in dgrp:
                    dgrp[q] = d_prefetch(wd[:, 2 * q:2 * q + 2, :], ds_cast[("dn", l)])

            def dpart(q, k, nr=(0, 1, 2, 3)):
                gi = q % 4
                slot, ld = dgrp[q]
                tk = d_part(q, k, [aT[:, gi, 0], aT[:, gi, 1]], slot, ld,
                            [grp_toks[(q, 0, "a")], grp_toks[(q, 1, "a")]], nr)
                if k == 3 and nr[-1] == 3:
                    aT_free[gi] = tk

            for b in range(22):
                dpre(b - LAG)
                dpre(b - LAG + 1)
                for half in range(2):
                    slot, ld = wl.pop(it)
                    up_prefetch(it + 2)
                    it += 1
                    if seq == 0 and tt == 0 and b == 0 and half == 0:
                        dump(wup[:, slot].rearrange("p k c -> p (k c)"), 2048, 4096, ld)
                    P.wait("pe", ld)
                    for ci in range(2):
                        ch = half * NCH_FF + b * 2 + ci
                        pi = st["chunk_i"] % 2
                        st["chunk_i"] += 1
                        P.wait("pe", pair_free[pi])
                        lp = None
                        for kc in range(16):
                            for hh in range(2):
                                lp = P.op("pe", lambda e: e.matmul(
                                    bank(2 * pi + hh)[:, 0:258], lhsT=wup[:, slot, kc, ci * 128:(ci + 1) * 128],
                                    rhs=xT[:, kc, tt, hh * 256:hh * 256 + 258], start=(kc == 0), stop=(kc == 15)),
                                    sig=(kc == 15 and hh == 1))
                            if kc == 7 and b >= LAG:
                                dpart(b - LAG, half * 2 + ci, (0, 1))
                        u_done = lp
                        if b >= LAG:
                            dpart(b - LAG, half * 2 + ci, (2, 3))
                        pv = ps[pi].rearrange("p (h c) -> p h c", h=2)
                        cti = pi
                        ct = ctmp[:, cti].rearrange("p (h c) -> p h c", h=2)
                        P.wait("act", lp, ct_free[cti])
                        ta = P.op("act", lambda e: e.activation(
                            out=ct, in_=pv[:, :, 1:257], func=AF.Identity, scale=cw[:, l, 1, ch:ch + 1]))
                        P.wait("dve", ta)
                        tb = P.op("dve", lambda e: e.scalar_tensor_tensor(
                            out=ct, in0=pv[:, :, 0:256], scalar=cw[:, l, 0, ch:ch + 1], in1=ct, op0=ALU.mult, op1=ALU.add))
                        P.wait("dve", tb)
                        tc = P.op("dve", lambda e: e.scalar_tensor_tensor(
                            out=ct, in0=pv[:, :, 2:258], scalar=cw[:, l, 2, ch:ch + 1], in1=ct, op0=ALU.mult, op1=ALU.add))
                        pair_free[pi] = tc
                        if seq == 0 and tt == 0 and b == 0 and ci == 0:
                            dump(ctmp[:, cti], 0 if half == 0 else 1024, 512, tc)
                        sgi = (b % 2) * 2 + ci
                        if half == 0:
                            P.wait("act", tc, sg_free[sgi])
                            ct_free[cti] = P.op("act", lambda e: e.activation(
                                out=sgb[:, sgi], in_=ctmp[:, cti], func=AF.Silu))
                            grp_toks[(b, ci, "sg")] = ct_free[cti]
                        else:
                            gi = b % 4
                            P.wait("pool", tc, grp_toks[(b, ci, "sg")], aT_free[gi])
                            tm = P.op("pool", lambda e: e.tensor_tensor(
                                out=aT[:, gi, ci], in0=sgb[:, sgi], in1=ctmp[:, cti], op=ALU.mult))
                            ct_free[cti] = tm
                            sg_free[sgi] = tm
                            grp_toks[(b, ci, "a")] = tm
                    wup_free[slot] = u_done
                if tt >= 1:
                    pt = tt - 1
                    if b == 0:
                        fin_chain(seq, pt, 0, rsrc, rdst); fin_cast(pt, 0); fin_chain(seq, pt, 1, rsrc, rdst)
                    elif b == 1:
                        fin_wb(pt, 0, True); fin_cast(pt, 1); fin_chain(seq, pt, 2, rsrc, rdst)
                    elif b == 2:
                        fin_wb(pt, 1, True); fin_cast(pt, 2); fin_chain(seq, pt, 3, rsrc, rdst)
                    elif b == 3:
                        fin_wb(pt, 2, True); fin_cast(pt, 3)
                    elif b == 4:
                        fin_wb(pt, 3, True)
            for q in range(22 - LAG, 22):
                dpre(q)
                for k in range(4):
                    dpart(q, k)
            if tt >= 1:
                P.wait("pool", u_done, *xT_ready)
                P.op("pool", lambda e: e.tensor_copy(out=xT[:, :, tt, 0:1], in_=xT[:, :, tt - 1, 512:513]))
            if seq == 0 and tt == 0:
                dump(acc[:, 0], 2048, 2048, P.last["dve"])
            if tt == 3:
                g2_finish(seq, tt, rsrc, rdst, True)

    def m2(seq, wsrc_l, cast_tok, g_ap, b_ap, l, rsrc, rdst, ycat_tok):
        load_gb(g_ap, b_ap, l)
        wv = wsrc_l.rearrange("(kc p) n -> p kc n", p=128)
        yc = ycat[seq].rearrange("(kc p) t -> p kc t", p=128)
        wblk = cv(69632, [2, 16, 512], BF16)
        kop2 = cv(102400, [2, 16, 512], BF16)
        kop_free = [None, None]
        wb_free = [None, None]
        kl = {}
        wq = {}
        wi_ = [0]

        def kpre(tt):
            if tt < 4 and tt not in kl:
                kb = tt % 2
                P.wait("sp", kop_free[kb], ycat_tok)
                kl[tt] = P.dma("sp", lambda e: e.dma_start(out=kop2[:, kb], in_=yc[:, :, tt * 512:(tt + 1) * 512]), ds_kop[kb])

        def wpre(i):
            if i < 16 and i not in wq:
                n = i % 4
                slot = wi_[0] % 2
                wi_[0] += 1
                P.wait("sp", wb_free[slot], cast_tok)
                wq[i] = (slot, P.dma("sp", lambda e: e.dma_start(out=wblk[:, slot], in_=wv[:, :, n * 512:(n + 1) * 512]), ds_wdn[slot]))

        kpre(0)
        wpre(0)
        ev = 0
        for tt in range(4):
            kb = tt % 2
            kpre(tt + 1)
            lp = None
            for n in range(4):
                if n == 1 and tt >= 1:
                    fin_cast(tt - 1, 2); fin_wb(tt - 1, 2, False); fin_cast(tt - 1, 3); fin_wb(tt - 1, 3, False)
                slot, ld = wq[tt * 4 + n]
                wpre(tt * 4 + n + 1)
                P.wait("pe", ld, kl[tt])
                for s in range(4):
                    bi = st["bankC"] % 2
                    st["bankC"] += 1
                    P.wait("pe", bankC_free[bi])
                    for kc in range(16):
                        lp = P.op("pe", lambda e: e.matmul(
                            bank(4 + bi), lhsT=kop2[:, kb, kc, s * 128:(s + 1) * 128], rhs=wblk[:, slot, kc, :],
                            start=(kc == 0), stop=(kc == 15)), sig=(kc == 15))
                    eng = "dve" if ev % 2 == 0 else "act"
                    ev += 1
                    dst = acc[:, s, n * 512:(n + 1) * 512]
                    P.wait(eng, lp, acc_free[s])
                    if eng == "dve":
                        bankC_free[bi] = P.op("dve", lambda e: e.tensor_copy(out=dst, in_=bank(4 + bi)))
                    else:
                        bankC_free[bi] = P.op("act", lambda e: e.activation(out=dst, in_=bank(4 + bi), func=AF.Copy))
                wb_free[slot] = lp
            kop_free[kb] = lp
            fin_chain(seq, tt, 0, rsrc, rdst); fin_chain(seq, tt, 1, rsrc, rdst)
            fin_cast(tt, 0); fin_wb(tt, 0, False)
            fin_chain(seq, tt, 2, rsrc, rdst)
            fin_cast(tt, 1); fin_wb(tt, 1, False)
            fin_chain(seq, tt, 3, rsrc, rdst)
            if tt == 3:
                fin_cast(tt, 2); fin_wb(tt, 2, False); fin_cast(tt, 3); fin_wb(tt, 3, False)

    def fourier_m1(seq):
        Yc = cv(0, [16, 512], BF16)
        Ys = cv(16384, [16, 512], BF16)
        CC = cv(32768, [4, 512], BF16)
        SC = cv(36864, [4, 512], BF16)
        blk = cv(40960, [4, 16, 256], BF16)
        fst = cv(73728, [4, 2048], BF16)
        blk_free = [None] * 4
        P.wait("sp", P.last["pe"])
        P.dma("sp", lambda e: e.dma_start(out=CC, in_=c_cc.rearrange("(k p) n -> p k n", p=128)), ds_misc)
        tcc = P.dma("sp", lambda e: e.dma_start(out=SC, in_=c_sc.rearrange("(k p) n -> p k n", p=128)), ds_misc)
        csv = c_cs.rearrange("(k p) n -> p k n", p=128)
        nsv = c_ns.rearrange("(k p) n -> p k n", p=128)
        bC = [None, None]
        y_free = [None]
        fst_free = [None]
        cnt = 0
        for g in range(4):
            P.wait("pe", tcc)
            ev = []
            for stl in range(16):
                slot, c0 = xcol(stl * 128)
                for which, (M, Yb, eng) in enumerate(((CC, Yc, "act"), (SC, Ys, "dve"))):
                    bi = which
                    P.wait("pe", bC[bi], y_free[0] if stl == 0 else None)
                    lp = None
                    for kc in range(4):
                        lp = P.op("pe", lambda e, kc=kc, slot=slot, c0=c0, M=M, bi=bi: e.matmul(
                            bank(4 + bi), lhsT=xT[:, 4 * g + kc, slot, c0:c0 + 128], rhs=M[:, kc, :],
                            start=(kc == 0), stop=(kc == 3)), sig=(kc == 3))
                    P.wait(eng, lp)
                    if eng == "act":
                        bC[bi] = P.op("act", lambda e, Yb=Yb, stl=stl, bi=bi: e.activation(out=Yb[:, stl], in_=bank(4 + bi), func=AF.Copy))
                    else:
                        bC[bi] = P.op("dve", lambda e, Yb=Yb, stl=stl, bi=bi: e.tensor_copy(out=Yb[:, stl], in_=bank(4 + bi)))
                    ev.append(bC[bi])
            P.wait("pe", ev[-1], ev[-2])
            P.wait("act", fst_free[0])
            lastpe = None
            for sp_ in range(8):
                bs = sp_ % 2
                P.wait("sp", blk_free[bs], blk_free[2 + bs])
                l1 = P.dma("sp", lambda e, bs=bs, sp_=sp_: e.dma_start(out=blk[:, bs], in_=csv[:, :, sp_ * 256:(sp_ + 1) * 256]), ds_blk[bs])
                l2 = P.dma("sp", lambda e, bs=bs, sp_=sp_: e.dma_start(out=blk[:, 2 + bs], in_=nsv[:, :, sp_ * 256:(sp_ + 1) * 256]), ds_blk[2 + bs])
                P.wait("pe", l1, l2)
                for cp_ in range(4):
                    bi = cnt % 2
                    cnt += 1
                    P.wait("pe", pair_free[bi])
                    lp = None
                    for which, Yb in enumerate((Yc, Ys)):
                        for sc in range(16):
                            lp = P.op("pe", lambda e, Yb=Yb, sc=sc, cp_=cp_, bs=bs, which=which, bi=bi: e.matmul(
                                bank(bi)[:, 0:256], lhsT=Yb[:, sc, cp_ * 128:(cp_ + 1) * 128], rhs=blk[:, 2 * which + bs, sc, :],
                                start=(which == 0 and sc == 0), stop=(which == 1 and sc == 15)),
                                sig=(which == 1 and sc == 15))
                    lastpe = lp
                    P.wait("act", lp)
                    pair_free[bi] = P.op("act", lambda e, cp_=cp_, sp_=sp_, bi=bi: e.activation(
                        out=fst[:, cp_, sp_ * 256:(sp_ + 1) * 256], in_=bank(bi)[:, 0:256], func=AF.Copy))
                blk_free[bs] = lastpe
                blk_free[2 + bs] = lastpe
            y_free[0] = lastpe
            P.wait("sp", P.last["act"])
            for cp_ in range(4):
                ch = 4 * g + cp_
                fst_free[0] = P.dma("sp", lambda e, cp_=cp_, ch=ch: e.dma_start(
                    out=ycat[seq, ch * 128:(ch + 1) * 128, :], in_=fst[:, cp_]), ds_f)
        return fst_free[0]

    def even_m1(seq, li):
        wi = wb_in[li].rearrange("(kc p) n -> p kc n", p=128)
        winb = cv(0, [3, 16, 256], BF16)
        cosb = cv(24576, [2048], F32)
        sinb = cv(32768, [2048], F32)
        maskb = cv(40960, [7, 512], BF16)
        stg = cv(49152, [6, 2050], F32)
        yst = cv(98352 + 16, [2, 2048], BF16)
        qb = cv(49152, [2, 2048], BF16)
        r1 = cv(57344, [512], F32)
        r2 = cv(59392, [512], F32)
        qT = cv(61440, [2, 2048], BF16)
        kT = cv(69632, [2, 2048], BF16)
        vT = cv(77824, [2, 2048], BF16)
        Vl = cv(86016, [3, 16, 128], BF16)
        Pb = cv(106560, [3, 512], BF16)
        rden = cv(109632, [512], F32)
        yat = cv(111680, [2048], BF16)
        win_free = [None] * 3
        wi_i = [0]
        P.wait("sp", P.last["pe"], P.last["act"], P.last["dve"], P.last["pool"])
        P.dma("sp", lambda e: e.dma_start(out=cosb, in_=c_cos[:, :]), ds_misc)
        P.dma("sp", lambda e: e.dma_start(out=sinb, in_=c_sin[:, :]), ds_misc)
        tconst = P.dma("sp", lambda e: e.dma_start(out=maskb, in_=c_mask.rearrange("m p c -> p m c")), ds_misc)
        for e_ in ("pe", "act", "dve", "pool"):
            P.wait(e_, tconst)
        for i in (4, 5):
            P.op("pool", lambda e, i=i: e.memset(stg[:, i, 0:1], 0.0))
            P.op("pool", lambda e, i=i: e.memset(stg[:, i, 2049:2050], 0.0))
        tpad = P.last["pool"]
        bk_free = [None] * 4
        bki = [0]
        ycat_tok = [None]

        col_order = []
        for c_ in range(4):
            col_order += [0 * 1024 + c_ * 256, 1 * 1024 + c_ * 256, 2 * 1024 + c_ * 256]
        for hp_ in range(4):
            col_order += [3 * 1024 + hp_ * 256, 4 * 1024 + hp_ * 256, 5 * 1024 + hp_ * 256]
        wl_in = {}
        blk_i = [0]

        def in_prefetch(i):
            if i >= len(col_order) or i in wl_in:
                return
            c0_ = col_order[i]
            slot_ = i % 3
            P.wait("sp", win_free[slot_], ds_cast[("in", li)])
            wl_in[i] = (slot_, P.dma("sp", lambda e: e.dma_start(out=winb[:, slot_], in_=wi[:, :, c0_:c0_ + 256]), ds_win[slot_]))

        def inproj_block(col0, epi):
            i = blk_i[0]
            blk_i[0] += 1
            assert col_order[i] == col0
            in_prefetch(i)
            slot, ld = wl_in.pop(i)
            in_prefetch(i + 1)
            P.wait("pe", ld)
            lp = None
            for ci in range(2):
                for tt in range(4):
                    bi = bki[0] % 4
                    bki[0] += 1
                    P.wait("pe", bk_free[bi])
                    for kc in range(16):
                        lp = P.op("pe", lambda e, kc=kc, ci=ci, tt=tt, bi=bi: e.matmul(
                            bank(bi), lhsT=winb[:, slot, kc, ci * 128:(ci + 1) * 128], rhs=xT[:, kc, tt, 1:513],
                            start=(kc == 0), stop=(kc == 15)), sig=(kc == 15))
                    bk_free[bi] = epi(ci, tt, bi, lp)
            win_free[slot] = lp

        yst_free = [None, None]
        for c in range(4):
            def epi_copy(base):
                def f(ci, tt, bi, lp):
                    P.wait("act", lp, yst_free[ci] if tt == 0 else None, tpad)
                    return P.op("act", lambda e: e.activation(
                        out=stg[:, base + ci, 1 + tt * 512:1 + (tt + 1) * 512], in_=bank(bi), func=AF.Copy))
                return f
            inproj_block(0 * 1024 + c * 256, epi_copy(0))
            inproj_block(1 * 1024 + c * 256, epi_copy(2))

            def epi_u(ci, tt, bi, lp):
                P.wait("dve", lp, P.last["act"])
                return P.op("dve", lambda e: e.tensor_tensor(
                    out=stg[:, 4 + ci, 1 + tt * 512:1 + (tt + 1) * 512], in0=stg[:, 2 + ci, 1 + tt * 512:1 + (tt + 1) * 512],
                    in1=bank(bi), op=ALU.mult))
            inproj_block(2 * 1024 + c * 256, epi_u)
            for ci in range(2):
                ch = c * 2 + ci
                u = stg[:, 4 + ci]
                tmp = stg[:, 2 + ci, 1:2049]
                P.wait("act", P.last["dve"])
                t1 = P.op("act", lambda e, u=u, tmp=tmp, ch=ch: e.activation(
                    out=tmp, in_=u[:, 1:2049], func=AF.Identity, scale=cs[:, li, 1, ch:ch + 1]))
                P.wait("dve", t1)
                t2 = P.op("dve", lambda e, u=u, tmp=tmp, ch=ch: e.scalar_tensor_tensor(
                    out=tmp, in0=u[:, 0:2048], scalar=cs[:, li, 0, ch:ch + 1], in1=tmp, op0=ALU.mult, op1=ALU.add))
                P.wait("dve", t2)
                t3 = P.op("dve", lambda e, u=u, tmp=tmp, ch=ch: e.scalar_tensor_tensor(
                    out=tmp, in0=u[:, 2:2050], scalar=cs[:, li, 2, ch:ch + 1], in1=tmp, op0=ALU.mult, op1=ALU.add))
                P.wait("pool", t3, yst_free[ci])
                t4 = P.op("pool", lambda e, ci=ci, tmp=tmp: e.tensor_tensor(
                    out=yst[:, ci], in0=tmp, in1=stg[:, ci, 1:2049], op=ALU.mult))
                P.wait("sp", t4)
                yst_free[ci] = P.dma("sp", lambda e, ci=ci, ch=ch: e.dma_start(
                    out=ycat[seq, ch * 128:(ch + 1) * 128, :], in_=yst[:, ci]), ds_y[ci])
                ycat_tok[0] = yst_free[ci]
        last_conv = [yst_free[0], yst_free[1]]

        P.barrier(extra=last_conv)
        scale = 1.0 / math.sqrt(128.0)
        scb_free = [None] * 3
        pb_free = [None] * 3
        nd_free = [None, None]
        yat_free = [None]
        sci = [0]
        for hp in range(4):
            def epi_rope(dst):
                def f(ci, tt, bi, lp):
                    sl = slice(tt * 512, (tt + 1) * 512)
                    P.wait("act", lp, P.last["pe"] if tt == 0 else None)
                    ta = P.op("act", lambda e: e.activation(out=qb[:, ci, sl], in_=bank(bi), func=AF.Copy))
                    P.wait("pe", ta, nd_free[0])
                    tsw = P.op("pe", lambda e: e.matmul(bank(4), lhsT=perm[:], rhs=qb[:, ci, sl], start=True, stop=True))
                    P.wait("dve", tsw, P.last["pool"])
                    t2_ = P.op("dve", lambda e: e.tensor_tensor(out=r2, in0=bank(4), in1=sinb[:, sl], op=ALU.mult))
                    nd_free[0] = t2_
                    P.wait("pool", ta, P.last["pool"])
                    t1_ = P.op("pool", lambda e: e.tensor_tensor(out=r1, in0=qb[:, ci, sl], in1=cosb[:, sl], op=ALU.mult))
                    P.wait("pool", t1_, t2_)
                    return P.op("pool", lambda e: e.tensor_tensor(out=dst[:, ci, sl], in0=r1, in1=r2, op=ALU.add))
                return f
            inproj_block(3 * 1024 + hp * 256, epi_rope(qT))
            inproj_block(4 * 1024 + hp * 256, epi_rope(kT))

            def epi_v(ci, tt, bi, lp):
                P.wait("act", lp)
                return P.op("act", lambda e: e.activation(out=vT[:, ci, tt * 512:(tt + 1) * 512], in_=bank(bi), func=AF.Copy))
            inproj_block(5 * 1024 + hp * 256, epi_v)
            for ci in range(2):
                h = hp * 2 + ci
                P.wait("pe", P.last["act"], P.last["pool"], P.last["dve"], nd_free[1])
                for li_, r in enumerate((1, 4, 16)):
                    ntile = 16 // r
                    for half in range(2):
                        tp = None
                        for jj in range(8):
                            j = half * 8 + jj
                            ph, kt = j // ntile, j % ntile
                            src = vT[:, ci, :].rearrange("p (l r) -> p r l", r=r)[:, ph, kt * 128:(kt + 1) * 128]
                            tp = P.op("pe", lambda e, src=src, jj=jj, half=half: e.transpose(
                                out=bank_bf(6 + half)[:, jj * 128:(jj + 1) * 128], in_=src, identity=ident[:]), sig=(jj == 7))
                        eng = "act" if half == 0 else "dve"
                        P.wait(eng, tp)
                        srcv = bank_bf(6 + half).rearrange("p (j c) -> p j c", j=8)
                        if eng == "act":
                            te = P.op("act", lambda e, li_=li_, half=half, srcv=srcv: e.activation(
                                out=Vl[:, li_, half * 8:half * 8 + 8, :], in_=srcv, func=AF.Copy))
                        else:
                            te = P.op("dve", lambda e, li_=li_, half=half, srcv=srcv: e.tensor_copy(
                                out=Vl[:, li_, half * 8:half * 8 + 8, :], in_=srcv))
                        P.wait("pe", te)
                allb = []
                for Qb in range(4):
                    i0 = Qb * 512
                    banks_ = []
                    segA, segB = [], []
                    col = 0
                    for (rel, n, mc, q0) in ((-1, 64, 192, 0), (0, 192, 64, 0), (1, 256, 0, 64)):
                        kt = 4 * Qb + rel
                        segA.append((0, kt, ("c", i0 + q0, n, 1), ("c", kt * 128, 1), col, n, ("c", q0, n, 1)))
                        col += n
                    col = 0
                    for (rel, n, q0) in ((2, 256, 192), (3, 192, 320), (4, 64, 448)):
                        kt = 4 * Qb + rel
                        segB.append((0, kt, ("c", i0 + q0, n, 1), ("c", kt * 128, 1), col, n, ("c", q0, n, 1)))
                        col += n
                    banks_.append((0, segA)); banks_.append((1, segB))
                    for pp in range(2):
                        seg = []
                        col = 0
                        for ph in (2 * pp, 2 * pp + 1):
                            for (rel, n, q0) in ((-1, 64, 0), (0, 128, 0), (1, 64, 64)):
                                kt = Qb + rel
                                seg.append((1, ph * 4 + kt if 0 <= kt < 4 else -1, ("c", 4 * (128 * Qb + q0) + ph, n, 4),
                                            ("c", 4 * (128 * kt) + ph, 4), col, n, ("c", 4 * q0 + ph, n, 4)))
                                col += n
                        banks_.append((2, seg))
                    seg = []
                    for ph in range(16):
                        seg.append((2, ph, ("c", 16 * (32 * Qb) + ph, 32, 16), ("c", ph, 16), ph * 32, 32, ("c", ph, 32, 16)))
                    banks_.append((3 + Qb, seg))
                    for bi_, (mi, seg) in enumerate(banks_):
                        valid = []
                        for (lay, vt, qs, ks, col, n, oc) in seg:
                            if lay == 0 and not (0 <= vt < 16):
                                continue
                            if lay == 1 and vt < 0:
                                continue
                            valid.append((lay, vt, qs, ks, col, n, oc))
                        allb.append((Qb, mi, valid, bi_ == 0, bi_ == len(banks_) - 1))

                def emit_scores(item):
                    Qb, mi, valid, isfirst, islast = item
                    sb_ = sci[0] % 3
                    sci[0] += 1
                    P.wait("pe", bk_free[sb_])
                    lp = None
                    for vi, (lay, vt, qs, ks, col, n, oc) in enumerate(valid):
                        qa = qT[:, ci, qs[1]:qs[1] + (qs[2] - 1) * qs[3] + 1:qs[3]]
                        ka = kT[:, ci, ks[1]:ks[1] + 127 * ks[2] + 1:ks[2]]
                        lp = P.op("pe", lambda e: e.matmul(
                            bank(sb_)[:, col:col + n], lhsT=ka, rhs=qa, start=True, stop=True),
                            sig=(vi == len(valid) - 1))
                    P.wait("act", lp, pb_free[sb_])
                    te = P.op("act", lambda e: e.activation(out=Pb[:, sb_], in_=bank(sb_), func=AF.Exp, scale=scale))
                    bk_free[sb_] = te
                    P.wait("pool", te)
                    tm = P.op("pool", lambda e: e.tensor_tensor(out=Pb[:, sb_], in0=Pb[:, sb_], in1=maskb[:, mi], op=ALU.mult))
                    return sb_, tm

                def emit_pv(item, sb_, tm):
                    Qb, mi, valid, isfirst, islast = item
                    pr = Qb % 2
                    nb, db = (4, 5) if pr == 0 else (6, 7)
                    if isfirst:
                        P.wait("pe", nd_free[pr])
                    P.wait("pe", tm)
                    ftok = None
                    for vi, (lay, vt, qs, ks, col, n, oc) in enumerate(valid):
                        oa = slice(oc[1], oc[1] + (oc[2] - 1) * oc[3] + 1, oc[3])
                        st_ = isfirst and vi == 0
                        P.op("pe", lambda e: e.matmul(
                            bank(nb)[:, oa], lhsT=Vl[:, lay, vt, :], rhs=Pb[:, sb_, col:col + n],
                            start=st_, stop=False, skip_group_check=True), sig=False)
                        ftok = P.op("pe", lambda e: e.matmul(
                            bank(db)[:, oa], lhsT=ones[:], rhs=Pb[:, sb_, col:col + n],
                            start=st_, stop=False, skip_group_check=True), sig=True)
                    pb_free[sb_] = ftok
                    if islast:
                        i0 = Qb * 512
                        P.wait("dve", ftok, yat_free[0] if Qb == 0 else None, P.last["dve"])
                        td = P.op("dve", lambda e: e.reciprocal(out=rden, in_=bank(db)))
                        P.wait("dve", td)
                        nd_free[pr] = P.op("dve", lambda e: e.tensor_tensor(out=yat[:, i0:i0 + 512], in0=bank(nb), in1=rden, op=ALU.mult))

                pend = None
                for item in allb + [None]:
                    cur = None
                    if item is not None:
                        cur = (item,) + emit_scores(item)
                    if pend is not None:
                        emit_pv(*pend)
                    pend = cur
                P.wait("sp", nd_free[0], nd_free[1])
                yat_free[0] = P.dma("sp", lambda e, h=h: e.dma_start(
                    out=ycat[seq, (8 + h) * 128:(9 + h) * 128, :], in_=yat), ds_ya)
                ycat_tok[0] = yat_free[0]
        return [yat_free[0]] + last_conv

    for seq in range(nseq):
        P.barrier(extra=[tz] + ctoks)
        load_x(seq)
        P.barrier()
        if mode == "ffn":
            dump(xT[:, 0, 0, :], 0, 514, P.last["dve"], P.last["act"])
            dump(xT[:, 5, 1, :], 514, 514, P.last["dve"], P.last["act"])
            dump(cw[:, 0].rearrange("p k c -> p (k c)"), 4096, 264, P.last["dve"])
            ffn(seq, 0, x_in, y_out)
            P.barrier(extra=xres_free)
            continue
        if mode == "four":
            tk = fourier_m1(seq)
            P.barrier(extra=[tk])
            m2(seq, wb_of[0], ds_cast[("of", 0)], ln_mix_g, ln_mix_b, 1, x_in, y_out, tk)
            P.barrier(extra=xres_free)
            continue
        if mode == "even":
            tks = even_m1(seq, 0)
            P.barrier(extra=tks)
            m2(seq, wb_om[0], ds_cast[("om", 0)], ln_mix_g, ln_mix_b, 0, x_in, y_out, tks[0])
            P.barrier(extra=xres_free)
            continue
        for l in range(DEPTH):
            P.new_epoch()
            rsrc = x_in if l == 0 else resid
            if l % 2 == 0:
                tks = even_m1(seq, l // 2)
                P.barrier(extra=tks)
                m2(seq, wb_om[l // 2], ds_cast[("om", l // 2)], ln_mix_g, ln_mix_b, l, rsrc, resid, tks[0])
            else:
                tk = fourier_m1(seq)
                P.barrier(extra=[tk])
                m2(seq, wb_of[l // 2], ds_cast[("of", l // 2)], ln_mix_g, ln_mix_b, l, rsrc, resid, tk)
            P.barrier(extra=xres_free)
            ffn(seq, l, resid, y_out if l == DEPTH - 1 else resid)
            P.barrier(extra=xres_free)
    P.barrier(extra=xres_free)

    block = stack.enter_context(nc.Block())

    @block.tensor
    def _(e):
        P.emit("pe", e)

    @block.scalar
    def _(e):
        P.emit("act", e)

    @block.vector
    def _(e):
        P.emit("dve", e)

    @block.gpsimd
    def _(e):
        P.emit("pool", e)

    @block.sync
    def _(e):
        P.emit("sp", e)


def _bf(a):
    return np.asarray(a, np.float32).astype(ml_dtypes.bfloat16)


def make_consts():
    c = {}
    c["c_ident"] = _bf(np.eye(128))
    pm = np.zeros((128, 128), np.float32)
    for m in range(128):
        pm[(m + 64) % 128, m] = 1.0
    c["c_perm"] = _bf(pm)
    c["c_identf"] = np.eye(128, dtype=np.float32)
    half = 64
    inv = (1.0 / (10000.0 ** (np.arange(half, dtype=np.float32) / np.float32(half)))).astype(np.float32)
    ang = (np.arange(S, dtype=np.float32)[:, None] * inv[None, :]).astype(np.float32)
    cos = np.cos(ang.astype(np.float64)).T
    sin = np.sin(ang.astype(np.float64)).T
    c["c_cos"] = np.concatenate([cos, cos], 0).astype(np.float32)
    c["c_sin"] = np.concatenate([-sin, sin], 0).astype(np.float32)
    a = np.arange(128)[:, None]
    cc = np.arange(256)[None, :]
    master = (np.abs(cc - 64 - a) <= 64).astype(np.float32)
    masks = np.zeros((7, 128, 512), np.float32)
    masks[0] = np.concatenate([master[:, 192:256], master[:, 64:256], master[:, 0:256]], 1)
    masks[1] = np.concatenate([master[:, 0:256], master[:, 0:192], master[:, 0:64]], 1)
    one = np.concatenate([master[:, 192:256], master[:, 64:192], master[:, 0:64]], 1)
    masks[2] = np.concatenate([one, one], 1)
    for Qb in range(4):
        masks[3 + Qb] = np.tile(master[:, 64 + 32 * Qb:64 + 32 * Qb + 32], (1, 16))
    c["c_mask"] = _bf(masks)
    k = np.arange(512, dtype=np.float64)
    th = 2 * np.pi * np.outer(k, k) / 512.0
    c["c_cc"] = _bf(np.cos(th) / math.sqrt(512.0))
    c["c_sc"] = _bf(np.sin(th) / math.sqrt(512.0))
    k = np.arange(S, dtype=np.float64)
    th = 2 * np.pi * ((np.outer(k, k)) % S) / S
    c["c_cs"] = _bf(np.cos(th) / math.sqrt(S))
    c["c_ns"] = _bf(-np.sin(th) / math.sqrt(S))
    return c


_NC_CACHE = {}


def kernel(x_prompt, x_sample, w_in_mix, conv_short, w_out_mix, w_out_fourier, ln_mix_g, ln_mix_b,
           w_up, conv_ffn_w, w_down, ln_ffn_g, ln_ffn_b):
    f = lambda a: np.ascontiguousarray(np.asarray(a, dtype=np.float32))
    xp, xs = f(x_prompt), f(x_sample)
    consts = make_consts()
    shared = dict(w_in_mix=f(w_in_mix), conv_short=f(conv_short), w_out_mix=f(w_out_mix),
                  w_out_fourier=f(w_out_fourier), ln_mix_g=f(ln_mix_g), ln_mix_b=f(ln_mix_b),
                  w_up=f(w_up), conv_ffn_w=f(conv_ffn_w), w_down=f(w_down), ln_ffn_g=f(ln_ffn_g),
                  ln_ffn_b=f(ln_ffn_b))
    shared.update(consts)
    in_maps = []
    for c in range(8):
        xs_c = np.ascontiguousarray(np.stack([xp[2 * c], xp[2 * c + 1], xs[c]], 0))
        m = dict(shared)
        m["x"] = xs_c
        in_maps.append(m)
    nc = build(NSEQ, "full")
    res = run_bass_kernel_spmd(nc, in_maps, core_ids=list(range(8)))
    yp = np.empty_like(xp)
    ys = np.empty_like(xs)
    for c in range(8):
        y = res.results[c]["y"]
        yp[2 * c] = y[0]
        yp[2 * c + 1] = y[1]
        ys[c] = y[2]
    return (yp, ys)
```

```python
import math
from contextlib import ExitStack
import numpy as np
import ml_dtypes
import concourse.bass as bass
import concourse.mybir as mybir
from concourse.bass_utils import run_bass_kernel_spmd

F32 = mybir.dt.float32
BF16 = mybir.dt.bfloat16
AF = mybir.ActivationFunctionType
ALU = mybir.AluOpType

D = 2048
S = 2048
DEPTH = 4
DFF = 5632
NCH_FF = DFF // 128
DIN = 6144
ALPHA = (2 * DEPTH) ** 0.25
LN_EPS = 1e-5
NSEQ = 3
ENG = ("pe", "act", "dve", "pool", "sp")


class DSem:
    def __init__(self, h):
        self.h = h
        self.count = 0


class Rec:
    def __init__(self):
        self.call = None

    def __getattr__(self, name):
        def f(*a, **k):
            self.call = (name, a, k)
            return self
        return f


def freeze(fn):
    r = Rec()
    fn(r)
    name, a, k = r.call
    return lambda e: getattr(e, name)(*a, **k)


class Plan:
    def __init__(self, nc, stack):
        self.nc = nc
        self.stack = stack
        self.q = {e: [] for e in ENG}
        self.cnt = {e: 0 for e in ENG}
        self.sem = {}
        self.waited = {}
        self.nsem = 0
        self.last = {e: None for e in ENG}
        self.new_epoch()

    def mksem(self, name):
        self.nsem += 1
        return self.stack.enter_context(self.nc.semaphore(f"{name}_{self.nsem}"))

    def dsem(self, name):
        return DSem(self.mksem(name))

    def new_epoch(self):
        for e in ("pe", "act", "dve", "pool"):
            self.sem[e] = self.mksem("e" + e)
            self.cnt[e] = 0

    def op(self, eng, fn, sig=True):
        fn = freeze(fn)
        if sig:
            self.cnt[eng] += 1
            tok = (self.sem[eng], self.cnt[eng])
            self.q[eng].append(("op", fn, self.sem[eng]))
            self.last[eng] = tok
            return tok
        self.q[eng].append(("op", fn, None))
        return None

    def wait(self, eng, *toks):
        for tok in toks:
            if tok is None:
                continue
            if isinstance(tok, (list,)):
                self.wait(eng, *tok)
                continue
            sem, val = tok
            key = (eng, id(sem))
            if self.waited.get(key, 0) >= val:
                continue
            self.waited[key] = val
            self.q[eng].append(("wait", sem, val))

    def dma(self, eng, fn, ds):
        fn = freeze(fn)
        ds.count += 16
        self.q[eng].append(("dma", fn, ds.h))
        return (ds.h, ds.count)

    def barrier(self, extra=()):
        toks = [self.last[e] for e in ("pe", "act", "dve", "pool")] + list(extra)
        for e in ENG:
            self.wait(e, *toks)

    def emit(self, eng, e):
        for kind, a, b in self.q[eng]:
            if kind == "op":
                inst = a(e)
                if b is not None:
                    inst.then_inc(b, 1)
            elif kind == "wait":
                e.wait_ge(a, b)
            else:
                a(e).then_inc(b, 16)


def build(nseq=NSEQ, mode="full"):
    nc = bass.Bass("TRN2", target_bir_lowering=False)
    stack = ExitStack()
    with stack:
        _build(nc, stack, nseq, mode)
    return nc


def _build(nc, stack, nseq, mode):
    def din(name, shape, dt=F32):
        return nc.dram_tensor(name, list(shape), dt, kind="ExternalInput").ap()

    def dscr(name, shape, dt):
        return nc.dram_tensor(name, list(shape), dt, kind="Internal").ap()

    x_in = din("x", [nseq, S, D])
    y_out = nc.dram_tensor("y", [nseq, S, D], F32, kind="ExternalOutput").ap()
    w_in = din("w_in_mix", [2, D, DIN])
    conv_short = din("conv_short", [2, 3, 1024])
    w_om = din("w_out_mix", [2, D, D])
    w_of = din("w_out_fourier", [2, D, D])
    ln_mix_g = din("ln_mix_g", [4, D]); ln_mix_b = din("ln_mix_b", [4, D])
    w_up = din("w_up", [4, D, 2 * DFF])
    conv_ffn = din("conv_ffn_w", [4, 3, 2 * DFF])
    w_dn = din("w_down", [4, DFF, D])
    ln_ffn_g = din("ln_ffn_g", [4, D]); ln_ffn_b = din("ln_ffn_b", [4, D])
    c_ident = din("c_ident", [128, 128], BF16)
    c_perm = din("c_perm", [128, 128], BF16)
    c_identf = din("c_identf", [128, 128], F32)
    c_cos = din("c_cos", [128, S], F32)
    c_sin = din("c_sin", [128, S], F32)
    c_mask = din("c_mask", [7, 128, 512], BF16)
    c_cc = din("c_cc", [512, 512], BF16)
    c_sc = din("c_sc", [512, 512], BF16)
    c_cs = din("c_cs", [S, S], BF16)
    c_ns = din("c_ns", [S, S], BF16)

    dbg = nc.dram_tensor("dbg", [128, 8192], F32, kind="ExternalOutput").ap() if mode != "full" else None
    ds_dbg = None
    dbg16 = nc.dram_tensor("dbg16", [128, 8192], BF16, kind="ExternalOutput").ap() if mode != "full" else None
    wb_in = dscr("wb_in", [2, D, DIN], BF16)
    wb_om = dscr("wb_om", [2, D, D], BF16)
    wb_of = dscr("wb_of", [2, D, D], BF16)
    wb_up = dscr("wb_up", [4, D, 2 * DFF], BF16)
    wb_dn = dscr("wb_dn", [4, DFF, D], BF16)
    resid = dscr("resid", [nseq, S, D], F32)
    ycat = dscr("ycat", [nseq, D, S], BF16)

    sb = lambda name, shape, dt: stack.enter_context(nc.sbuf_tensor(name, list(shape), dt))
    xT = sb("xT", [128, 16, 4, 514], BF16)
    ident = sb("ident", [128, 128], BF16)
    perm = sb("perm", [128, 128], BF16)
    ones = sb("ones", [128, 128], BF16)
    identf = sb("identf", [128, 128], F32)
    cw = sb("cw", [128, 4, 3, 88], F32)
    cs = sb("cs", [128, 2, 3, 8], F32)
    small = sb("small", [128, 64], F32)
    NW = 34816
    WK = sb("WK", [128, NW], F32)
    ps = [stack.enter_context(nc.psum_tensor(f"ps{i}", [128, 1024], F32)) for i in range(4)]

    def cv(off, shape, dt):
        n = int(np.prod(shape))
        if dt == F32:
            assert off % 4 == 0
            ap = WK[:, off // 4: off // 4 + n]
        else:
            ap = WK[:, off // 4: off // 4 + (n + 1) // 2].bitcast(BF16)
        if len(shape) == 1:
            return ap
        if len(shape) == 2:
            return ap.rearrange("p (a b) -> p a b", a=shape[0])
        if len(shape) == 3:
            return ap.rearrange("p (a b c) -> p a b c", a=shape[0], b=shape[1])
        raise ValueError

    def bank(i):
        return ps[i // 2][:, (i % 2) * 512:(i % 2) * 512 + 512]

    def bank_bf(i):
        return ps[i // 2][:, (i % 2) * 512:(i % 2) * 512 + 512].bitcast(BF16)

    P = Plan(nc, stack)

    acc = cv(0, [4, 2048], F32)
    xres = cv(32768, [2, 2048], F32)
    ybf = cv(49152, [2048], BF16)
    gam = cv(53248, [2048], F32)
    bet = cv(61440, [2048], F32)
    wdn = cv(69632, [3, 2, 2048], BF16)
    wup = cv(94208, [3, 16, 256], BF16)
    aT = cv(118784, [4, 2, 512], BF16)
    ctmp = cv(126976, [2, 512], F32)
    sgb = cv(131072, [4, 512], F32)
    kopb = cv(94208, [2, 16, 512], BF16)
    assert 139264 <= NW * 4
    st6 = small[:, 0:48].rearrange("p (b c s) -> p b c s", b=2, c=4)
    mv = small[:, 48:52].rearrange("p (b s) -> p b s", b=2)
    rstd = small[:, 52:54]
    nmr = small[:, 54:56]
    sdv = small[:, 56:58]

    ds_const = P.dsem("const")
    ds_cast = {}
    ds_wdn = [P.dsem("wdn") for _ in range(3)]
    ds_wup = [P.dsem("wup") for _ in range(3)]
    ds_xres = [P.dsem("xres") for _ in range(2)]
    ds_yst = [P.dsem("yst") for _ in range(2)]
    ds_gb = P.dsem("gb")
    ds_kop = [P.dsem("kop") for _ in range(2)]
    ds_misc = P.dsem("misc")
    ds_dbg = P.dsem("dbg")
    ds_blk = [P.dsem("blk") for _ in range(4)]
    ds_f = P.dsem("fst")
    ds_win = [P.dsem("win") for _ in range(3)]
    ds_y = [P.dsem("ycs") for _ in range(2)]
    ds_ya = P.dsem("yat")

    def dump(ap, col0, n, *toks):
        if dbg is None:
            return
        P.wait("sp", *toks)
        tgt = dbg16 if ap.dtype == BF16 else dbg
        t = P.dma("sp", lambda e: e.dma_start(out=tgt[:, col0:col0 + n], in_=ap), ds_dbg)
        for e_ in ENG:
            P.wait(e_, t)

    ctoks = []
    ctoks.append(P.dma("sp", lambda e: e.dma_start(out=ident[:], in_=c_ident[:, :]), ds_const))
    ctoks.append(P.dma("sp", lambda e: e.dma_start(out=perm[:], in_=c_perm[:, :]), ds_const))
    ctoks.append(P.dma("sp", lambda e: e.dma_start(out=identf[:], in_=c_identf[:, :]), ds_const))
    cwraw = cv(0, [4, 3, 128], F32)
    csraw = cv(8192, [2, 3, 128], F32)
    for l in range(4):
        ctoks.append(P.dma("sp", lambda e, l=l: e.dma_start(
            out=cwraw[0:88, l], in_=conv_ffn[l].rearrange("k (c p) -> c k p", p=128)), ds_const))
    for i in range(2):
        ctoks.append(P.dma("sp", lambda e, i=i: e.dma_start(
            out=csraw[0:8, i], in_=conv_short[i].rearrange("k (c p) -> c k p", p=128)), ds_const))
    P.wait("pe", ctoks[-1]); P.wait("pool", ctoks[-1]); P.wait("dve", ctoks[-1]); P.wait("act", ctoks[-1])
    P.op("pool", lambda e: e.memset(ones[:], 1.0))
    P.op("pool", lambda e: e.memset(xT[:, :, 0, 0:1], 0.0))
    tz = P.op("pool", lambda e: e.memset(xT[:, :, 3, 513:514], 0.0))
    tcw = None
    for l in range(4):
        for k in range(3):
            idx = l * 3 + k
            bk, col = idx // 5, (idx % 5) * 88
            tcw = P.op("pe", lambda e, l=l, k=k, bk=bk, col=col: e.transpose(
                out=bank(bk)[:, col:col + 88], in_=cwraw[0:88, l, k, :], identity=identf[0:88, 0:88]),
                sig=(idx == 11))
    P.wait("dve", tcw)
    for l in range(4):
        for k in range(3):
            idx = l * 3 + k
            bk, col = idx // 5, (idx % 5) * 88
            P.op("dve", lambda e, l=l, k=k, bk=bk, col=col: e.tensor_copy(out=cw[:, l, k, :], in_=bank(bk)[:, col:col + 88]))
    tcs = None
    for i in range(2):
        for k in range(3):
            idx = i * 3 + k
            tcs = P.op("pe", lambda e, i=i, k=k, idx=idx: e.transpose(
                out=bank(3)[:, idx * 8:idx * 8 + 8], in_=csraw[0:8, i, k, :], identity=identf[0:8, 0:8]),
                sig=(idx == 5))
    P.wait("dve", tcs)
    for i in range(2):
        for k in range(3):
            idx = i * 3 + k
            P.op("dve", lambda e, i=i, k=k, idx=idx: e.tensor_copy(out=cs[:, i, k, :], in_=bank(3)[:, idx * 8:idx * 8 + 8]))

    def cast(name, src, dst, l):
        sflat = src[l].rearrange("k (j c) -> (k j) c", c=1024)
        dflat = dst[l].rearrange("k (j c) -> (k j) c", c=1024)
        rows = sflat.shape[0]
        dsx = P.dsem("cast" + name)
        tok = None
        r = 0
        while r < rows:
            n = min(8192, rows - r)
            tok = P.dma("pool", lambda e, r=r, n=n: e.dma_start(out=dflat[r:r + n, :], in_=sflat[r:r + n, :]), dsx)
            r += n
        ds_cast[(name, l)] = tok

    for l in range(4):
        if l % 2 == 0:
            cast("in", w_in, wb_in, l // 2); cast("om", w_om, wb_om, l // 2)
        else:
            cast("of", w_of, wb_of, l // 2)
        cast("up", w_up, wb_up, l); cast("dn", w_dn, wb_dn, l)

    st = dict(wdn_i=0, wup_i=0, bankC=0, xres_i=0, chunk_i=0)
    wdn_free = [None] * 3
    wup_free = [None] * 3
    bankC_free = [None, None]
    bankD_free = [None]
    acc_free = [None] * 4
    xres_free = [None, None]
    ybf_free = [None]
    xT_ready = []
    pair_free = [None, None]
    aT_free = [None] * 4
    sg_free = [None] * 4
    ct_free = [None, None]
    gb_tok = [None]

    def xcol(t):
        return t // 512, t % 512 + 1

    def writeback(ybf_tok, t0, ffn_inplace):
        slot, c0 = xcol(t0)
        P.wait("pe", ybf_tok, bankD_free[0])
        tp = None
        for d in range(16):
            tp = P.op("pe", lambda e, d=d: e.transpose(
                out=bank_bf(6 + d // 8)[:, (d % 8) * 128:(d % 8) * 128 + 128],
                in_=ybf[:, d * 128:(d + 1) * 128], identity=ident[:]), sig=(d == 15))
        ybf_free[0] = tp
        toks = []
        for half, eng in ((0, "dve"), (1, "act")):
            src = bank_bf(6 + half).rearrange("p (d c) -> p d c", d=8)
            P.wait(eng, tp)
            if eng == "dve":
                cp = lambda e, o, i: e.tensor_copy(out=o, in_=i)
            else:
                cp = lambda e, o, i: e.activation(out=o, in_=i, func=AF.Copy)
            dsl = slice(half * 8, half * 8 + 8)
            tk = P.op(eng, lambda e, cp=cp, dsl=dsl, src=src: cp(e, xT[:, dsl, slot, c0:c0 + 128], src))
            if t0 % 512 == 0 and slot > 0:
                tk = P.op(eng, lambda e, cp=cp, dsl=dsl, src=src: cp(e, xT[:, dsl, slot - 1, 513:514], src[:, :, 0:1]))
            if t0 % 512 == 384 and slot < 3 and not ffn_inplace:
                tk = P.op(eng, lambda e, cp=cp, dsl=dsl, src=src: cp(e, xT[:, dsl, slot + 1, 0:1], src[:, :, 127:128]))
            toks.append(tk)
        bankD_free[0] = toks
        xT_ready[:] = toks
        return toks

    def load_gb(g_ap, b_ap, l):
        P.wait("sp", P.last["pool"], P.last["act"], P.last["dve"])
        P.dma("sp", lambda e: e.dma_start(out=gam, in_=g_ap[l:l + 1, :].partition_broadcast(128)), ds_gb)
        gb_tok[0] = P.dma("sp", lambda e: e.dma_start(out=bet, in_=b_ap[l:l + 1, :].partition_broadcast(128)), ds_gb)

    def d_prefetch(wsrc, cast_tok):
        slot = st["wdn_i"] % 3
        st["wdn_i"] += 1
        P.wait("sp", wdn_free[slot], cast_tok)
        ld = P.dma("sp", lambda e: e.dma_start(out=wdn[:, slot], in_=wsrc), ds_wdn[slot])
        return slot, ld

    def d_part(q, k, kops, slot, ld, kop_toks, nr=(0, 1, 2, 3)):
        P.wait("pe", ld, *kop_toks)
        last_pe = None
        s = k
        for n in nr:
            bi = st["bankC"] % 2
            st["bankC"] += 1
            P.wait("pe", bankC_free[bi])
            for g in range(2):
                last_pe = P.op("pe", lambda e, g=g: e.matmul(
                    bank(4 + bi), lhsT=kops[g][:, s * 128:(s + 1) * 128],
                    rhs=wdn[:, slot, g, n * 512:(n + 1) * 512], start=(g == 0), stop=(g == 1)), sig=(g == 1))
            P.wait("dve", last_pe)
            dst = acc[:, s, n * 512:(n + 1) * 512]
            if q == 0:
                P.wait("dve", acc_free[s])
                bankC_free[bi] = P.op("dve", lambda e: e.tensor_copy(out=dst, in_=bank(4 + bi)))
            else:
                bankC_free[bi] = P.op("dve", lambda e: e.tensor_tensor(out=dst, in0=dst, in1=bank(4 + bi), op=ALU.add))
        if k == 3 and nr[-1] == 3:
            wdn_free[slot] = last_pe
        return last_pe

    fin = {}

    def fin_chain(seq, tt, s, rsrc, rdst):
        t0 = tt * 512 + s * 128
        b = st["xres_i"] % 2
        st["xres_i"] += 1
        P.wait("sp", xres_free[b])
        ldx = P.dma("sp", lambda e: e.dma_start(out=xres[:, b], in_=rsrc[seq, t0:t0 + 128, :]), ds_xres[b])
        P.wait("dve", ldx, P.last["dve"], P.last["act"])
        P.op("dve", lambda e: e.scalar_tensor_tensor(
            out=acc[:, s], in0=xres[:, b], scalar=float(ALPHA), in1=acc[:, s], op0=ALU.mult, op1=ALU.add))
        P.wait("dve", P.last["dve"])
        for c in range(4):
            P.op("dve", lambda e, c=c: e.bn_stats(out=st6[:, b, c], in_=acc[:, s, c * 512:(c + 1) * 512]))
        P.wait("dve", P.last["dve"])
        P.op("dve", lambda e: e.bn_aggr(out=mv[:, b], in_=st6[:, b].rearrange("p c s -> p (c s)")))
        P.wait("act", P.last["dve"])
        tsd = P.op("act", lambda e: e.activation(
            out=sdv[:, b:b + 1], in_=mv[:, b, 1:2], func=AF.Sqrt, bias=float(LN_EPS), scale=1.0))
        P.wait("dve", tsd)
        P.op("dve", lambda e: e.reciprocal(out=rstd[:, b:b + 1], in_=sdv[:, b:b + 1]))
        P.wait("dve", P.last["dve"])
        tstat = P.op("dve", lambda e: e.scalar_tensor_tensor(
            out=nmr[:, b:b + 1], in0=mv[:, b, 0:1], scalar=-1.0, in1=rstd[:, b:b + 1], op0=ALU.mult, op1=ALU.mult))
        P.wait("act", tstat)
        txh = P.op("act", lambda e: e.activation(
            out=xres[:, b], in_=acc[:, s], func=AF.Identity, scale=rstd[:, b:b + 1], bias=nmr[:, b:b + 1]))
        acc_free[s] = txh
        P.wait("pool", txh, gb_tok[0])
        P.op("pool", lambda e: e.tensor_tensor(out=xres[:, b, 0:1024], in0=xres[:, b, 0:1024], in1=gam[:, 0:1024], op=ALU.mult))
        P.wait("pool", P.last["pool"])
        ty1 = P.op("pool", lambda e: e.tensor_tensor(out=xres[:, b, 0:1024], in0=xres[:, b, 0:1024], in1=bet[:, 0:1024], op=ALU.add))
        P.wait("dve", txh, gb_tok[0])
        P.op("dve", lambda e: e.tensor_tensor(out=xres[:, b, 1024:2048], in0=xres[:, b, 1024:2048], in1=gam[:, 1024:2048], op=ALU.mult))
        P.wait("dve", P.last["dve"])
        ty2 = P.op("dve", lambda e: e.tensor_tensor(out=xres[:, b, 1024:2048], in0=xres[:, b, 1024:2048], in1=bet[:, 1024:2048], op=ALU.add))
        fin[(tt, s)] = (b, [ty1, ty2], t0, seq, rdst)

    def fin_cast(tt, s):
        b, ty, t0, seq, rdst = fin[(tt, s)]
        P.wait("act", ty, ybf_free[0])
        tyb = P.op("act", lambda e: e.activation(out=ybf, in_=xres[:, b], func=AF.Copy))
        P.wait("sp", ty, tyb)
        xres_free[b] = P.dma("sp", lambda e: e.dma_start(out=rdst[seq, t0:t0 + 128, :], in_=xres[:, b]), ds_yst[b])
        fin[(tt, s)] = (tyb, t0)

    def fin_wb(tt, s, ffn_inplace):
        tyb, t0 = fin.pop((tt, s))
        writeback(tyb, t0, ffn_inplace)

    def g2_finish(seq, tt, rsrc, rdst, ffn_inplace):
        fin_chain(seq, tt, 0, rsrc, rdst)
        for s in range(4):
            if s + 1 < 4:
                fin_chain(seq, tt, s + 1, rsrc, rdst)
            fin_cast(tt, s)
            fin_wb(tt, s, ffn_inplace)

    def load_x(seq):
        for s in range(16):
            t0 = s * 128
            b = st["xres_i"] % 2
            st["xres_i"] += 1
            P.wait("sp", xres_free[b])
            ldx = P.dma("sp", lambda e, b=b, t0=t0: e.dma_start(out=xres[:, b], in_=x_in[seq, t0:t0 + 128, :]), ds_xres[b])
            P.wait("act", ldx, ybf_free[0])
            tyb = P.op("act", lambda e, b=b: e.activation(out=ybf, in_=xres[:, b], func=AF.Copy))
            xres_free[b] = tyb
            writeback(tyb, t0, False)

    def ffn(seq, l, rsrc, rdst):
        load_gb(ln_ffn_g, ln_ffn_b, l)
        wu = wb_up[l].rearrange("(kc p) n -> p kc n", p=128)
        wd = wb_dn[l].rearrange("(j p) n -> p j n", p=128)
        LAG = 3
        items = [(tt, b, half) for tt in range(4) for b in range(22) for half in range(2)]
        wl = {}

        def up_prefetch(i):
            if i >= len(items) or i in wl:
                return
            tt_, b_, half_ = items[i]
            col0 = half_ * DFF + b_ * 256
            slot = st["wup_i"] % 3
            st["wup_i"] += 1
            P.wait("sp", wup_free[slot], ds_cast[("up", l)])
            wl[i] = (slot, P.dma("sp", lambda e: e.dma_start(out=wup[:, slot], in_=wu[:, :, col0:col0 + 256]), ds_wup[slot]))

        up_prefetch(0)
        up_prefetch(1)
        it = 0
        for tt in range(4):
            u_done = None
            grp_toks = {}
            dgrp = {}

            def dpre(q):
                if 0 <= q < 22 and q not in dgrp:
                    dgrp[q] = d_prefetch(wd[:, 2 * q:2 * q + 2, :], ds_cast[("dn", l)])

            def dpart(q, k, nr=(0, 1, 2, 3)):
                gi = q % 4
                slot, ld = dgrp[q]
                tk = d_part(q, k, [aT[:, gi, 0], aT[:, gi, 1]], slot, ld,
                            [grp_toks[(q, 0, "a")], grp_toks[(q, 1, "a")]], nr)
                if k == 3 and nr[-1] == 3:
                    aT_free[gi] = tk

            for b in range(22):
                dpre(b - LAG)
                dpre(b - LAG + 1)
                for half in range(2):
                    slot, ld = wl.pop(it)
                    up_prefetch(it + 2)
                    it += 1
                    if seq == 0 and tt == 0 and b == 0 and half == 0:
                        dump(wup[:, slot].rearrange("p k c -> p (k c)"), 2048, 4096, ld)
                    P.wait("pe", ld)
                    for ci in range(2):
                        ch = half * NCH_FF + b * 2 + ci
                        pi = st["chunk_i"] % 2
                        st["chunk_i"] += 1
                        P.wait("pe", pair_free[pi])
                        lp = None
                        for kc in range(16):
                            for hh in range(2):
                                lp = P.op("pe", lambda e: e.matmul(
                                    bank(2 * pi + hh)[:, 0:258], lhsT=wup[:, slot, kc, ci * 128:(ci + 1) * 128],
                                    rhs=xT[:, kc, tt, hh * 256:hh * 256 + 258], start=(kc == 0), stop=(kc == 15)),
                                    sig=(kc == 15 and hh == 1))
                            if kc == 7 and b >= LAG:
                                dpart(b - LAG, half * 2 + ci, (0, 1))
                        u_done = lp
                        if b >= LAG:
                            dpart(b - LAG, half * 2 + ci, (2, 3))
                        pv = ps[pi].rearrange("p (h c) -> p h c", h=2)
                        cti = pi
                        ct = ctmp[:, cti].rearrange("p (h c) -> p h c", h=2)
                        P.wait("act", lp, ct_free[cti])
                        ta = P.op("act", lambda e: e.activation(
                            out=ct, in_=pv[:, :, 1:257], func=AF.Identity, scale=cw[:, l, 1, ch:ch + 1]))
                        P.wait("dve", ta)
                        tb = P.op("dve", lambda e: e.scalar_tensor_tensor(
                            out=ct, in0=pv[:, :, 0:256], scalar=cw[:, l, 0, ch:ch + 1], in1=ct, op0=ALU.mult, op1=ALU.add))
                        P.wait("dve", tb)
                        tc = P.op("dve", lambda e: e.scalar_tensor_tensor(
                            out=ct, in0=pv[:, :, 2:258], scalar=cw[:, l, 2, ch:ch + 1], in1=ct, op0=ALU.mult, op1=ALU.add))
                        pair_free[pi] = tc
                        if seq == 0 and tt == 0 and b == 0 and ci == 0:
                            dump(ctmp[:, cti], 0 if half == 0 else 1024, 512, tc)
                        sgi = (b % 2) * 2 + ci
                        if half == 0:
                            P.wait("act", tc, sg_free[sgi])
                            ct_free[cti] = P.op("act", lambda e: e.activation(
                                out=sgb[:, sgi], in_=ctmp[:, cti], func=AF.Silu))
                            grp_toks[(b, ci, "sg")] = ct_free[cti]
                        else:
                            gi = b % 4
                            P.wait("pool", tc, grp_toks[(b, ci, "sg")], aT_free[gi])
                            tm = P.op("pool", lambda e: e.tensor_tensor(
                                out=aT[:, gi, ci], in0=sgb[:, sgi], in1=ctmp[:, cti], op=ALU.mult))
                            ct_free[cti] = tm
                            sg_free[sgi] = tm
                            grp_toks[(b, ci, "a")] = tm
                    wup_free[slot] = u_done
                if tt >= 1:
                    pt = tt - 1
                    if b == 0:
                        fin_chain(seq, pt, 0, rsrc, rdst); fin_cast(pt, 0); fin_chain(seq, pt, 1, rsrc, rdst)
                    elif b == 1:
                        fin_wb(pt, 0, True); fin_cast(pt, 1); fin_chain(seq, pt, 2, rsrc, rdst)
                    elif b == 2:
                        fin_wb(pt, 1, True); fin_cast(pt, 2); fin_chain(seq, pt, 3, rsrc, rdst)
                    elif b == 3:
                        fin_wb(pt, 2, True); fin_cast(pt, 3)
                    elif b == 4:
                        fin_wb(pt, 3, True)
            for q in range(22 - LAG, 22):
                dpre(q)
                for k in range(4):
                    dpart(q, k)
            if tt >= 1:
                P.wait("pool", u_done, *xT_ready)
                P.op("pool", lambda e: e.tensor_copy(out=xT[:, :, tt, 0:1], in_=xT[:, :, tt - 1, 512:513]))
            if seq == 0 and tt == 0:
                dump(acc[:, 0], 2048, 2048, P.last["dve"])
            if tt == 3:
                g2_finish(seq, tt, rsrc, rdst, True)

    def m2(seq, wsrc_l, cast_tok, g_ap, b_ap, l, rsrc, rdst, ycat_tok):
        load_gb(g_ap, b_ap, l)
        wv = wsrc_l.rearrange("(kc p) n -> p kc n", p=128)
        yc = ycat[seq].rearrange("(kc p) t -> p kc t", p=128)
        wblk = cv(69632, [2, 16, 512], BF16)
        kop2 = cv(102400, [2, 16, 512], BF16)
        kop_free = [None, None]
        wb_free = [None, None]
        kl = {}
        wq = {}
        wi_ = [0]

        def kpre(tt):
            if tt < 4 and tt not in kl:
                kb = tt % 2
                P.wait("sp", kop_free[kb], ycat_tok)
                kl[tt] = P.dma("sp", lambda e: e.dma_start(out=kop2[:, kb], in_=yc[:, :, tt * 512:(tt + 1) * 512]), ds_kop[kb])

        def wpre(i):
            if i < 16 and i not in wq:
                n = i % 4
                slot = wi_[0] % 2
                wi_[0] += 1
                P.wait("sp", wb_free[slot], cast_tok)
                wq[i] = (slot, P.dma("sp", lambda e: e.dma_start(out=wblk[:, slot], in_=wv[:, :, n * 512:(n + 1) * 512]), ds_wdn[slot]))

        kpre(0)
        wpre(0)
        evc = [0]

        def G(tt, n, s):
            kb = tt % 2
            i = tt * 4 + n
            slot, ld = wq[i]
            if s == 0:
                wpre(i + 1)
            P.wait("pe", ld, kl[tt])
            bi = st["bankC"] % 2
            st["bankC"] += 1
            P.wait("pe", bankC_free[bi])
            lp = None
            for kc in range(16):
                lp = P.op("pe", lambda e: e.matmul(
                    bank(4 + bi), lhsT=kop2[:, kb, kc, s * 128:(s + 1) * 128], rhs=wblk[:, slot, kc, :],
                    start=(kc == 0), stop=(kc == 15)), sig=(kc == 15))
            eng = "dve" if evc[0] % 2 == 0 else "act"
            evc[0] += 1
            dst = acc[:, s, n * 512:(n + 1) * 512]
            P.wait(eng, lp, acc_free[s])
            if eng == "dve":
                bankC_free[bi] = P.op("dve", lambda e: e.tensor_copy(out=dst, in_=bank(4 + bi)))
            else:
                bankC_free[bi] = P.op("act", lambda e: e.activation(out=dst, in_=bank(4 + bi), func=AF.Copy))
            if s == 3:
                wb_free[slot] = lp
            if n == 3 and s == 3:
                kop_free[kb] = lp

        for tt in range(4):
            kpre(tt + 1)
            for n in range(4):
                if n == 1 and tt >= 1:
                    fin_cast(tt - 1, 2); fin_wb(tt - 1, 2, False); fin_cast(tt - 1, 3); fin_wb(tt - 1, 3, False)
                for s in range(4):
                    if tt >= 1 and n == 0:
                        continue
                    G(tt, n, s)
            fin_chain(seq, tt, 0, rsrc, rdst); fin_chain(seq, tt, 1, rsrc, rdst)
            if tt < 3:
                G(tt + 1, 0, 0)
            fin_cast(tt, 0); fin_wb(tt, 0, False)
            fin_chain(seq, tt, 2, rsrc, rdst)
            if tt < 3:
                G(tt + 1, 0, 1)
            fin_cast(tt, 1); fin_wb(tt, 1, False)
            fin_chain(seq, tt, 3, rsrc, rdst)
            if tt < 3:
                G(tt + 1, 0, 2)
                G(tt + 1, 0, 3)
            if tt == 3:
                fin_cast(tt, 2); fin_wb(tt, 2, False); fin_cast(tt, 3); fin_wb(tt, 3, False)

    def fourier_m1(seq):
        Yc = cv(0, [16, 512], BF16)
        Ys = cv(16384, [16, 512], BF16)
        CC = cv(32768, [4, 512], BF16)
        SC = cv(36864, [4, 512], BF16)
        blk = cv(40960, [4, 16, 256], BF16)
        fst = cv(73728, [4, 2048], BF16)
        blk_free = [None] * 4
        P.wait("sp", P.last["pe"])
        P.dma("sp", lambda e: e.dma_start(out=CC, in_=c_cc.rearrange("(k p) n -> p k n", p=128)), ds_misc)
        tcc = P.dma("sp", lambda e: e.dma_start(out=SC, in_=c_sc.rearrange("(k p) n -> p k n", p=128)), ds_misc)
        csv = c_cs.rearrange("(k p) n -> p k n", p=128)
        nsv = c_ns.rearrange("(k p) n -> p k n", p=128)
        bC = [None, None]
        y_free = [None]
        fst_free = [None]
        cnt = 0
        for g in range(4):
            P.wait("pe", tcc)
            ev = []
            for stl in range(16):
                slot, c0 = xcol(stl * 128)
                for which, (M, Yb, eng) in enumerate(((CC, Yc, "act"), (SC, Ys, "dve"))):
                    bi = which
                    P.wait("pe", bC[bi], y_free[0] if stl == 0 else None)
                    lp = None
                    for kc in range(4):
                        lp = P.op("pe", lambda e, kc=kc, slot=slot, c0=c0, M=M, bi=bi: e.matmul(
                            bank(4 + bi), lhsT=xT[:, 4 * g + kc, slot, c0:c0 + 128], rhs=M[:, kc, :],
                            start=(kc == 0), stop=(kc == 3)), sig=(kc == 3))
                    P.wait(eng, lp)
                    if eng == "act":
                        bC[bi] = P.op("act", lambda e, Yb=Yb, stl=stl, bi=bi: e.activation(out=Yb[:, stl], in_=bank(4 + bi), func=AF.Copy))
                    else:
                        bC[bi] = P.op("dve", lambda e, Yb=Yb, stl=stl, bi=bi: e.tensor_copy(out=Yb[:, stl], in_=bank(4 + bi)))
                    ev.append(bC[bi])
            P.wait("pe", ev[-1], ev[-2])
            P.wait("act", fst_free[0])
            lastpe = None
            for sp_ in range(8):
                bs = sp_ % 2
                P.wait("sp", blk_free[bs], blk_free[2 + bs])
                l1 = P.dma("sp", lambda e, bs=bs, sp_=sp_: e.dma_start(out=blk[:, bs], in_=csv[:, :, sp_ * 256:(sp_ + 1) * 256]), ds_blk[bs])
                l2 = P.dma("sp", lambda e, bs=bs, sp_=sp_: e.dma_start(out=blk[:, 2 + bs], in_=nsv[:, :, sp_ * 256:(sp_ + 1) * 256]), ds_blk[2 + bs])
                P.wait("pe", l1, l2)
                for cp_ in range(4):
                    bi = cnt % 2
                    cnt += 1
                    P.wait("pe", pair_free[bi])
                    lp = None
                    for which, Yb in enumerate((Yc, Ys)):
                        for sc in range(16):
                            lp = P.op("pe", lambda e, Yb=Yb, sc=sc, cp_=cp_, bs=bs, which=which, bi=bi: e.matmul(
                                bank(bi)[:, 0:256], lhsT=Yb[:, sc, cp_ * 128:(cp_ + 1) * 128], rhs=blk[:, 2 * which + bs, sc, :],
                                start=(which == 0 and sc == 0), stop=(which == 1 and sc == 15)),
                                sig=(which == 1 and sc == 15))
                    lastpe = lp
                    P.wait("act", lp)
                    pair_free[bi] = P.op("act", lambda e, cp_=cp_, sp_=sp_, bi=bi: e.activation(
                        out=fst[:, cp_, sp_ * 256:(sp_ + 1) * 256], in_=bank(bi)[:, 0:256], func=AF.Copy))
                blk_free[bs] = lastpe
                blk_free[2 + bs] = lastpe
            y_free[0] = lastpe
            P.wait("sp", P.last["act"])
            for cp_ in range(4):
                ch = 4 * g + cp_
                fst_free[0] = P.dma("sp", lambda e, cp_=cp_, ch=ch: e.dma_start(
                    out=ycat[seq, ch * 128:(ch + 1) * 128, :], in_=fst[:, cp_]), ds_f)
        return fst_free[0]

    def even_m1(seq, li):
        wi = wb_in[li].rearrange("(kc p) n -> p kc n", p=128)
        winb = cv(0, [3, 16, 256], BF16)
        cosb = cv(24576, [2048], F32)
        sinb = cv(32768, [2048], F32)
        maskb = cv(40960, [7, 512], BF16)
        stg = cv(49152, [6, 2050], F32)
        yst = cv(98352 + 16, [2, 2048], BF16)
        qb = cv(49152, [2, 2048], BF16)
        r1 = cv(57344, [512], F32)
        r2 = cv(59392, [512], F32)
        qT = cv(61440, [2, 2048], BF16)
        kT = cv(69632, [2, 2048], BF16)
        vT = cv(77824, [2, 2048], BF16)
        Vl = cv(86016, [3, 16, 128], BF16)
        Pb = cv(106560, [3, 512], BF16)
        rden = cv(109632, [512], F32)
        yat = cv(111680, [2048], BF16)
        win_free = [None] * 3
        wi_i = [0]
        P.wait("sp", P.last["pe"], P.last["act"], P.last["dve"], P.last["pool"])
        P.dma("sp", lambda e: e.dma_start(out=cosb, in_=c_cos[:, :]), ds_misc)
        P.dma("sp", lambda e: e.dma_start(out=sinb, in_=c_sin[:, :]), ds_misc)
        tconst = P.dma("sp", lambda e: e.dma_start(out=maskb, in_=c_mask.rearrange("m p c -> p m c")), ds_misc)
        for e_ in ("pe", "act", "dve", "pool"):
            P.wait(e_, tconst)
        for i in (4, 5):
            P.op("pool", lambda e, i=i: e.memset(stg[:, i, 0:1], 0.0))
            P.op("pool", lambda e, i=i: e.memset(stg[:, i, 2049:2050], 0.0))
        tpad = P.last["pool"]
        bk_free = [None] * 4
        bki = [0]
        ycat_tok = [None]

        col_order = []
        for c_ in range(4):
            col_order += [0 * 1024 + c_ * 256, 1 * 1024 + c_ * 256, 2 * 1024 + c_ * 256]
        for hp_ in range(4):
            col_order += [3 * 1024 + hp_ * 256, 4 * 1024 + hp_ * 256, 5 * 1024 + hp_ * 256]
        wl_in = {}
        blk_i = [0]

        def in_prefetch(i):
            if i >= len(col_order) or i in wl_in:
                return
            c0_ = col_order[i]
            slot_ = i % 3
            P.wait("sp", win_free[slot_], ds_cast[("in", li)])
            wl_in[i] = (slot_, P.dma("sp", lambda e: e.dma_start(out=winb[:, slot_], in_=wi[:, :, c0_:c0_ + 256]), ds_win[slot_]))

        def inproj_block(col0, epi):
            i = blk_i[0]
            blk_i[0] += 1
            assert col_order[i] == col0
            in_prefetch(i)
            slot, ld = wl_in.pop(i)
            in_prefetch(i + 1)
            P.wait("pe", ld)
            lp = None
            for ci in range(2):
                for tt in range(4):
                    bi = bki[0] % 4
                    bki[0] += 1
                    P.wait("pe", bk_free[bi])
                    for kc in range(16):
                        lp = P.op("pe", lambda e, kc=kc, ci=ci, tt=tt, bi=bi: e.matmul(
                            bank(bi), lhsT=winb[:, slot, kc, ci * 128:(ci + 1) * 128], rhs=xT[:, kc, tt, 1:513],
                            start=(kc == 0), stop=(kc == 15)), sig=(kc == 15))
                    bk_free[bi] = epi(ci, tt, bi, lp)
            win_free[slot] = lp

        yst_free = [None, None]
        for c in range(4):
            def epi_copy(base):
                def f(ci, tt, bi, lp):
                    P.wait("act", lp, yst_free[ci] if tt == 0 else None, tpad)
                    return P.op("act", lambda e: e.activation(
                        out=stg[:, base + ci, 1 + tt * 512:1 + (tt + 1) * 512], in_=bank(bi), func=AF.Copy))
                return f
            inproj_block(0 * 1024 + c * 256, epi_copy(0))
            inproj_block(1 * 1024 + c * 256, epi_copy(2))

            def epi_u(ci, tt, bi, lp):
                P.wait("dve", lp, P.last["act"])
                return P.op("dve", lambda e: e.tensor_tensor(
                    out=stg[:, 4 + ci, 1 + tt * 512:1 + (tt + 1) * 512], in0=stg[:, 2 + ci, 1 + tt * 512:1 + (tt + 1) * 512],
                    in1=bank(bi), op=ALU.mult))
            inproj_block(2 * 1024 + c * 256, epi_u)
            for ci in range(2):
                ch = c * 2 + ci
                u = stg[:, 4 + ci]
                tmp = stg[:, 2 + ci, 1:2049]
                P.wait("act", P.last["dve"])
                t1 = P.op("act", lambda e, u=u, tmp=tmp, ch=ch: e.activation(
                    out=tmp, in_=u[:, 1:2049], func=AF.Identity, scale=cs[:, li, 1, ch:ch + 1]))
                P.wait("dve", t1)
                t2 = P.op("dve", lambda e, u=u, tmp=tmp, ch=ch: e.scalar_tensor_tensor(
                    out=tmp, in0=u[:, 0:2048], scalar=cs[:, li, 0, ch:ch + 1], in1=tmp, op0=ALU.mult, op1=ALU.add))
                P.wait("dve", t2)
                t3 = P.op("dve", lambda e, u=u, tmp=tmp, ch=ch: e.scalar_tensor_tensor(
                    out=tmp, in0=u[:, 2:2050], scalar=cs[:, li, 2, ch:ch + 1], in1=tmp, op0=ALU.mult, op1=ALU.add))
                P.wait("pool", t3, yst_free[ci])
                t4 = P.op("pool", lambda e, ci=ci, tmp=tmp: e.tensor_tensor(
                    out=yst[:, ci], in0=tmp, in1=stg[:, ci, 1:2049], op=ALU.mult))
                P.wait("sp", t4)
                yst_free[ci] = P.dma("sp", lambda e, ci=ci, ch=ch: e.dma_start(
                    out=ycat[seq, ch * 128:(ch + 1) * 128, :], in_=yst[:, ci]), ds_y[ci])
                ycat_tok[0] = yst_free[ci]
        last_conv = [yst_free[0], yst_free[1]]

        P.barrier(extra=last_conv)
        scale = 1.0 / math.sqrt(128.0)
        scb_free = [None] * 3
        pb_free = [None] * 3
        nd_free = [None, None]
        yat_free = [None]
        sci = [0]
        for hp in range(4):
            def epi_rope(dst):
                def f(ci, tt, bi, lp):
                    sl = slice(tt * 512, (tt + 1) * 512)
                    P.wait("act", lp, P.last["pe"] if tt == 0 else None)
                    ta = P.op("act", lambda e: e.activation(out=qb[:, ci, sl], in_=bank(bi), func=AF.Copy))
                    P.wait("pe", ta, nd_free[0])
                    tsw = P.op("pe", lambda e: e.matmul(bank(4), lhsT=perm[:], rhs=qb[:, ci, sl], start=True, stop=True))
                    P.wait("dve", tsw, P.last["pool"])
                    t2_ = P.op("dve", lambda e: e.tensor_tensor(out=r2, in0=bank(4), in1=sinb[:, sl], op=ALU.mult))
                    nd_free[0] = t2_
                    P.wait("pool", ta, P.last["pool"])
                    t1_ = P.op("pool", lambda e: e.tensor_tensor(out=r1, in0=qb[:, ci, sl], in1=cosb[:, sl], op=ALU.mult))
                    P.wait("pool", t1_, t2_)
                    return P.op("pool", lambda e: e.tensor_tensor(out=dst[:, ci, sl], in0=r1, in1=r2, op=ALU.add))
                return f
            inproj_block(3 * 1024 + hp * 256, epi_rope(qT))
            inproj_block(4 * 1024 + hp * 256, epi_rope(kT))

            def epi_v(ci, tt, bi, lp):
                P.wait("act", lp)
                return P.op("act", lambda e: e.activation(out=vT[:, ci, tt * 512:(tt + 1) * 512], in_=bank(bi), func=AF.Copy))
            inproj_block(5 * 1024 + hp * 256, epi_v)
            for ci in range(2):
                h = hp * 2 + ci
                P.wait("pe", P.last["act"], P.last["pool"], P.last["dve"], nd_free[0], nd_free[1])
                vev = []
                for li_, r in enumerate((1, 4, 16)):
                    ntile = 16 // r
                    for half in range(2):
                        g_ = li_ * 2 + half
                        vb = 4 + (g_ % 4)
                        if g_ >= 4:
                            P.wait("pe", vev[g_ - 4])
                        tp = None
                        for jj in range(8):
                            j = half * 8 + jj
                            ph, kt = j // ntile, j % ntile
                            src = vT[:, ci, :].rearrange("p (l r) -> p r l", r=r)[:, ph, kt * 128:(kt + 1) * 128]
                            tp = P.op("pe", lambda e: e.transpose(
                                out=bank_bf(vb)[:, jj * 128:(jj + 1) * 128], in_=src, identity=ident[:]), sig=(jj == 7))
                        eng = "act" if half == 0 else "dve"
                        P.wait(eng, tp)
                        srcv = bank_bf(vb).rearrange("p (j c) -> p j c", j=8)
                        if eng == "act":
                            te = P.op("act", lambda e: e.activation(
                                out=Vl[:, li_, half * 8:half * 8 + 8, :], in_=srcv, func=AF.Copy))
                        else:
                            te = P.op("dve", lambda e: e.tensor_copy(
                                out=Vl[:, li_, half * 8:half * 8 + 8, :], in_=srcv))
                        vev.append(te)
                nd_free[0] = list(vev)
                nd_free[1] = list(vev)
                P.wait("pe", *vev)
                allb = []
                for Qb in range(4):
                    i0 = Qb * 512
                    banks_ = []
                    segA, segB = [], []
                    col = 0
                    for (rel, n, mc, q0) in ((-1, 64, 192, 0), (0, 192, 64, 0), (1, 256, 0, 64)):
                        kt = 4 * Qb + rel
                        segA.append((0, kt, ("c", i0 + q0, n, 1), ("c", kt * 128, 1), col, n, ("c", q0, n, 1)))
                        col += n
                    col = 0
                    for (rel, n, q0) in ((2, 256, 192), (3, 192, 320), (4, 64, 448)):
                        kt = 4 * Qb + rel
                        segB.append((0, kt, ("c", i0 + q0, n, 1), ("c", kt * 128, 1), col, n, ("c", q0, n, 1)))
                        col += n
                    banks_.append((0, segA)); banks_.append((1, segB))
                    for pp in range(2):
                        seg = []
                        col = 0
                        for ph in (2 * pp, 2 * pp + 1):
                            for (rel, n, q0) in ((-1, 64, 0), (0, 128, 0), (1, 64, 64)):
                                kt = Qb + rel
                                seg.append((1, ph * 4 + kt if 0 <= kt < 4 else -1, ("c", 4 * (128 * Qb + q0) + ph, n, 4),
                                            ("c", 4 * (128 * kt) + ph, 4), col, n, ("c", 4 * q0 + ph, n, 4)))
                                col += n
                        banks_.append((2, seg))
                    seg = []
                    for ph in range(16):
                        seg.append((2, ph, ("c", 16 * (32 * Qb) + ph, 32, 16), ("c", ph, 16), ph * 32, 32, ("c", ph, 32, 16)))
                    banks_.append((3 + Qb, seg))
                    for bi_, (mi, seg) in enumerate(banks_):
                        valid = []
                        for (lay, vt, qs, ks, col, n, oc) in seg:
                            if lay == 0 and not (0 <= vt < 16):
                                continue
                            if lay == 1 and vt < 0:
                                continue
                            valid.append((lay, vt, qs, ks, col, n, oc))
                        allb.append((Qb, mi, valid, bi_ == 0, bi_ == len(banks_) - 1))

                def emit_scores(item):
                    Qb, mi, valid, isfirst, islast = item
                    sb_ = sci[0] % 3
                    sci[0] += 1
                    P.wait("pe", bk_free[sb_])
                    lp = None
                    for vi, (lay, vt, qs, ks, col, n, oc) in enumerate(valid):
                        qa = qT[:, ci, qs[1]:qs[1] + (qs[2] - 1) * qs[3] + 1:qs[3]]
                        ka = kT[:, ci, ks[1]:ks[1] + 127 * ks[2] + 1:ks[2]]
                        lp = P.op("pe", lambda e: e.matmul(
                            bank(sb_)[:, col:col + n], lhsT=ka, rhs=qa, start=True, stop=True),
                            sig=(vi == len(valid) - 1))
                    P.wait("act", lp, pb_free[sb_])
                    te = P.op("act", lambda e: e.activation(out=Pb[:, sb_], in_=bank(sb_), func=AF.Exp, scale=scale))
                    bk_free[sb_] = te
                    P.wait("pool", te)
                    tm = P.op("pool", lambda e: e.tensor_tensor(out=Pb[:, sb_], in0=Pb[:, sb_], in1=maskb[:, mi], op=ALU.mult))
                    return sb_, tm

                def emit_pv(item, sb_, tm):
                    Qb, mi, valid, isfirst, islast = item
                    pr = Qb % 2
                    nb, db = (4, 5) if pr == 0 else (6, 7)
                    if isfirst:
                        P.wait("pe", nd_free[pr])
                    P.wait("pe", tm)
                    ftok = None
                    for vi, (lay, vt, qs, ks, col, n, oc) in enumerate(valid):
                        oa = slice(oc[1], oc[1] + (oc[2] - 1) * oc[3] + 1, oc[3])
                        st_ = isfirst and vi == 0
                        P.op("pe", lambda e: e.matmul(
                            bank(nb)[:, oa], lhsT=Vl[:, lay, vt, :], rhs=Pb[:, sb_, col:col + n],
                            start=st_, stop=False, skip_group_check=True), sig=False)
                        ftok = P.op("pe", lambda e: e.matmul(
                            bank(db)[:, oa], lhsT=ones[:], rhs=Pb[:, sb_, col:col + n],
                            start=st_, stop=False, skip_group_check=True), sig=True)
                    pb_free[sb_] = ftok
                    if islast:
                        i0 = Qb * 512
                        P.wait("dve", ftok, yat_free[0] if Qb == 0 else None, P.last["dve"])
                        td = P.op("dve", lambda e: e.reciprocal(out=rden, in_=bank(db)))
                        P.wait("dve", td)
                        nd_free[pr] = P.op("dve", lambda e: e.tensor_tensor(out=yat[:, i0:i0 + 512], in0=bank(nb), in1=rden, op=ALU.mult))

                pend = None
                for item in allb + [None]:
                    cur = None
                    if item is not None:
                        cur = (item,) + emit_scores(item)
                    if pend is not None:
                        emit_pv(*pend)
                    pend = cur
                P.wait("sp", nd_free[0], nd_free[1])
                yat_free[0] = P.dma("sp", lambda e, h=h: e.dma_start(
                    out=ycat[seq, (8 + h) * 128:(9 + h) * 128, :], in_=yat), ds_ya)
                ycat_tok[0] = yat_free[0]
        return [yat_free[0]] + last_conv

    for seq in range(nseq):
        P.barrier(extra=[tz] + ctoks)
        load_x(seq)
        P.barrier()
        if mode == "ffn":
            dump(xT[:, 0, 0, :], 0, 514, P.last["dve"], P.last["act"])
            dump(xT[:, 5, 1, :], 514, 514, P.last["dve"], P.last["act"])
            dump(cw[:, 0].rearrange("p k c -> p (k c)"), 4096, 264, P.last["dve"])
            ffn(seq, 0, x_in, y_out)
            P.barrier(extra=xres_free)
            continue
        if mode == "four":
            tk = fourier_m1(seq)
            P.barrier(extra=[tk])
            m2(seq, wb_of[0], ds_cast[("of", 0)], ln_mix_g, ln_mix_b, 1, x_in, y_out, tk)
            P.barrier(extra=xres_free)
            continue
        if mode == "even":
            tks = even_m1(seq, 0)
            P.barrier(extra=tks)
            m2(seq, wb_om[0], ds_cast[("om", 0)], ln_mix_g, ln_mix_b, 0, x_in, y_out, tks[0])
            P.barrier(extra=xres_free)
            continue
        for l in range(DEPTH):
            P.new_epoch()
            rsrc = x_in if l == 0 else resid
            if l % 2 == 0:
                tks = even_m1(seq, l // 2)
                P.barrier(extra=tks)
                m2(seq, wb_om[l // 2], ds_cast[("om", l // 2)], ln_mix_g, ln_mix_b, l, rsrc, resid, tks[0])
            else:
                tk = fourier_m1(seq)
                P.barrier(extra=[tk])
                m2(seq, wb_of[l // 2], ds_cast[("of", l // 2)], ln_mix_g, ln_mix_b, l, rsrc, resid, tk)
            P.barrier(extra=xres_free)
            ffn(seq, l, resid, y_out if l == DEPTH - 1 else resid)
            P.barrier(extra=xres_free)
    P.barrier(extra=xres_free)

    block = stack.enter_context(nc.Block())

    @block.tensor
    def _(e):
        P.emit("pe", e)

    @block.scalar
    def _(e):
        P.emit("act", e)

    @block.vector
    def _(e):
        P.emit("dve", e)

    @block.gpsimd
    def _(e):
        P.emit("pool", e)

    @block.sync
    def _(e):
        P.emit("sp", e)


def _bf(a):
    return np.asarray(a, np.float32).astype(ml_dtypes.bfloat16)


def make_consts():
    c = {}
    c["c_ident"] = _bf(np.eye(128))
    pm = np.zeros((128, 128), np.float32)
    for m in range(128):
        pm[(m + 64) % 128, m] = 1.0
    c["c_perm"] = _bf(pm)
    c["c_identf"] = np.eye(128, dtype=np.float32)
    half = 64
    inv = (1.0 / (10000.0 ** (np.arange(half, dtype=np.float32) / np.float32(half)))).astype(np.float32)
    ang = (np.arange(S, dtype=np.float32)[:, None] * inv[None, :]).astype(np.float32)
    cos = np.cos(ang.astype(np.float64)).T
    sin = np.sin(ang.astype(np.float64)).T
    c["c_cos"] = np.concatenate([cos, cos], 0).astype(np.float32)
    c["c_sin"] = np.concatenate([-sin, sin], 0).astype(np.float32)
    a = np.arange(128)[:, None]
    cc = np.arange(256)[None, :]
    master = (np.abs(cc - 64 - a) <= 64).astype(np.float32)
    masks = np.zeros((7, 128, 512), np.float32)
    masks[0] = np.concatenate([master[:, 192:256], master[:, 64:256], master[:, 0:256]], 1)
    masks[1] = np.concatenate([master[:, 0:256], master[:, 0:192], master[:, 0:64]], 1)
    one = np.concatenate([master[:, 192:256], master[:, 64:192], master[:, 0:64]], 1)
    masks[2] = np.concatenate([one, one], 1)
    for Qb in range(4):
        masks[3 + Qb] = np.tile(master[:, 64 + 32 * Qb:64 + 32 * Qb + 32], (1, 16))
    c["c_mask"] = _bf(masks)
    k = np.arange(512, dtype=np.float64)
    th = 2 * np.pi * np.outer(k, k) / 512.0
    c["c_cc"] = _bf(np.cos(th) / math.sqrt(512.0))
    c["c_sc"] = _bf(np.sin(th) / math.sqrt(512.0))
    k = np.arange(S, dtype=np.float64)
    th = 2 * np.pi * ((np.outer(k, k)) % S) / S
    c["c_cs"] = _bf(np.cos(th) / math.sqrt(S))
    c["c_ns"] = _bf(-np.sin(th) / math.sqrt(S))
    return c


_NC_CACHE = {}


def kernel(x_prompt, x_sample, w_in_mix, conv_short, w_out_mix, w_out_fourier, ln_mix_g, ln_mix_b,
           w_up, conv_ffn_w, w_down, ln_ffn_g, ln_ffn_b):
    f = lambda a: np.ascontiguousarray(np.asarray(a, dtype=np.float32))
    xp, xs = f(x_prompt), f(x_sample)
    consts = make_consts()
    shared = dict(w_in_mix=f(w_in_mix), conv_short=f(conv_short), w_out_mix=f(w_out_mix),
                  w_out_fourier=f(w_out_fourier), ln_mix_g=f(ln_mix_g), ln_mix_b=f(ln_mix_b),
                  w_up=f(w_up), conv_ffn_w=f(conv_ffn_w), w_down=f(w_down), ln_ffn_g=f(ln_ffn_g),
                  ln_ffn_b=f(ln_ffn_b))
    shared.update(consts)
    in_maps = []
    for c in range(8):
        xs_c = np.ascontiguousarray(np.stack([xp[2 * c], xp[2 * c + 1], xs[c]], 0))
        m = dict(shared)
        m["x"] = xs_c
        in_maps.append(m)
    nc = build(NSEQ, "full")
    res = run_bass_kernel_spmd(nc, in_maps, core_ids=list(range(8)))
    yp = np.empty_like(xp)
    ys = np.empty_like(xs)
    for c in range(8):
        y = res.results[c]["y"]
        yp[2 * c] = y[0]
        yp[2 * c + 1] = y[1]
        ys[c] = y[2]
    return (yp, ys)
```
